# Optimizing a Trainium2 kernel written in Bass

```python
import jax, jax.numpy as jnp
from jax import lax
import numpy as np

D_MODEL = 1024
BATCH = 8
SEQ = 2048
DEPTH = 2

HEAD_DIM = 64
ATTN_HEADS = 8
ATTN_KV_HEADS = 2
ATTN_GROUP = ATTN_HEADS // ATTN_KV_HEADS
WINDOW = 128
ATTN_BLOCK = 128
ROPE_THETA = 10000.0

RET_HEADS = 4
RET_DK = 64
RET_DV = 64
RET_CHUNK = 128

GLA_HEADS = 4
GLA_DK = 32
GLA_DV = 64
GLA_CHUNK = 64
GLA_GATE_RANK = 16
GLA_GATE_NORMALIZER = 16.0

EPS = 1e-6

ATTN_W = ATTN_HEADS * HEAD_DIM
RET_W = RET_HEADS * RET_DV
GLA_W = GLA_HEADS * GLA_DV
D_MIX = ATTN_W + RET_W + GLA_W

IN_SIZES = (
    ATTN_HEADS * HEAD_DIM, ATTN_KV_HEADS * HEAD_DIM, ATTN_KV_HEADS * HEAD_DIM, ATTN_W,
    RET_HEADS * RET_DK, RET_HEADS * RET_DK, RET_HEADS * RET_DV, RET_W,
    GLA_HEADS * GLA_DK, GLA_HEADS * GLA_DK, GLA_HEADS * GLA_DV, GLA_W,
    GLA_GATE_RANK,
)
D_IN = sum(IN_SIZES)
IN_SPLITS = [int(s) for s in np.cumsum(IN_SIZES)[:-1]]

kernel_name = "hymba_style_swa_retention_gla_hybrid"


def rms_norm(x, gain=None):
    xf = x.astype(jnp.float32)
    y = xf * lax.rsqrt(jnp.mean(xf * xf, axis=-1, keepdims=True) + EPS)
    if gain is not None:
        y = y * gain.astype(jnp.float32)
    return y.astype(x.dtype)


def apply_rotary(x, positions, inv_freq):
    ang = positions.astype(jnp.float32)[..., None] * inv_freq
    cos = jnp.cos(ang)[:, :, None, :]
    sin = jnp.sin(ang)[:, :, None, :]
    x1, x2 = jnp.split(x.astype(jnp.float32), 2, axis=-1)
    out = jnp.concatenate([x1 * cos - x2 * sin, x2 * cos + x1 * sin], axis=-1)
    return out.astype(x.dtype)


def sliding_window_attention(q, k, v, sinks):
    B, S, H, D = q.shape
    nb = S // ATTN_BLOCK
    qb = q.reshape(B, nb, ATTN_BLOCK, ATTN_KV_HEADS, ATTN_GROUP, D)
    kb = k.reshape(B, nb, ATTN_BLOCK, ATTN_KV_HEADS, D)
    vb = v.reshape(B, nb, ATTN_BLOCK, ATTN_KV_HEADS, D)

    def with_prev(t):
        prev = jnp.pad(t, ((0, 0), (1, 0), (0, 0), (0, 0), (0, 0)))[:, :-1]
        return jnp.concatenate([prev, t], axis=2)

    kk, vv = with_prev(kb), with_prev(vb)
    s = jnp.einsum('bnqhgd,bnkhd->bnhgqk', qb, kk).astype(jnp.float32) * (D ** -0.5)
    i = jnp.arange(ATTN_BLOCK)[:, None]
    j = jnp.arange(2 * ATTN_BLOCK)[None, :]
    rel = i + ATTN_BLOCK - j
    kpos = (jnp.arange(nb)[:, None] - 1) * ATTN_BLOCK + jnp.arange(2 * ATTN_BLOCK)[None, :]
    valid = ((rel >= 0) & (rel < WINDOW))[None] & (kpos >= 0)[:, None, :]
    s = jnp.where(valid[None, :, None, None], s, -1e30)
    sink = sinks.astype(jnp.float32).reshape(ATTN_KV_HEADS, ATTN_GROUP)[None, None, :, :, None, None]
    sink = jnp.broadcast_to(sink, s.shape[:-1] + (1,))
    p = jax.nn.softmax(jnp.concatenate([s, sink], axis=-1), axis=-1)[..., :-1]
    o = jnp.einsum('bnhgqk,bnkhd->bnqhgd', p.astype(v.dtype), vv)
    return o.reshape(B, S, H * D)


def retention(q, k, v):
    B, S, H, Dk = q.shape
    Dv = v.shape[-1]
    C = RET_CHUNK
    nc = S // C
    f32 = jnp.float32
    log_g = jnp.log(1.0 - 2.0 ** (-5.0 - jnp.arange(H, dtype=f32)))
    idx = jnp.arange(C, dtype=f32)
    diff = idx[:, None] - idx[None, :]
    dmask = jnp.where(diff >= 0, jnp.exp(log_g[:, None, None] * jnp.maximum(diff, 0.0)), 0.0)
    q_decay = jnp.exp(log_g[:, None] * (idx + 1.0))[..., None]
    k_decay = jnp.exp(log_g[:, None] * (C - 1.0 - idx))[..., None]
    chunk_decay = jnp.exp(log_g * C)[:, None, None]

    def to_chunks(t):
        return t.astype(f32).reshape(B, nc, C, H, t.shape[-1]).transpose(1, 0, 3, 2, 4)

    qc, kc, vc = to_chunks(q), to_chunks(k * (Dk ** -0.5)), to_chunks(v)

    def step(state, inp):
        qi, ki, vi = inp
        sc = jnp.einsum('bhid,bhjd->bhij', qi, ki) * dmask
        intra = jnp.einsum('bhij,bhjv->bhiv', sc, vi)
        inter = jnp.einsum('bhid,bhdv->bhiv', qi * q_decay, state)
        new_state = chunk_decay * state + jnp.einsum('bhjd,bhjv->bhdv', ki * k_decay, vi)
        return new_state, intra + inter

    state0 = jnp.zeros((B, H, Dk, Dv), f32)
    _, out = lax.scan(step, state0, (qc, kc, vc))
    return out.transpose(1, 0, 3, 2, 4).reshape(B, S, H, Dv).astype(v.dtype)


def gated_linear_attention(q, k, v, log_a):
    B, S, H, Dk = q.shape
    Dv = v.shape[-1]
    C = GLA_CHUNK
    nc = S // C
    f32 = jnp.float32

    def to_chunks(t):
        return t.astype(f32).reshape(B, nc, C, H, t.shape[-1]).transpose(1, 0, 3, 2, 4)

    qc, kc, vc, gc = to_chunks(q * (Dk ** -0.5)), to_chunks(k), to_chunks(v), to_chunks(log_a)
    causal = jnp.tril(jnp.ones((C, C), dtype=bool))[..., None]

    def step(state, inp):
        qi, ki, vi, gi = inp
        b = jnp.cumsum(gi, axis=2)
        rel = b[:, :, :, None, :] - b[:, :, None, :, :]
        decay = jnp.where(causal, jnp.exp(jnp.minimum(rel, 0.0)), 0.0)
        sc = jnp.einsum('bhid,bhijd,bhjd->bhij', qi, decay, ki)
        intra = jnp.einsum('bhij,bhjv->bhiv', sc, vi)
        inter = jnp.einsum('bhid,bhdv->bhiv', qi * jnp.exp(b), state)
        b_last = b[:, :, -1:, :]
        new_state = (jnp.exp(b_last)[:, :, 0, :, None] * state
                     + jnp.einsum('bhjd,bhjv->bhdv', ki * jnp.exp(b_last - b), vi))
        return new_state, intra + inter

    state0 = jnp.zeros((B, H, Dk, Dv), f32)
    _, out = lax.scan(step, state0, (qc, kc, vc, gc))
    return out.transpose(1, 0, 3, 2, 4).reshape(B, S, H, Dv).astype(v.dtype)


def hybrid_layer(x, c_act, positions, w_mod, b_mod, pre_gain, post_gain, w_in, sinks,
                 gla_gate_w, gla_gate_b, gla_norm_gain, w_out):
    B, S, _ = x.shape
    mod = c_act @ w_mod + b_mod
    shift, scale, gate = jnp.split(mod[:, None, :], 3, axis=-1)
    h = rms_norm(x, pre_gain) * (1.0 + scale) + shift
    proj = h @ w_in
    (aq, ak, av, ag, rq, rk, rv, rg, gq, gk, gv, gg, ga) = jnp.split(proj, IN_SPLITS, axis=-1)

    rope_freq = ROPE_THETA ** (-jnp.arange(0, HEAD_DIM, 2, dtype=jnp.float32) / HEAD_DIM)
    aq = apply_rotary(aq.reshape(B, S, ATTN_HEADS, HEAD_DIM), positions, rope_freq)
    ak = apply_rotary(ak.reshape(B, S, ATTN_KV_HEADS, HEAD_DIM), positions, rope_freq)
    av = av.reshape(B, S, ATTN_KV_HEADS, HEAD_DIM)
    a_out = sliding_window_attention(aq, ak, av, sinks) * jax.nn.silu(ag)

    ret_freq = 1.0 / (10000.0 ** jnp.linspace(0.0, 1.0, RET_DK // 2, dtype=jnp.float32))
    rq = apply_rotary(rq.reshape(B, S, RET_HEADS, RET_DK), positions, ret_freq)
    rk = apply_rotary(rk.reshape(B, S, RET_HEADS, RET_DK), positions, ret_freq)
    r = retention(rq, rk, rv.reshape(B, S, RET_HEADS, RET_DV))
    r_out = rms_norm(r).reshape(B, S, RET_W) * jax.nn.silu(rg)

    gate_logits = (ga @ gla_gate_w + gla_gate_b).astype(jnp.float32)
    log_a = (jax.nn.log_sigmoid(gate_logits) / GLA_GATE_NORMALIZER).reshape(B, S, GLA_HEADS, GLA_DK)
    g = gated_linear_attention(gq.reshape(B, S, GLA_HEADS, GLA_DK), gk.reshape(B, S, GLA_HEADS, GLA_DK),
                               gv.reshape(B, S, GLA_HEADS, GLA_DV), log_a)
    g_out = rms_norm(g, gla_norm_gain).reshape(B, S, GLA_W) * jax.nn.silu(gg)

    y = jnp.concatenate([a_out, r_out, g_out], axis=-1) @ w_out
    return x + gate * rms_norm(y, post_gain)


def setup_inputs(seed: int = 0) -> dict:
    key = jax.random.key(seed)
    ks = jax.random.split(key, 16)
    f32 = jnp.float32
    x = jax.random.normal(ks[0], (BATCH, SEQ, D_MODEL), f32)
    c = jax.random.normal(ks[1], (BATCH, D_MODEL), f32)
    positions = jnp.broadcast_to(jnp.arange(SEQ, dtype=jnp.int32)[None, :], (BATCH, SEQ))
    w_mod = jax.random.normal(ks[2], (DEPTH, D_MODEL, 3 * D_MODEL), f32) * (0.5 * D_MODEL ** -0.5)
    b_mod = jax.random.normal(ks[3], (DEPTH, 3 * D_MODEL), f32) * 0.01
    pre_norm_gain = 1.0 + 0.02 * jax.random.normal(ks[4], (DEPTH, D_MODEL), f32)
    post_norm_gain = 1.0 + 0.02 * jax.random.normal(ks[5], (DEPTH, D_MODEL), f32)
    w_in = jax.random.normal(ks[6], (DEPTH, D_MODEL, D_IN), f32) * (D_MODEL ** -0.5)
    attn_sinks = 0.5 * jax.random.normal(ks[7], (DEPTH, ATTN_HEADS), f32)
    gla_gate_w = jax.random.normal(ks[8], (DEPTH, GLA_GATE_RANK, GLA_HEADS * GLA_DK), f32) * (GLA_GATE_RANK ** -0.5)
    gla_gate_b = 0.01 * jax.random.normal(ks[9], (DEPTH, GLA_HEADS * GLA_DK), f32)
    gla_norm_gain = 1.0 + 0.02 * jax.random.normal(ks[10], (DEPTH, GLA_DV), f32)
    w_out = jax.random.normal(ks[11], (DEPTH, D_MIX, D_MODEL), f32) * (D_MIX ** -0.5)
    return {"x": x, "c": c, "positions": positions, "w_mod": w_mod, "b_mod": b_mod,
            "pre_norm_gain": pre_norm_gain, "post_norm_gain": post_norm_gain, "w_in": w_in,
            "attn_sinks": attn_sinks, "gla_gate_w": gla_gate_w, "gla_gate_b": gla_gate_b,
            "gla_norm_gain": gla_norm_gain, "w_out": w_out}


def reference(x, c, positions, w_mod, b_mod, pre_norm_gain, post_norm_gain, w_in,
              attn_sinks, gla_gate_w, gla_gate_b, gla_norm_gain, w_out):
    c_act = jax.nn.silu(c)
    for l in range(DEPTH):
        x = hybrid_layer(x, c_act, positions, w_mod[l], b_mod[l], pre_norm_gain[l], post_norm_gain[l],
                         w_in[l], attn_sinks[l], gla_gate_w[l], gla_gate_b[l], gla_norm_gain[l], w_out[l])
    return x
```

```python
import os
import numpy as np
import concourse.bass as bass
import concourse.mybir as mybir
from concourse.bass_utils import run_bass_kernel_spmd

F32 = mybir.dt.float32
BF16 = mybir.dt.bfloat16
I32 = mybir.dt.int32
AF = mybir.ActivationFunctionType
ALU = mybir.AluOpType
AX = mybir.AxisListType

S = 2048
D = 1024
NT = 16
DIN = 3088
EPS = 1e-6
NEG = -30000.0


class Sem:
    def __init__(self, h, name):
        self.h = h
        self.count = 0
        self.name = name


class Res:
    def __init__(self, name, excl=False):
        self.name = name
        self.excl = excl
        self.w = None
        self.r = {}


class Eng:
    def __init__(self, name, h, sem):
        self.name = name
        self.h = h
        self.sem = sem
        self.seen = {}

    def wait(self, tok):
        if tok is None:
            return
        if tok[0] == "PENDING":
            if self.name == "pe":
                return
            raise RuntimeError("wait on pending PE token by " + self.name)
        s, v, _ = tok
        if self.seen.get(id(s), 0) >= v:
            return
        self.h.wait_ge(s.h, v)
        self.seen[id(s)] = v


class _Probe:
    def __init__(self):
        self.n = 0
        self.opname = ""
        self.fp32 = False
        self.accum = False

    def __getattr__(self, name):
        def f(*a, **k):
            out = k.get("out", a[0] if a else None)
            n = 1
            for d in out.shape[1:]:
                n *= d
            self.n = n
            self.opname = name
            self.call = (name, a, k)
            lt = k.get("lhsT", None)
            self.fp32 = lt is not None and lt.dtype == F32
            self.accum = k.get("accum_out", None) is not None
            return self
        return f

    def then_inc(self, *a, **k):
        return self

    def replay(self):
        name, a, k = self.call
        return lambda e: getattr(e, name)(*a, **k)


class Unit:
    __slots__ = ("kind", "eng", "fns", "reads", "writes", "dur", "busy", "idx", "args", "deps", "nsucc")


def _est(ename, pr):
    n = pr.n
    if ename == "pe":
        if pr.opname == "transpose":
            return 108.0
        return (max(64, n) / 2.4 + 6.0) * (4.0 if pr.fp32 else 1.0)
    if ename == "act":
        return 190.0 + n / 1.2 + (90.0 if pr.accum else 0.0)
    if ename == "dve":
        if pr.opname == "reciprocal":
            return 80.0 + 8.0 * n
        return 70.0 + n * 1.05
    if ename == "pool":
        if pr.opname == "tensor_tensor" and n <= 16:
            return 750.0
        return 150.0 + n * 2.3
    return 100.0


class FW:
    def __init__(self, nc, ndma_sems=16):
        self.rec = None
        self.cur_pe = None
        self.nc = nc
        self._ctx = []
        self.engs = {}
        for name, h in (("pe", nc.tensor), ("dve", nc.vector), ("act", nc.scalar),
                        ("pool", nc.gpsimd), ("sp", nc.sync)):
            self.engs[name] = Eng(name, h, self._sem("s_" + name))
        self.dsems = [self._sem("d%d" % i) for i in range(ndma_sems)]
        self.dnext = 0
        self.dsems_nb = [self._sem("n%d" % i) for i in range(8)]
        self.dnext_nb = 0
        self.dsems_sw = []
        self.pe_pending = []

    def _sem(self, name):
        cm = self.nc.semaphore(name)
        h = cm.__enter__()
        self._ctx.append(cm)
        return Sem(h, name)

    def barrier(self):
        was = self.rec is not None
        self.flush()
        self._barrier()
        if was:
            self.start_recording()

    def _barrier(self):
        assert not self.pe_pending
        toks = [(e.sem, e.sem.count, e.name) for e in self.engs.values() if e.sem.count]
        toks += [(s, s.count, "dma") for s in self.dsems if s.count]
        toks += [(s, s.count, "dma") for s, nb in self.dsems_sw if s.count and not nb]
        for e in self.engs.values():
            for t in toks:
                if t[2] != e.name:
                    e.wait(t)

    def sb(self, name, shape, dt):
        n = 1
        for d in shape[1:]:
            n *= d
        self.nbytes = getattr(self, "nbytes", 0) + n * (2 if dt == BF16 else 4)
        cm = self.nc.sbuf_tensor("sb_" + name, list(shape), dt)
        t = cm.__enter__()
        self._ctx.append(cm)
        return t

    def ps(self, name, shape, dt):
        cm = self.nc.psum_tensor(name, list(shape), dt)
        t = cm.__enter__()
        self._ctx.append(cm)
        return t

    def close(self):
        for cm in reversed(self._ctx):
            cm.__exit__(None, None, None)
        self._ctx = []

    def _acq(self, e, reads, writes):
        for r in reads:
            e.wait(r.w)
            if r.excl:
                for en, t in r.r.items():
                    if en != e.name:
                        e.wait(t)
        for w in writes:
            e.wait(w.w)
            for en, t in w.r.items():
                e.wait(t)

    def _rel(self, ename, tok, reads, writes):
        for r in reads:
            r.r[ename] = tok
        for w in writes:
            w.w = tok
            w.r = {}

    def start_recording(self):
        self.rec = []
        self.cur_pe = None

    def flush(self):
        if self.rec is None:
            return
        assert self.cur_pe is None
        units = self.rec
        self.rec = None
        n = len(units)
        lastw = {}
        readers = {}
        succ = [[] for _ in range(n)]
        for i, u in enumerate(units):
            u.idx = i
            deps = set()
            for r in u.reads:
                k = id(r)
                if r.excl:
                    if k in lastw:
                        deps.add(lastw[k])
                    deps.update(readers.get(k, ()))
                elif k in lastw:
                    deps.add(lastw[k])
            for w in u.writes:
                k = id(w)
                if k in lastw:
                    deps.add(lastw[k])
                deps.update(readers.get(k, ()))
            deps.discard(i)
            u.deps = deps
            for d in deps:
                succ[d].append(i)
            for r in u.reads:
                k = id(r)
                if r.excl:
                    lastw[k] = i
                    readers[k] = []
                else:
                    readers.setdefault(k, []).append(i)
            for w in u.writes:
                k = id(w)
                lastw[k] = i
                readers[k] = []
        LAT = float(os.environ.get("SCHED_LAT", "250"))
        blevel = [0.0] * n
        for i in range(n - 1, -1, -1):
            u = units[i]
            m = 0.0
            for j in succ[i]:
                v = blevel[j] + (LAT if units[j].eng != u.eng else 40.0)
                if v > m:
                    m = v
            blevel[i] = u.dur + m
        PEB = float(os.environ.get("SCHED_PEB", "0"))
        if PEB:
            for i in range(n):
                if units[i].eng == "pe":
                    blevel[i] += PEB
        ndep = [len(u.deps) for u in units]
        finish = [0.0] * n
        efree = {}
        ready = [i for i in range(n) if ndep[i] == 0]
        order = []
        SLACK = float(os.environ.get("SCHED_SLACK", "120"))
        while ready:
            ests = []
            mn = None
            for i in ready:
                u = units[i]
                st = efree.get(u.eng, 0.0)
                for d in u.deps:
                    f = finish[d] + (LAT if units[d].eng != u.eng else 40.0)
                    if f > st:
                        st = f
                ests.append(st)
                if mn is None or st < mn:
                    mn = st
            best = None
            bs = None
            bl = -1.0
            for i, st in zip(ready, ests):
                if st <= mn + SLACK and (blevel[i] > bl + 1e-9 or (abs(blevel[i] - bl) <= 1e-9 and i < best)):
                    bl = blevel[i]
                    best = i
                    bs = st
            ready.remove(best)
            u = units[best]
            if getattr(self, "diag", None) is not None and bs > efree.get(u.eng, 0.0) + 1.0:
                bd = max(u.deps, key=lambda d: finish[d] + (LAT if units[d].eng != u.eng else 40.0)) if u.deps else None
                if bd is not None:
                    ud = units[bd]
                    shared = [r.name for r in list(u.reads) + list(u.writes) if r in ud.reads or r in ud.writes]
                    key = (u.eng, shared[0] if shared else "?", ud.eng)
                    self.diag[key] = self.diag.get(key, 0.0) + bs - efree.get(u.eng, 0.0)
            efree[u.eng] = bs + u.busy
            finish[best] = bs + u.dur
            order.append(best)
            for j in succ[best]:
                ndep[j] -= 1
                if ndep[j] == 0:
                    ready.append(j)
        assert len(order) == n
        self.sched_span = getattr(self, "sched_span", 0.0) + max(finish) if n else 0.0
        for i in order:
            u = units[i]
            if u.kind == "op":
                self.op(u.eng, u.fns[0], u.reads, u.writes)
            elif u.kind == "mm":
                for j, (fn, rd, wr) in enumerate(u.fns):
                    self.mm(fn, rd, wr, last=(j == len(u.fns) - 1))
            else:
                self.dma(u.eng, u.args[0], u.args[1], u.reads, u.writes, nobar=u.args[2])

    def _record(self, kind, eng, fns, reads, writes, dur, busy, args=None):
        u = Unit()
        u.kind = kind
        u.eng = eng
        u.fns = fns
        u.reads = tuple(reads)
        u.writes = tuple(writes)
        u.dur = dur
        u.busy = busy
        u.args = args
        self.rec.append(u)

    def op(self, ename, fn, reads=(), writes=()):
        if self.rec is not None:
            pr = _Probe()
            fn(pr)
            d = _est(ename, pr)
            self._record("op", ename, [pr.replay()], reads, writes, d, d)
            return None
        e = self.engs[ename]
        self._acq(e, reads, writes)
        inst = fn(e.h)
        e.sem.count += 1
        inst.then_inc(e.sem.h, 1)
        self._rel(ename, (e.sem, e.sem.count, ename), reads, writes)
        return inst

    def mm(self, fn, reads=(), writes=(), last=False):
        if self.rec is not None:
            pr = _Probe()
            fn(pr)
            d = _est("pe", pr)
            if self.cur_pe is None:
                self.cur_pe = [[], [], [], 0.0]
            g = self.cur_pe
            g[0].append((pr.replay(), tuple(reads), tuple(writes)))
            for r in reads:
                if r not in g[1]:
                    g[1].append(r)
            for w in writes:
                if w not in g[2]:
                    g[2].append(w)
            g[3] += d
            if last:
                self.cur_pe = None
                self._record("mm", "pe", g[0], g[1], g[2], g[3] + 60.0, g[3])
            return None
        e = self.engs["pe"]
        self._acq(e, reads, writes)
        inst = fn(e.h)
        self.pe_pending.append((tuple(reads), tuple(writes)))
        if last:
            e.sem.count += 1
            inst.then_inc(e.sem.h, 1)
            tok = (e.sem, e.sem.count, "pe")
            for rd, wr in self.pe_pending:
                self._rel("pe", tok, rd, wr)
            self.pe_pending = []
        else:
            for w in writes:
                w.w = ("PENDING",)
                w.r = {}
            for r in reads:
                r.r["pe"] = ("PENDING",)
        return inst

    def dma(self, qname, out, in_, reads=(), writes=(), nobar=False):
        if self.rec is not None:
            n = 1
            for d_ in out.shape:
                n *= d_
            self._record("dma", qname, None, reads, writes, 2500.0 + n * 4 / 150.0, 120.0, (out, in_, nobar))
            return None
        e = self.engs[qname]
        self._acq(e, reads, writes)
        if qname == "pool":
            s = self._sem("w%d" % len(self.dsems_sw))
            self.dsems_sw.append((s, nobar))
        elif nobar:
            s = self.dsems_nb[self.dnext_nb]
            self.dnext_nb = (self.dnext_nb + 1) % len(self.dsems_nb)
        else:
            s = self.dsems[self.dnext]
            self.dnext = (self.dnext + 1) % len(self.dsems)
        if s.count:
            e.wait((s, s.count, "dma"))
        inst = e.h.dma_start(out=out, in_=in_)
        s.count += 16
        inst.then_inc(s.h, 16)
        tok = (s, s.count, "dma:" + s.name)
        self._rel("dma:" + s.name, tok, reads, writes)
        return tok


class T:
    def __init__(self, fw, name, shape, dt, view=None):
        self.t = fw.sb(name, shape, dt) if view is None else view
        self.r = Res(name)


class Arena:
    def __init__(self, t, nwords):
        self.t = t
        self.n = nwords
        self.o = 0

    def reset(self):
        self.o = 0

    def get(self, name, shape, dt):
        n = 1
        for d in shape[1:]:
            n *= d
        words = n if dt != BF16 else (n + 1) // 2
        assert self.o + words <= self.n, (name, self.o, words, self.n)
        v = self.t[:, self.o:self.o + words]
        self.o += words
        if dt != F32:
            v = v.bitcast(dt)
        if len(shape) == 3:
            v = v.rearrange("p (a b) -> p a b", b=shape[2])
        elif len(shape) == 4:
            v = v.rearrange("p (a b c) -> p a b c", b=shape[2], c=shape[3])
        return T(None, name, shape, dt, view=v)

    def __getitem__(self, k):
        return self.t[k]


def _consts():
    p = np.arange(128)
    cols = {}
    ident = np.eye(128, dtype=np.float32)
    cols["ident"] = ident
    k = p[:, None]
    q = p[None, :]
    mcur = np.where(k <= q, 0.0, NEG).astype(np.float32)
    mprev = np.where(k > q, 0.0, NEG).astype(np.float32)
    causal = (k <= q).astype(np.float32)
    global _CMASK
    _CMASK = np.ascontiguousarray(np.concatenate([mcur, mprev, causal], axis=1))
    cols["tri_in"] = causal * (-1.0 / 16.0)
    cols["tri_rev"] = (k > q).astype(np.float32) * (-1.0 / 16.0)
    cols["ncol"] = np.full((128, 2), -1.0 / 16.0, np.float32)
    h = np.arange(4, dtype=np.float32)
    log_g = np.log(1.0 - 2.0 ** (-5.0 - h)).astype(np.float32)
    i1 = (p[:, None] + 1).astype(np.float32)
    qdec = np.exp(log_g[None, :] * i1)
    kdec = np.exp(-log_g[None, :] * i1) / 8.0
    cols["dec8"] = np.concatenate([qdec, kdec], axis=1).astype(np.float32)
    cdec = np.exp(log_g * 128.0)
    sdec = np.zeros((128, 2), np.float32)
    for c in range(2):
        sdec[0:64, c] = cdec[2 * c]
        sdec[64:128, c] = cdec[2 * c + 1]
    cols["sdec"] = sdec
    bd64 = np.zeros((128, 128), np.float32)
    bd64[0:64, 0:64] = 1.0
    bd64[64:128, 64:128] = 1.0
    cols["bd64dec"] = np.concatenate([bd64 * sdec[:, 0:1], bd64 * sdec[:, 1:2]], axis=1)
    bd32 = np.zeros((128, 256), np.float32)
    for hh in range(4):
        bd32[32 * hh:32 * hh + 32, 64 * hh:64 * hh + 64] = 1.0
    cols["bd32"] = bd32
    gm = np.zeros((128, 2), np.float32)
    gm[0:16, 0] = 1.0
    gm[16, 1] = 1.0
    cols["gamask"] = gm
    fa = (10000.0 ** (-np.arange(0, 64, 2, dtype=np.float32) / 64.0)).astype(np.float32)
    fr = (1.0 / (10000.0 ** np.linspace(0.0, 1.0, 32, dtype=np.float32))).astype(np.float32)
    cols["freq"] = np.tile(np.concatenate([fa, fr])[None, :], (128, 1)).astype(np.float32)
    cols["neghalf"] = np.full((128, 8), -0.5, np.float32)
    cols["neghalf16"] = np.full((128, 16), -0.5, np.float32)
    off = {}
    o = 0
    parts = []
    for kname, v in cols.items():
        off[kname] = (o, v.shape[1])
        o += v.shape[1]
        parts.append(v.astype(np.float32))
    return np.ascontiguousarray(np.concatenate(parts, axis=1)), off


_CBLOB, _COFF = _consts()
NCONST = _CBLOB.shape[1]

_AQ = np.concatenate([np.arange(64 * hh, 64 * hh + 64) for hh in (0, 4, 1, 5, 2, 6, 3, 7)])
_r = lambda a, b: np.arange(a, b)
_PERM = np.concatenate([
    _r(3072, 3088),
    _AQ,
    _r(512, 640), _r(640, 768), _r(1280, 1536),
    _r(1536, 1792), _r(1792, 2048),
    _r(768, 1280),
    _r(2048, 2304), _r(2816, 3072),
    _r(2304, 2432), _r(2432, 2560), _r(2560, 2816),
])
NW = _PERM.shape[0]
_QK32 = np.concatenate([_r(1280, 1536), _r(2304, 2432), _r(1536, 1792), _r(2432, 2560)])
GOFF = 0
COFFS = [16 + 512 * i for i in range(6)]


def build(n_layers=2, n_tiles=NT, dbg=None, sched=True):
    nc = bass.Bass("TRN2", target_bir_lowering=False)
    fw = FW(nc)
    if sched:
        fw.start_recording()
    L = n_layers

    def din(name, shape, dt=F32):
        return nc.dram_tensor(name, list(shape), dt, kind="ExternalInput").ap()

    x_d = din("x", [S, D])
    c_d = din("c", [128, 8])
    pos_d = din("pos", [128, NT], I32)
    wmod_d = din("w_mod", [L, D, 3 * D])
    bmodF_d = din("bmodF", [L, 128, 24])
    bgate_d = din("bgate", [L, D])
    pregF_d = din("pregF", [L, 128, 8])
    postg_d = din("postg", [L, D])
    win_d = din("w_in", [L, D, NW])
    sink_d = din("sinks", [L, 8])
    gwe_d = din("gwe", [L, 128, 128])
    ggain_d = din("ggain", [L, 64])
    wout_d = din("w_out", [L, D, D])
    wqk_d = din("w_qk32", [L, D, 768])
    const_d = din("consts", [128, NCONST])
    cmask_d = din("cmask", [128, 384])
    out_d = nc.dram_tensor("out", [S, D], F32, kind="ExternalOutput").ap()

    xs = fw.sb("xs", [128, NT, D], F32)
    R_x = [Res("x%d" % t) for t in range(NT)]
    win = fw.sb("win", [128, 8, NW], BF16)
    R_winc = [[Res("win%d_%d" % (kc, hf)) for hf in range(2)] for kc in range(8)]
    wout = fw.sb("wout", [128, 8, D], BF16)
    R_woutc = [Res("wout%d" % kc) for kc in range(8)]
    cst = T(fw, "cst", [128, NCONST], F32)

    def C(name):
        o, n = _COFF[name]
        return cst.t[:, o:o + n]

    identb = T(fw, "identb", [128, 128], BF16)
    mcurb = T(fw, "mcurb", [128, 512], BF16)
    mprevb = T(fw, "mprevb", [128, 512], BF16)
    gpg = T(fw, "gpg", [128, D], F32)
    gmF = T(fw, "gmF", [128, 8], F32)
    shF = T(fw, "shF", [128, 8], F32)
    esink2 = T(fw, "esink2", [128, 8], F32)
    gwe = T(fw, "gwe", [128, 128], F32)
    ggain = T(fw, "ggain", [128, 64], F32)
    tabs = T(fw, "tabs", [128, 4, NT, 32], F32)
    small = T(fw, "small", [128, 64], F32)
    retS = T(fw, "retS", [128, 256], F32)
    retSb = T(fw, "retSb", [128, 256], BF16)
    glaS = T(fw, "glaS", [128, 256], F32)
    glaSb = T(fw, "glaSb", [128, 256], BF16)
    big32_ = T(fw, "big32", [128, 512], F32)
    big32 = [big32_, big32_]
    xn = T(fw, "xn", [128, D], BF16)
    hT = T(fw, "hT", [128, 8, 128], BF16)
    gaTe = T(fw, "gaTe", [128, 128], F32)
    nl = T(fw, "nl", [128, 128], F32)
    Eq = T(fw, "Eq", [128, 128], F32)
    Ek = T(fw, "Ek", [128, 128], F32)
    Er = T(fw, "Er", [128, 128], F32)
    aqk = T(fw, "aqk", [128, 640], BF16)
    rqk32 = T(fw, "rqk32", [128, 512], F32)
    gqk = T(fw, "gqk", [128, 256], BF16)
    scT = T(fw, "scT", [128, 512], BF16)
    stat = T(fw, "stat", [128, 32], F32)
    statA = T(fw, "statA", [128, 8], F32)
    h0F = T(fw, "h0F", [128, 16], F32)
    e0r = T(fw, "e0r", [128, 2], F32)
    s00 = T(fw, "s00", [128, 8], F32)
    ssq = [T(fw, "ssq%d" % i, [128, NT], F32) for i in range(n_layers)]
    rstdT = [T(fw, "rstd%d" % i, [128, NT], F32) for i in range(n_layers)]
    stat1 = T(fw, "stat1", [128, 8], F32)
    stat2 = T(fw, "stat2", [128, 8], F32)
    stat3 = T(fw, "stat3", [128, 8], F32)
    causalb = T(fw, "causalb", [128, 512], BF16)
    kT = [T(fw, "kT%d" % i, [128, 128], BF16) for i in range(3)]
    vext = [T(fw, "vext%d" % i, [128, 2, 65], BF16) for i in range(3)]
    ebl = [T(fw, "ebl%d" % i, [128, 2], F32) for i in range(2)]
    rqk = [T(fw, "rqk%d" % i, [128, 512], BF16) for i in range(2)]
    gkh = [T(fw, "gkh%d" % i, [128, 128], BF16) for i in range(2)]
    qbd = [[T(fw, "qbd%d%d" % (i, g), [128, 4, 128], BF16) for g in range(2)] for i in range(2)]
    rqbd = [[T(fw, "rqbd%d%d" % (i, c), [128, 2, 128], BF16) for c in range(2)] for i in range(2)]
    rqT = [T(fw, "rqT%d" % i, [128, 2, 128], BF16) for i in range(2)]
    rkT = [T(fw, "rkT%d" % i, [128, 2, 128], BF16) for i in range(2)]
    gqbd = [T(fw, "gqbd%d" % i, [128, 4, 128], BF16) for i in range(2)]
    gqT = [T(fw, "gqT%d" % i, [128, 128], BF16) for i in range(2)]
    gkT = [T(fw, "gkT%d" % i, [128, 128], BF16) for i in range(2)]
    rv = [T(fw, "rv%d" % i, [128, 256], BF16) for i in range(2)]
    gv = [T(fw, "gv%d" % i, [128, 256], BF16) for i in range(2)]
    sg1 = T(fw, "sg1", [128, D], BF16)
    rotA2 = T(fw, "rotA2", [128, 256], F32)
    rotB2 = T(fw, "rotB2", [128, 256], F32)
    AW = 5120
    arena_t = fw.sb("arena", [128, AW], F32)
    arA = Arena(arena_t, AW)
    wst = [arA.get("wst%d" % i, [128, D], F32) for i in range(2)]
    cb = arA.get("cb", [128, 8, 128], F32)
    ang = arA.get("ang", [128, NT, 32], F32)
    tu = arA.get("tu", [128, NT, 32], F32)
    tki = arA.get("tki", [128, NT, 32], I32)
    ty = arA.get("ty", [128, NT, 32], F32)
    arA2 = Arena(arena_t, AW)
    arA2.o = 3072
    wst_extra = [arA2.get("wst%d" % i, [128, D], F32) for i in (2, 3)]
    cmk = T(None, "cmk", [128, 384], F32, view=rqk32.t[:, 0:384])
    cmk.r = rqk32.r
    arB = Arena(arena_t, AW)
    rotA1 = arB.get("rotA", [128, 640], F32)
    rotB1 = arB.get("rotB", [128, 640], F32)
    th = arB.get("th", [128, 512], BF16)
    sg0 = arB.get("sg", [128, D], BF16)
    sg = [sg0, sg1]
    PT = [[arB.get("PT%d%d" % (g, b), [128, 512], BF16) for b in range(2)] for g in range(2)]
    an = arB.get("an", [128, 256], F32)
    rraw = arB.get("rraw", [128, 256], F32)
    rsq = arB.get("rsq", [128, 256], F32)
    mix = arB.get("mix", [128, D], BF16)
    mixT = arB.get("mixT", [128, 8, 128], BF16)
    dS = arB.get("dS", [128, 256], F32)

    pp = [fw.ps("pp%d" % i, [128, 512], F32) for i in range(2)]
    R_pp = [Res("pp%d" % i, excl=True) for i in range(2)]
    ptA = fw.ps("ptA", [128, 1024], BF16)
    R_tA = Res("ptA", excl=True)
    ptB = fw.ps("ptB", [128, 1024], BF16)
    R_tB = Res("ptB", excl=True)
    psc = [fw.ps("psc%d" % i, [128, 512], F32) for i in range(2)]
    R_sc = [Res("psc%d" % i, excl=True) for i in range(2)]
    po = [fw.ps("po%d" % i, [128, 512], F32) for i in range(2)]
    R_o = [Res("po%d" % i, excl=True) for i in range(2)]

    dve = lambda fn, reads=(), writes=(): fw.op("dve", fn, reads, writes)
    act = lambda fn, reads=(), writes=(): fw.op("act", fn, reads, writes)
    pool = lambda fn, reads=(), writes=(): fw.op("pool", fn, reads, writes)
    dbg_outs = {}

    def dump(name, ap, R, n):
        if dbg is None:
            return
        o = nc.dram_tensor("dbg_" + name, [128, n], F32, kind="ExternalOutput").ap()
        for a0 in range(0, n, 512):
            a1 = min(n, a0 + 512)
            dve(lambda e: e.tensor_copy(out=big32_.t[:, 0:a1 - a0], in_=ap[:, a0:a1]), [R], [big32_.r])
            fw.dma("sp", o[:, a0:a1], big32_.t[:, 0:a1 - a0], reads=[big32_.r])

    def load_weights(l, early=False):
        for kc in range(8):
            for hf in range(2):
                c0 = hf * (NW // 2)
                fw.dma("pool", win[:, kc, c0:c0 + NW // 2],
                       win_d[l, kc * 128:(kc + 1) * 128, c0:c0 + NW // 2], writes=[R_winc[kc][hf]], nobar=early)
        for kc in range(8):
            fw.dma("pool", wout[:, kc, :], wout_d[l, kc * 128:(kc + 1) * 128, :], writes=[R_woutc[kc]], nobar=True)

    fw.dma("sp", cst.t[:], const_d, writes=[cst.r])
    fw.dma("sp", xs[:, 0, :], x_d[0:128, :], writes=[R_x[0]])
    load_weights(0)
    if n_tiles > 1:
        fw.dma("sp", xs[:, 1, :], x_d[128:256, :], writes=[R_x[1]])
    dve(lambda e: e.tensor_copy(out=identb.t[:], in_=C("ident")), [cst.r], [identb.r])
    fw.dma("sp", cmk.t[:], cmask_d, writes=[cmk.r])
    for i_, dst_ in enumerate((mcurb, mprevb, causalb)):
        dve(lambda e: e.tensor_copy(out=dst_.t[:].rearrange("p (r q) -> p r q", q=128),
                                    in_=cmk.t[:, 128 * i_:128 * (i_ + 1)].unsqueeze(1).broadcast_to([128, 4, 128])),
            [cmk.r], [dst_.r])
    for i_ in range(2):
        for g in range(2):
            pool(lambda e: e.memset(qbd[i_][g].t[:], 0.0), [], [qbd[i_][g].r])
            pool(lambda e: e.memset(rqbd[i_][g].t[:], 0.0), [], [rqbd[i_][g].r])
        pool(lambda e: e.memset(gqbd[i_].t[:], 0.0), [], [gqbd[i_].r])
    for i_ in range(3):
        pool(lambda e: e.memset(vext[i_].t[:], 1.0), [], [vext[i_].r])

    posi = T(fw, "posi", [128, NT], I32)
    posf = T(fw, "posf", [128, NT], F32)
    fw.dma("sp", posi.t[:], pos_d, writes=[posi.r])
    dve(lambda e: e.tensor_copy(out=posf.t[:], in_=posi.t[:]), [posi.r], [posf.r])
    TWO_PI = float(2 * np.pi)
    PI = float(np.pi)
    C1_2PI = float(np.float32(2 * np.pi))
    C2_2PI = float(np.float32(2 * np.pi - C1_2PI))

    def wrap(dst, src, shift):
        dve(lambda e: e.tensor_scalar(out=dst.t[:], in0=src.t[:], scalar1=float(shift), scalar2=None,
                                      op0=ALU.add), [src.r], [dst.r])
        dve(lambda e: e.tensor_scalar(out=tu.t[:], in0=dst.t[:], scalar1=PI, scalar2=-TWO_PI,
                                      op0=ALU.is_gt, op1=ALU.mult), [dst.r], [tu.r])
        dve(lambda e: e.tensor_tensor(out=dst.t[:], in0=dst.t[:], in1=tu.t[:], op=ALU.add),
            [dst.r, tu.r], [dst.r])
        dve(lambda e: e.tensor_scalar(out=tu.t[:], in0=dst.t[:], scalar1=-PI, scalar2=TWO_PI,
                                      op0=ALU.is_lt, op1=ALU.mult), [dst.r], [tu.r])
        dve(lambda e: e.tensor_tensor(out=dst.t[:], in0=dst.t[:], in1=tu.t[:], op=ALU.add),
            [dst.r, tu.r], [dst.r])

    fo, _ = _COFF["freq"]
    for which in range(2):
        fr_ap = cst.t[:, fo + 32 * which: fo + 32 * which + 32].unsqueeze(1).broadcast_to([128, NT, 32])
        pos_ap = posf.t[:].unsqueeze(2).broadcast_to([128, NT, 32])
        dve(lambda e: e.tensor_tensor(out=ang.t[:], in0=pos_ap, in1=fr_ap, op=ALU.mult),
            [posf.r, cst.r], [ang.r])
        dve(lambda e: e.tensor_scalar(out=tu.t[:], in0=ang.t[:], scalar1=float(1.0 / TWO_PI), scalar2=None,
                                      op0=ALU.mult), [ang.r], [tu.r])
        dve(lambda e: e.tensor_copy(out=tki.t[:], in_=tu.t[:]), [tu.r], [tki.r])
        dve(lambda e: e.tensor_copy(out=tu.t[:], in_=tki.t[:]), [tki.r], [tu.r])
        dve(lambda e: e.scalar_tensor_tensor(out=ty.t[:], in0=tu.t[:], scalar=-C1_2PI, in1=ang.t[:],
                                             op0=ALU.mult, op1=ALU.add), [tu.r, ang.r], [ty.r])
        dve(lambda e: e.scalar_tensor_tensor(out=ty.t[:], in0=tu.t[:], scalar=-C2_2PI, in1=ty.t[:],
                                             op0=ALU.mult, op1=ALU.add), [tu.r, ty.r], [ty.r])
        wrap(ang, ty, 0.0)
        act(lambda e: e.activation(out=tabs.t[:, 2 * which + 1, :, :], in_=ang.t[:], func=AF.Sin),
            [ang.r], [tabs.r])
        wrap(ty, ang, PI / 2)
        act(lambda e: e.activation(out=tabs.t[:, 2 * which, :, :], in_=ty.t[:], func=AF.Sin),
            [ty.r], [tabs.r])

    c32 = T(fw, "c32", [128, 8], F32)
    cth = T(fw, "cth", [128, 8], F32)
    fw.dma("sp", c32.t[:], c_d, writes=[c32.r])
    act(lambda e: e.activation(out=cth.t[:], in_=c32.t[:], func=AF.Tanh, scale=0.5), [c32.r], [cth.r])
    dve(lambda e: e.scalar_tensor_tensor(out=cth.t[:], in0=cth.t[:], scalar=1.0, in1=c32.t[:],
                                         op0=ALU.add, op1=ALU.mult), [cth.r, c32.r], [cth.r])
    dve(lambda e: e.tensor_scalar(out=cth.t[:], in0=cth.t[:], scalar1=0.5, scalar2=None, op0=ALU.mult),
        [cth.r], [cth.r])

    out_toks = []

    for l in range(L):
        bmodF = T(fw, "bmodF%d" % l, [128, 24], F32)
        pregF = T(fw, "pregF%d" % l, [128, 8], F32)
        fw.dma("sp", bmodF.t[:], bmodF_d[l], writes=[bmodF.r])
        fw.dma("sp", pregF.t[:], pregF_d[l], writes=[pregF.r])
        fw.dma("sp", gwe.t[:], gwe_d[l], writes=[gwe.r])
        fw.dma("sp", ggain.t[:], ggain_d[l].partition_broadcast(128), writes=[ggain.r])
        fw.dma("sp", esink2.t[:], sink_d[l].partition_broadcast(128), writes=[esink2.r])
        act(lambda e: e.activation(out=esink2.t[:], in_=esink2.t[:], func=AF.Exp), [esink2.r], [esink2.r])
        dve(lambda e: e.tensor_scalar(out=esink2.t[:], in0=esink2.t[:], scalar1=2.0, scalar2=None,
                                      op0=ALU.mult), [esink2.r], [esink2.r])
        dve(lambda e: e.tensor_copy(out=cb.t[:], in_=cth.t[:].unsqueeze(2).broadcast_to([128, 8, 128])),
            [cth.r], [cb.r])
        banks = [(pp[0], R_pp[0]), (pp[1], R_pp[1]), (psc[0], R_sc[0]), (psc[1], R_sc[1]),
                 (po[0], R_o[0]), (po[1], R_o[1])]
        i = 0
        for kc in range(8):
            for third in range(3):
                stl = wst if l == 0 else wst + wst_extra
                st = stl[i % len(stl)]
                i += 1
                fw.dma("sp", st.t[:], wmod_d[l, kc * 128:(kc + 1) * 128, third * 1024:(third + 1) * 1024],
                       writes=[st.r])
                for hf in range(2):
                    bk, rb = banks[third * 2 + hf]
                    fw.mm(lambda e: e.matmul(bk[:, :], lhsT=cb.t[:, kc, :], rhs=st.t[:, hf * 512:(hf + 1) * 512],
                                             start=(kc == 0), stop=(kc == 7)),
                          reads=[cb.r, st.r], writes=[rb], last=True)
        for which, dst in ((0, shF), (1, small)):
            for hf in range(2):
                bk, rb = banks[which * 2 + hf]
                dve(lambda e: e.tensor_tensor(
                    out=big32_.t[:].rearrange("p (k n) -> p k n", n=128),
                    in0=bk[:, :].rearrange("p (k n) -> p k n", n=128),
                    in1=C("ident").unsqueeze(1).broadcast_to([128, 4, 128]), op=ALU.mult),
                    [rb, cst.r], [big32_.r])
                dve(lambda e: e.tensor_reduce(out=dst.t[:, 4 * hf:4 * hf + 4],
                                              in_=big32_.t[:].rearrange("p (k n) -> p k n", n=128),
                                              op=ALU.add, axis=AX.X), [big32_.r], [dst.r])
        dve(lambda e: e.tensor_tensor(out=shF.t[:], in0=shF.t[:], in1=bmodF.t[:, 0:8], op=ALU.add),
            [shF.r, bmodF.r], [shF.r])
        dve(lambda e: e.tensor_tensor(out=small.t[:, 0:8], in0=small.t[:, 0:8], in1=bmodF.t[:, 8:16], op=ALU.add),
            [small.r, bmodF.r], [small.r])
        dve(lambda e: e.scalar_tensor_tensor(out=gmF.t[:], in0=small.t[:, 0:8], scalar=1.0, in1=pregF.t[:],
                                             op0=ALU.add, op1=ALU.mult), [small.r, pregF.r], [gmF.r])
        fw.dma("sp", gpg.t[:], bgate_d[l].partition_broadcast(128), writes=[gpg.r])
        for hf in range(2):
            bk, rb = banks[4 + hf]
            dve(lambda e: e.tensor_tensor(out=gpg.t[:, hf * 512:(hf + 1) * 512], in0=bk[:, :],
                                          in1=gpg.t[:, hf * 512:(hf + 1) * 512], op=ALU.add),
                [rb, gpg.r], [gpg.r])
        for hf in range(2):
            fw.dma("sp", rqk32.t[:], postg_d[l, hf * 512:(hf + 1) * 512].partition_broadcast(128), writes=[rqk32.r])
            dve(lambda e: e.tensor_tensor(out=gpg.t[:, hf * 512:(hf + 1) * 512], in0=gpg.t[:, hf * 512:(hf + 1) * 512],
                                          in1=rqk32.t[:], op=ALU.mult), [gpg.r, rqk32.r], [gpg.r])
        nst = min(2, n_tiles) if l == 0 else n_tiles
        if l == 0:
            for t_ in range(nst):
                act(lambda e: e.activation(out=xn.t[:], in_=xs[:, t_, :], func=AF.Square,
                                           accum_out=ssq[0].t[:, t_:t_ + 1]), [R_x[t_]], [xn.r, ssq[0].r])
        dve(lambda e: e.tensor_scalar(out=rstdT[l].t[:, 0:nst], in0=ssq[l].t[:, 0:nst], scalar1=1.0 / D,
                                      scalar2=EPS, op0=ALU.mult, op1=ALU.add), [ssq[l].r], [rstdT[l].r])
        pool(lambda e: e.tensor_tensor(out=rstdT[l].t[:, 0:nst], in0=rstdT[l].t[:, 0:nst],
                                       in1=C("neghalf16")[:, 0:nst], op=ALU.pow), [rstdT[l].r, cst.r], [rstdT[l].r])
        dve(lambda e: e.tensor_scalar(out=e0r.t[:], in0=C("ident")[:, 0:2], scalar1=rstdT[l].t[:, 0:1], scalar2=None,
                                      op0=ALU.mult), [cst.r, rstdT[l].r], [e0r.r])
        for kc in range(8):
            fw.mm(lambda e: e.matmul(po[1][:, 2 * kc:2 * kc + 2], lhsT=xs[:, 0, kc * 128:(kc + 1) * 128], rhs=e0r.t[:],
                                     start=True, stop=True),
                  reads=[R_x[0], e0r.r], writes=[R_o[1]], last=(kc == 7))
        dve(lambda e: e.tensor_tensor(out=h0F.t[:, 0:8], in0=po[1][:, 0:16].rearrange("p (k two) -> p k two", two=2)[:, :, 0],
                                      in1=gmF.t[:], op=ALU.mult), [R_o[1], gmF.r], [h0F.r])
        dve(lambda e: e.tensor_tensor(out=h0F.t[:, 0:8], in0=h0F.t[:, 0:8], in1=shF.t[:], op=ALU.add),
            [h0F.r, shF.r], [h0F.r])
        dve(lambda e: e.tensor_copy(out=cb.t[:], in_=h0F.t[:, 0:8].unsqueeze(2).broadcast_to([128, 8, 128])),
            [h0F.r], [cb.r])
        for kc in range(8):
            st = wst[kc % 2]
            fw.dma("sp", st.t[:, 0:768], wqk_d[l, kc * 128:(kc + 1) * 128, :], writes=[st.r])
            for hf in range(2):
                fw.mm(lambda e: e.matmul(pp[hf][:, 0:384], lhsT=cb.t[:, kc, :], rhs=st.t[:, hf * 384:(hf + 1) * 384],
                                         start=(kc == 0), stop=(kc == 7)),
                      reads=[cb.r, st.r], writes=[R_pp[hf]], last=True)
        act(lambda e: e.activation(out=big32_.t[:, 0:384], in_=pp[0][:, 0:384], func=AF.Copy), [R_pp[0]], [big32_.r])
        dve(lambda e: e.tensor_tensor(out=big32_.t[:, 0:384], in0=big32_.t[:, 0:384], in1=pp[1][:, 0:384], op=ALU.mult),
            [big32_.r, R_pp[1]], [big32_.r])
        dve(lambda e: e.tensor_reduce(out=s00.t[:, 0:4], in_=big32_.t[:, 0:256].rearrange("p (h d) -> p h d", d=64),
                                      axis=AX.X, op=ALU.add), [big32_.r], [s00.r])
        dve(lambda e: e.tensor_reduce(out=s00.t[:, 4:8], in_=big32_.t[:, 256:384].rearrange("p (h d) -> p h d", d=32),
                                      axis=AX.X, op=ALU.add), [big32_.r], [s00.r])
        dve(lambda e: e.tensor_scalar(out=s00.t[:, 0:4], in0=s00.t[:, 0:4], scalar1=0.125, scalar2=None, op0=ALU.mult),
            [s00.r], [s00.r])
        dve(lambda e: e.tensor_scalar(out=s00.t[:, 4:8], in0=s00.t[:, 4:8], scalar1=float(32 ** -0.5), scalar2=None,
                                      op0=ALU.mult), [s00.r], [s00.r])
        pool(lambda e: e.memset(retS.t[:], 0.0), [], [retS.r])
        pool(lambda e: e.memset(retSb.t[:], 0.0), [], [retSb.r])
        pool(lambda e: e.memset(glaS.t[:], 0.0), [], [glaS.r])
        pool(lambda e: e.memset(glaSb.t[:], 0.0), [], [glaSb.r])

        fw.barrier()
        def genA(t):
            p2 = t % 2
            p3 = t % 3
            xt = xs[:, t, :]
            Rx = R_x[t]
            AB = [(pp[0], R_pp[0]), (pp[1], R_pp[1]), (psc[1], R_sc[1])]
            GB = (psc[0], R_sc[0])
            dve(lambda e: e.tensor_scalar(out=xn.t[:], in0=xt, scalar1=rstdT[l].t[:, t:t + 1], scalar2=None,
                                          op0=ALU.mult), [Rx, rstdT[l].r], [xn.r])
            for kc in range(8):
                fw.mm(lambda e: e.transpose(out=ptA[:, kc * 128:(kc + 1) * 128], in_=xn.t[:, kc * 128:(kc + 1) * 128],
                                            identity=identb.t[:]),
                      reads=[xn.r, identb.r], writes=[R_tA], last=(kc == 7))
            if l == 0 and t + 2 < n_tiles:
                t2 = t + 2
                act(lambda e: e.activation(out=xn.t[:], in_=xs[:, t2, :], func=AF.Square,
                                           accum_out=ssq[0].t[:, t2:t2 + 1]), [R_x[t2]], [xn.r, ssq[0].r])
                dve(lambda e: e.tensor_scalar(out=rstdT[0].t[:, t2:t2 + 1], in0=ssq[0].t[:, t2:t2 + 1], scalar1=1.0 / D,
                                              scalar2=EPS, op0=ALU.mult, op1=ALU.add), [ssq[0].r], [rstdT[0].r])
                pool(lambda e: e.tensor_tensor(out=rstdT[0].t[:, t2:t2 + 1], in0=rstdT[0].t[:, t2:t2 + 1],
                                               in1=C("neghalf")[:, 0:1], op=ALU.pow), [rstdT[0].r, cst.r], [rstdT[0].r])
            for kc in range(8):
                if kc < 4:
                    act(lambda e: e.activation(out=hT.t[:, kc, :], in_=ptA[:, kc * 128:(kc + 1) * 128],
                                               func=AF.Identity, scale=gmF.t[:, kc:kc + 1], bias=shF.t[:, kc:kc + 1]),
                        [R_tA, gmF.r, shF.r], [hT.r])
                else:
                    dve(lambda e: e.tensor_scalar(out=hT.t[:, kc, :], in0=ptA[:, kc * 128:(kc + 1) * 128],
                                                  scalar1=gmF.t[:, kc:kc + 1], scalar2=shF.t[:, kc:kc + 1],
                                                  op0=ALU.mult, op1=ALU.add),
                        [R_tA, gmF.r, shF.r], [hT.r])
            yield

            def proj(ci, bank):
                for kc in range(8):
                    fw.mm(lambda e: e.matmul(AB[bank][0][:, :], lhsT=hT.t[:, kc, :],
                                             rhs=win[:, kc, COFFS[ci]:COFFS[ci] + 512],
                                             start=(kc == 0), stop=(kc == 7)),
                          reads=[hT.r] + R_winc[kc], writes=[AB[bank][1]], last=(kc == 7))

            cA = tabs.t[:, 0, t, :]
            sA = tabs.t[:, 1, t, :]
            cR = tabs.t[:, 2, t, :]
            sR = tabs.t[:, 3, t, :]

            def rotary(src, R_src, nh, cos, sin, dst, dst_off, R_dst, final_eng):
                n = nh * 64
                rotA, rotB = (rotA1, rotB1) if nh == 8 else (rotA2, rotB2)
                s4 = src.rearrange("p (h two f) -> p h two f", two=2, f=32)
                a4 = rotA.t[:, 0:n].rearrange("p (h two f) -> p h two f", two=2, f=32)
                b4 = rotB.t[:, 0:n].rearrange("p (h two f) -> p h two f", two=2, f=32)
                d4 = dst[:, dst_off:dst_off + n].rearrange("p (h two f) -> p h two f", two=2, f=32)
                cos4 = cos.unsqueeze(1).unsqueeze(1).broadcast_to([128, nh, 2, 32])
                sin3 = sin.unsqueeze(1).broadcast_to([128, nh, 32])
                dve(lambda e: e.tensor_tensor(out=a4, in0=s4, in1=cos4, op=ALU.mult),
                    [R_src, tabs.r], [rotA.r])
                dve(lambda e: e.tensor_tensor(out=b4[:, :, 0, :], in0=s4[:, :, 1, :], in1=sin3, op=ALU.mult),
                    [R_src, tabs.r], [rotB.r])
                dve(lambda e: e.tensor_tensor(out=b4[:, :, 1, :], in0=s4[:, :, 0, :], in1=sin3, op=ALU.mult),
                    [R_src, tabs.r], [rotB.r])
                fw.op(final_eng, lambda e: e.tensor_tensor(out=d4[:, :, 0, :], in0=a4[:, :, 0, :],
                                                           in1=b4[:, :, 0, :], op=ALU.subtract),
                      [rotA.r, rotB.r], [R_dst])
                fw.op(final_eng, lambda e: e.tensor_tensor(out=d4[:, :, 1, :], in0=a4[:, :, 1, :],
                                                           in1=b4[:, :, 1, :], op=ALU.add),
                      [rotA.r, rotB.r], [R_dst])

            for kc in range(8):
                fw.mm(lambda e: e.matmul(GB[0][:, 0:128], lhsT=win[:, kc, 0:128], rhs=hT.t[:, kc, :],
                                         start=(kc == 0), stop=(kc == 7)),
                      reads=[hT.r] + R_winc[kc], writes=[GB[1]], last=(kc == 7))
            gmk = C("gamask")
            act(lambda e: e.activation(out=gaTe.t[:], in_=GB[0][:, 0:128], func=AF.Identity,
                                       scale=gmk[:, 0:1], bias=gmk[:, 1:2]), [GB[1], cst.r], [gaTe.r])
            proj(0, 0)
            fw.mm(lambda e: e.matmul(GB[0][:, 128:256], lhsT=gaTe.t[:], rhs=gwe.t[:], start=True, stop=True),
                  reads=[gaTe.r, gwe.r], writes=[GB[1]], last=True)
            act(lambda e: e.activation(out=nl.t[:], in_=GB[0][:, 128:256], func=AF.Exp, scale=-1.0),
                [GB[1]], [nl.r])
            act(lambda e: e.activation(out=nl.t[:], in_=nl.t[:], func=AF.Ln, bias=1.0), [nl.r], [nl.r])
            rotary(AB[0][0][:, 0:512], AB[0][1], 8, cA, sA, aqk.t, 0, aqk.r, "pool")
            yield
            proj(1, 1)
            rotary(AB[1][0][:, 0:128], AB[1][1], 2, cA, sA, aqk.t, 512, aqk.r, "pool")
            act(lambda e: e.activation(out=vext[p3].t[:, :, 0:64],
                                       in_=AB[1][0][:, 128:256].rearrange("p (g d) -> p g d", d=64), func=AF.Copy),
                [AB[1][1]], [vext[p3].r])
            rotary(AB[1][0][:, 256:512], AB[1][1], 4, cR, sR, rqk32.t, 0, rqk32.r, "pool")
            yield
            proj(2, 2)
            rotary(AB[2][0][:, 0:256], AB[2][1], 4, cR, sR, rqk32.t, 256, rqk32.r, "pool")
            act(lambda e: e.activation(out=rv[p2].t[:], in_=AB[2][0][:, 256:512], func=AF.Copy), [AB[2][1]], [rv[p2].r])
            pool(lambda e: e.tensor_tensor(out=rqk[p2].t[:].rearrange("p (h d) -> p h d", d=64),
                                           in0=rqk32.t[:].rearrange("p (h d) -> p h d", d=64),
                                           in1=C("dec8").unsqueeze(2).broadcast_to([128, 8, 64]), op=ALU.mult),
                 [rqk32.r, cst.r], [rqk[p2].r])
            yield
            for ci, bank in ((3, 0), (4, 1)):
                proj(ci, bank)
                o = (ci - 3) * 512
                act(lambda e: e.activation(out=th.t[:], in_=AB[bank][0][:, :], func=AF.Tanh, scale=0.5),
                    [AB[bank][1]], [th.r])
                dve(lambda e: e.scalar_tensor_tensor(out=sg[p2].t[:, o:o + 512], in0=th.t[:], scalar=1.0,
                                                     in1=AB[bank][0][:, :], op0=ALU.add, op1=ALU.mult),
                    [th.r, AB[bank][1]], [sg[p2].r])
                yield
            fw.mm(lambda e: e.matmul(GB[0][:, 128:256], lhsT=C("tri_in"), rhs=nl.t[:], start=True, stop=True),
                  reads=[cst.r, nl.r], writes=[GB[1]], last=False)
            fw.mm(lambda e: e.matmul(GB[0][:, 256:384], lhsT=C("tri_rev"), rhs=nl.t[:], start=True, stop=True),
                  reads=[cst.r, nl.r], writes=[GB[1]], last=False)
            fw.mm(lambda e: e.matmul(GB[0][:, 384:386], lhsT=nl.t[:], rhs=C("ncol"), start=True, stop=True),
                  reads=[cst.r, nl.r], writes=[GB[1]], last=True)
            act(lambda e: e.activation(out=Eq.t[:], in_=GB[0][:, 128:256], func=AF.Exp), [GB[1]], [Eq.r])
            act(lambda e: e.activation(out=Ek.t[:], in_=GB[0][:, 128:256], func=AF.Exp, scale=-1.0),
                [GB[1]], [Ek.r])
            act(lambda e: e.activation(out=Er.t[:], in_=GB[0][:, 256:384], func=AF.Exp), [GB[1]], [Er.r])
            act(lambda e: e.activation(out=ebl[p2].t[:], in_=GB[0][:, 384:386], func=AF.Exp), [GB[1]], [ebl[p2].r])
            proj(5, 2)
            dve(lambda e: e.scalar_tensor_tensor(out=gqk.t[:, 0:128], in0=AB[2][0][:, 0:128], scalar=float(32 ** -0.5),
                                                 in1=Eq.t[:], op0=ALU.mult, op1=ALU.mult),
                [AB[2][1], Eq.r], [gqk.r])
            dve(lambda e: e.tensor_tensor(out=gqk.t[:, 128:256], in0=AB[2][0][:, 128:256], in1=Ek.t[:], op=ALU.mult),
                [AB[2][1], Ek.r], [gqk.r])
            dve(lambda e: e.tensor_tensor(out=gkh[p2].t[:], in0=AB[2][0][:, 128:256], in1=Er.t[:], op=ALU.mult),
                [AB[2][1], Er.r], [gkh[p2].r])
            act(lambda e: e.activation(out=gv[p2].t[:], in_=AB[2][0][:, 256:512], func=AF.Copy), [AB[2][1]], [gv[p2].r])
            yield
            for j in range(5):
                fw.mm(lambda e: e.transpose(out=ptB[:, j * 128:(j + 1) * 128], in_=aqk.t[:, j * 128:(j + 1) * 128],
                                            identity=identb.t[:]),
                      reads=[aqk.r, identb.r], writes=[R_tB], last=(j == 4))
            src4 = ptB[:, 0:512].rearrange("p (c q) -> p c q", q=128)
            act(lambda e: e.activation(out=qbd[p2][0].t[0:64, :, :], in_=src4[0:64, :, :], func=AF.Copy),
                [R_tB], [qbd[p2][0].r])
            act(lambda e: e.activation(out=kT[p3].t[:], in_=ptB[:, 512:640], func=AF.Copy), [R_tB], [kT[p3].r])
            dve(lambda e: e.tensor_copy(out=qbd[p2][1].t[64:128, :, :], in_=src4[64:128, :, :]), [R_tB], [qbd[p2][1].r])
            yield
            for j in range(4):
                fw.mm(lambda e: e.transpose(out=ptB[:, j * 128:(j + 1) * 128], in_=rqk[p2].t[:, j * 128:(j + 1) * 128],
                                            identity=identb.t[:]),
                      reads=[rqk[p2].r, identb.r], writes=[R_tB], last=False)
            for j in range(2):
                fw.mm(lambda e: e.transpose(out=ptB[:, (4 + j) * 128:(5 + j) * 128],
                                            in_=gqk.t[:, j * 128:(j + 1) * 128], identity=identb.t[:]),
                      reads=[gqk.r, identb.r], writes=[R_tB], last=(j == 1))
            for c in range(2):
                act(lambda e: e.activation(out=rqbd[p2][c].t[0:64, 0, :], in_=ptB[0:64, c * 128:(c + 1) * 128],
                                           func=AF.Copy), [R_tB], [rqbd[p2][c].r])
            act(lambda e: e.activation(out=rkT[p2].t[:].rearrange("p c q -> p (c q)"), in_=ptB[:, 256:512],
                                       func=AF.Copy), [R_tB], [rkT[p2].r])
            act(lambda e: e.activation(out=gqT[p2].t[:], in_=ptB[:, 512:640], func=AF.Copy), [R_tB], [gqT[p2].r])
            for hh in (0, 2):
                act(lambda e: e.activation(out=gqbd[p2].t[32 * hh:32 * hh + 32, hh, :],
                                           in_=ptB[32 * hh:32 * hh + 32, 512:640], func=AF.Copy),
                    [R_tB], [gqbd[p2].r])
            for c in range(2):
                dve(lambda e: e.tensor_copy(out=rqbd[p2][c].t[64:128, 1, :], in_=ptB[64:128, c * 128:(c + 1) * 128]),
                    [R_tB], [rqbd[p2][c].r])
            dve(lambda e: e.tensor_copy(out=rqT[p2].t[:].rearrange("p c q -> p (c q)"), in_=ptB[:, 0:256]),
                [R_tB], [rqT[p2].r])
            for hh in (1, 3):
                dve(lambda e: e.tensor_copy(out=gqbd[p2].t[32 * hh:32 * hh + 32, hh, :],
                                            in_=ptB[32 * hh:32 * hh + 32, 512:640]), [R_tB], [gqbd[p2].r])
            dve(lambda e: e.tensor_copy(out=gkT[p2].t[:], in_=ptB[:, 640:768]), [R_tB], [gkT[p2].r])
            yield

        def genB(t):
            p2 = t % 2
            p3 = t % 3
            pv3 = (t - 1) % 3
            xt = xs[:, t, :]
            Rx = R_x[t]
            blocks = [(p3, mcurb)] + ([(pv3, mprevb)] if t > 0 else [])
            for g in range(2):
                for bi, (kp, mk) in enumerate(blocks):
                    bank = 0
                    fw.mm(lambda e: e.matmul(psc[bank][:, :], lhsT=kT[kp].t[:],
                                             rhs=qbd[p2][g].t[:].rearrange("p c q -> p (c q)"), start=True, stop=False),
                          reads=[kT[kp].r, qbd[p2][g].r], writes=[R_sc[bank]], last=False)
                    fw.mm(lambda e: e.matmul(psc[bank][:, :], lhsT=identb.t[:], rhs=mk.t[:], start=False, stop=True),
                          reads=[identb.r, mk.r], writes=[R_sc[bank]], last=True)
                    act(lambda e: e.activation(out=PT[g][bi].t[:], in_=psc[bank][:, :], func=AF.Exp, scale=0.125),
                        [R_sc[bank]], [PT[g][bi].r])
                yield
            for g in range(2):
                for c in range(4):
                    for bi, (kp, mk) in enumerate(blocks):
                        fw.mm(lambda e: e.matmul(po[g][:, c * 65:(c + 1) * 65], lhsT=PT[g][bi].t[:, c * 128:(c + 1) * 128],
                                                 rhs=vext[kp].t[:, g, :], start=(bi == 0), stop=(bi == len(blocks) - 1)),
                              reads=[PT[g][bi].r, vext[kp].r], writes=[R_o[g]],
                              last=(c == 3 and bi == len(blocks) - 1))
                o4 = po[g][:, 0:260].rearrange("p (c d) -> p c d", d=65)
                dve(lambda e: e.scalar_tensor_tensor(out=stat1.t[:, 0:4], in0=o4[:, :, 64], scalar=2.0,
                                                     in1=esink2.t[:, 4 * g:4 * g + 4], op0=ALU.mult, op1=ALU.add),
                    [R_o[g], esink2.r], [stat1.r])
                dve(lambda e: e.reciprocal(out=stat1.t[:, 4:8], in_=stat1.t[:, 0:4]), [stat1.r], [stat1.r])
                for c in range(4):
                    dve(lambda e: e.scalar_tensor_tensor(out=mix.t[:, g * 256 + c * 64:g * 256 + (c + 1) * 64],
                                                         in0=o4[:, c, 0:64], scalar=stat1.t[:, 4 + c:5 + c],
                                                         in1=sg[p2].t[:, g * 256 + c * 64:g * 256 + (c + 1) * 64],
                                                         op0=ALU.mult, op1=ALU.mult),
                        [R_o[g], stat1.r, sg[p2].r], [mix.r])
                yield

            def head_norm(bank, R_bank, off, gain, rraw, rsqt, rsq_r, stat2):
                act(lambda e: e.activation(out=rraw.t[:], in_=bank[:, 0:256], func=AF.Copy), [R_bank], [rraw.r])
                act(lambda e: e.activation(out=rsqt, in_=bank[:, 0:256], func=AF.Square), [R_bank], [rsq_r])
                dve(lambda e: e.tensor_reduce(out=stat2.t[:, 0:4], in_=rsqt.rearrange("p (h d) -> p h d", d=64),
                                              axis=AX.X, op=ALU.add), [rsq_r], [stat2.r])
                dve(lambda e: e.tensor_scalar(out=stat2.t[:, 0:4], in0=stat2.t[:, 0:4], scalar1=4.0 / 64.0,
                                              scalar2=4.0 * EPS, op0=ALU.mult, op1=ALU.add), [stat2.r], [stat2.r])
                pool(lambda e: e.tensor_tensor(out=stat2.t[:, 4:8], in0=stat2.t[:, 0:4], in1=C("neghalf")[:, 0:4],
                                               op=ALU.pow), [stat2.r, cst.r], [stat2.r])
                dve(lambda e: e.tensor_tensor(out=rraw.t[:].rearrange("p (h d) -> p h d", d=64),
                                              in0=rraw.t[:].rearrange("p (h d) -> p h d", d=64),
                                              in1=stat2.t[:, 4:8].unsqueeze(2).broadcast_to([128, 4, 64]), op=ALU.mult),
                    [rraw.r, stat2.r], [rraw.r])
                if gain is not None:
                    pool(lambda e: e.tensor_tensor(out=rraw.t[:].rearrange("p (h d) -> p h d", d=64),
                                                   in0=rraw.t[:].rearrange("p (h d) -> p h d", d=64),
                                                   in1=gain.t[:].unsqueeze(1).broadcast_to([128, 4, 64]), op=ALU.mult),
                         [rraw.r, gain.r], [rraw.r])
                pool(lambda e: e.tensor_tensor(out=mix.t[:, off:off + 256], in0=rraw.t[:], in1=sg[p2].t[:, off:off + 256],
                                               op=ALU.mult), [rraw.r, sg[p2].r], [mix.r])

            for c in range(2):
                fw.mm(lambda e: e.matmul(psc[0][:, c * 256:(c + 1) * 256], lhsT=rkT[p2].t[:, c, :],
                                         rhs=rqbd[p2][c].t[:].rearrange("p h q -> p (h q)"), start=True, stop=True),
                      reads=[rkT[p2].r, rqbd[p2][c].r], writes=[R_sc[0]], last=(c == 1))
            dve(lambda e: e.tensor_tensor(out=scT.t[:], in0=psc[0][:, :], in1=causalb.t[:], op=ALU.mult),
                [R_sc[0], causalb.r], [scT.r])
            if t == 0:
                dve(lambda e: e.tensor_copy(out=scT.t[0:1, :].rearrange("p (h i) -> p h i", i=128)[:, :, 0],
                                            in_=s00.t[0:1, 0:4]), [scT.r, s00.r], [scT.r])
            yield
            for hh in range(4):
                fw.mm(lambda e: e.matmul(po[0][:, hh * 64:(hh + 1) * 64], lhsT=scT.t[:, hh * 128:(hh + 1) * 128],
                                         rhs=rv[p2].t[:, hh * 64:(hh + 1) * 64], start=(hh == 0), stop=False),
                      reads=[scT.r, rv[p2].r], writes=[R_o[0]], last=False)
            for c in range(2):
                fw.mm(lambda e: e.matmul(po[0][:, c * 128:(c + 1) * 128], lhsT=rqT[p2].t[:, c, :],
                                         rhs=retSb.t[:, c * 128:(c + 1) * 128], start=False, stop=(c == 1)),
                      reads=[rqT[p2].r, retSb.r], writes=[R_o[0]], last=False)
            for c in range(2):
                fw.mm(lambda e: e.matmul(po[0][:, 256 + c * 128:256 + (c + 1) * 128],
                                         lhsT=rqk[p2].t[:, 256 + c * 128:256 + (c + 1) * 128],
                                         rhs=rv[p2].t[:, c * 128:(c + 1) * 128], start=True, stop=True),
                      reads=[rqk[p2].r, rv[p2].r], writes=[R_o[0]], last=(c == 1))
            head_norm(po[0], R_o[0], 512, None, rraw, rsq.t[:], rsq.r, stat2)
            dve(lambda e: e.tensor_tensor(out=dS.t[:], in0=po[0][:, 256:512], in1=C("bd64dec"), op=ALU.mult),
                [R_o[0], cst.r], [dS.r])
            pool(lambda e: e.tensor_tensor(out=retS.t[:].rearrange("p (c n) -> p c n", n=128),
                                           in0=retS.t[:].rearrange("p (c n) -> p c n", n=128),
                                           in1=C("sdec").unsqueeze(2).broadcast_to([128, 2, 128]), op=ALU.mult),
                 [retS.r, cst.r], [retS.r])
            pool(lambda e: e.tensor_tensor(out=retS.t[:], in0=retS.t[:], in1=dS.t[:], op=ALU.add),
                 [retS.r, dS.r], [retS.r])
            act(lambda e: e.activation(out=retSb.t[:], in_=retS.t[:], func=AF.Copy), [retS.r], [retSb.r])
            yield

            fw.mm(lambda e: e.matmul(psc[0][:, :], lhsT=gkT[p2].t[:], rhs=gqbd[p2].t[:].rearrange("p h q -> p (h q)"),
                                     start=True, stop=True),
                  reads=[gkT[p2].r, gqbd[p2].r], writes=[R_sc[0]], last=True)
            dve(lambda e: e.tensor_tensor(out=scT.t[:], in0=psc[0][:, :], in1=causalb.t[:], op=ALU.mult),
                [R_sc[0], causalb.r], [scT.r])
            if t == 0:
                dve(lambda e: e.tensor_copy(out=scT.t[0:1, :].rearrange("p (h i) -> p h i", i=128)[:, :, 0],
                                            in_=s00.t[0:1, 4:8]), [scT.r, s00.r], [scT.r])
            yield
            for hh in range(4):
                fw.mm(lambda e: e.matmul(po[1][:, hh * 64:(hh + 1) * 64], lhsT=scT.t[:, hh * 128:(hh + 1) * 128],
                                         rhs=gv[p2].t[:, hh * 64:(hh + 1) * 64], start=(hh == 0), stop=False),
                      reads=[scT.r, gv[p2].r], writes=[R_o[1]], last=False)
            fw.mm(lambda e: e.matmul(po[1][:, 0:256], lhsT=gqT[p2].t[:], rhs=glaSb.t[:], start=False, stop=True),
                  reads=[gqT[p2].r, glaSb.r], writes=[R_o[1]], last=False)
            fw.mm(lambda e: e.matmul(po[1][:, 256:512], lhsT=gkh[p2].t[:], rhs=gv[p2].t[:], start=True, stop=True),
                  reads=[gkh[p2].r, gv[p2].r], writes=[R_o[1]], last=True)
            head_norm(po[1], R_o[1], 768, ggain, an, big32_.t[:, 0:256], big32_.r, stat3)
            dve(lambda e: e.tensor_tensor(out=dS.t[:], in0=po[1][:, 256:512], in1=C("bd32"), op=ALU.mult),
                [R_o[1], cst.r], [dS.r])
            dve(lambda e: e.scalar_tensor_tensor(out=glaS.t[:], in0=glaS.t[:], scalar=ebl[p2].t[:, 0:1], in1=dS.t[:],
                                                 op0=ALU.mult, op1=ALU.add), [glaS.r, ebl[p2].r, dS.r], [glaS.r])
            act(lambda e: e.activation(out=glaSb.t[:], in_=glaS.t[:], func=AF.Copy), [glaS.r], [glaSb.r])
            yield

            if dbg is not None and l == 0 and t in dbg:
                dump("hT_%d" % t, hT.t[:].rearrange("p k q -> p (k q)"), hT.r, 1024)
                dump("aqk_%d" % t, aqk.t[:], aqk.r, 640)
                dump("rqk_%d" % t, rqk[p2].t[:], rqk[p2].r, 512)
                dump("gqk_%d" % t, gqk.t[:], gqk.r, 256)
                dump("gkh_%d" % t, gkh[p2].t[:], gkh[p2].r, 128)
                dump("sg_%d" % t, sg[p2].t[:], sg[p2].r, 1024)
                dump("nl_%d" % t, nl.t[:], nl.r, 128)
                dump("mix_%d" % t, mix.t[:], mix.r, 1024)
                dump("retS_%d" % t, retS.t[:], retS.r, 256)
                dump("glaS_%d" % t, glaS.t[:], glaS.r, 256)
            for j in range(8):
                fw.mm(lambda e: e.transpose(out=ptB[:, j * 128:(j + 1) * 128], in_=mix.t[:, j * 128:(j + 1) * 128],
                                            identity=identb.t[:]),
                      reads=[mix.r, identb.r], writes=[R_tB], last=(j == 7))
            act(lambda e: e.activation(out=mixT.t[:, 0:4, :].rearrange("p k q -> p (k q)"), in_=ptB[:, 0:512],
                                       func=AF.Copy), [R_tB], [mixT.r])
            dve(lambda e: e.tensor_copy(out=mixT.t[:, 4:8, :].rearrange("p k q -> p (k q)"), in_=ptB[:, 512:1024]),
                [R_tB], [mixT.r])
            yield
            for hf in range(2):
                for kc in range(8):
                    fw.mm(lambda e: e.matmul(po[hf][:, :], lhsT=mixT.t[:, kc, :],
                                             rhs=wout[:, kc, hf * 512:(hf + 1) * 512], start=(kc == 0), stop=(kc == 7)),
                          reads=[mixT.r, R_woutc[kc]], writes=[R_o[hf]], last=(kc == 7))
                yield
            for hf in range(2):
                act(lambda e: e.activation(out=mix.t[:, hf * 512:(hf + 1) * 512], in_=po[hf][:, :], func=AF.Square,
                                           accum_out=stat.t[:, 16 + hf:17 + hf]), [R_o[hf]], [mix.r, stat.r])
            dve(lambda e: e.tensor_tensor(out=stat.t[:, 18:19], in0=stat.t[:, 16:17], in1=stat.t[:, 17:18], op=ALU.add),
                [stat.r], [stat.r])
            dve(lambda e: e.tensor_scalar(out=stat.t[:, 19:20], in0=stat.t[:, 18:19], scalar1=1.0 / D, scalar2=EPS,
                                          op0=ALU.mult, op1=ALU.add), [stat.r], [stat.r])
            pool(lambda e: e.tensor_tensor(out=stat.t[:, 20:21], in0=stat.t[:, 19:20], in1=C("neghalf")[:, 0:1],
                                           op=ALU.pow), [stat.r, cst.r], [stat.r])
            for hf in range(2):
                dve(lambda e: e.scalar_tensor_tensor(out=big32[hf].t[:], in0=po[hf][:, :],
                                                     scalar=stat.t[:, 20:21], in1=gpg.t[:, hf * 512:(hf + 1) * 512],
                                                     op0=ALU.mult, op1=ALU.mult),
                    [R_o[hf], stat.r, gpg.r], [big32[hf].r])
                dve(lambda e: e.tensor_tensor(out=xt[:, hf * 512:(hf + 1) * 512], in0=xt[:, hf * 512:(hf + 1) * 512],
                                              in1=big32[hf].t[:], op=ALU.add), [Rx, big32[hf].r], [Rx])
            if l == L - 1:
                out_toks.append(fw.dma("sp", out_d[t * 128:(t + 1) * 128, :], xt, reads=[Rx]))
            else:
                act(lambda e: e.activation(out=mix.t[:], in_=xt, func=AF.Square, accum_out=ssq[l + 1].t[:, t:t + 1]),
                    [Rx], [mix.r, ssq[l + 1].r])
            yield

        if l == 0:
            for t_ in range(2, n_tiles):
                fw.dma("sp", xs[:, t_, :], x_d[t_ * 128:(t_ + 1) * 128, :], writes=[R_x[t_]], nobar=True)
        for _ in genA(0):
            pass
        for t in range(n_tiles):
            gb = genB(t)
            ga_ = genA(t + 1) if t + 1 < n_tiles else iter(())
            alive_a = alive_b = True
            while alive_a or alive_b:
                if alive_b:
                    try:
                        next(gb)
                    except StopIteration:
                        alive_b = False
                if alive_a:
                    try:
                        next(ga_)
                    except StopIteration:
                        alive_a = False
        if l + 1 < L:
            load_weights(l + 1, early=True)
        if dbg is not None and l == 0:
            dump("gmF", gmF.t[:], gmF.r, 8)
            dump("shF", shF.t[:], shF.r, 8)
            dump("gpg", gpg.t[:], gpg.r, 1024)
            dump("tabs", tabs.t[:].rearrange("p a t f -> p (a t f)"), tabs.r, 4 * NT * 32)
        fw.barrier()

    fw.flush()
    e = fw.engs["sp"]
    for s in fw.dsems + fw.dsems_nb + [x[0] for x in fw.dsems_sw]:
        if s.count:
            e.wait((s, s.count, "dma"))
    fw.close()
    return nc


_NC_CACHE = {}


def _prep_shared(w_mod, b_mod, pre_norm_gain, post_norm_gain, w_in, attn_sinks, gla_gate_w, gla_gate_b,
                 gla_norm_gain, w_out):
    L = w_in.shape[0]
    f = lambda a: np.ascontiguousarray(np.asarray(a, dtype=np.float32))
    b_mod = np.asarray(b_mod, np.float32)
    gwe = np.zeros((L, 128, 128), np.float32)
    gwe[:, 0:16, :] = np.asarray(gla_gate_w, np.float32)
    gwe[:, 16, :] = np.asarray(gla_gate_b, np.float32)
    return {
        "w_mod": f(w_mod),
        "bmodF": f(b_mod.reshape(L, 24, 128).transpose(0, 2, 1)),
        "bgate": f(b_mod[:, 2048:3072]),
        "pregF": f(np.asarray(pre_norm_gain, np.float32).reshape(L, 8, 128).transpose(0, 2, 1)),
        "postg": f(post_norm_gain),
        "w_in": f(np.asarray(w_in, np.float32)[:, :, _PERM]),
        "w_qk32": f(np.asarray(w_in, np.float32)[:, :, _QK32]),
        "sinks": f(attn_sinks),
        "gwe": gwe,
        "ggain": f(gla_norm_gain),
        "w_out": f(w_out),
        "consts": _CBLOB,
        "cmask": _CMASK,
    }


def kernel(x, c, positions, w_mod, b_mod, pre_norm_gain, post_norm_gain, w_in,
           attn_sinks, gla_gate_w, gla_gate_b, gla_norm_gain, w_out):
    x = np.asarray(x, np.float32)
    c = np.asarray(c, np.float32)
    positions = np.asarray(positions, np.int32)
    B = x.shape[0]
    shared = _prep_shared(w_mod, b_mod, pre_norm_gain, post_norm_gain, w_in, attn_sinks, gla_gate_w,
                          gla_gate_b, gla_norm_gain, w_out)
    if "nc" not in _NC_CACHE:
        _NC_CACHE["nc"] = build(n_layers=2)
    nc = _NC_CACHE["nc"]
    in_maps = []
    for b in range(B):
        m = dict(shared)
        m["x"] = np.ascontiguousarray(x[b])
        m["c"] = np.ascontiguousarray(c[b].reshape(8, 128).T)
        m["pos"] = np.ascontiguousarray(positions[b].reshape(NT, 128).T)
        in_maps.append(m)
    res = run_bass_kernel_spmd(nc, in_maps, core_ids=list(range(B)))
    return np.stack([np.asarray(r["out"], np.float32) for r in res.results], axis=0)
```

```python
import os
import numpy as np
import concourse.bass as bass
import concourse.mybir as mybir
from concourse.bass_utils import run_bass_kernel_spmd

F32 = mybir.dt.float32
BF16 = mybir.dt.bfloat16
I32 = mybir.dt.int32
AF = mybir.ActivationFunctionType
ALU = mybir.AluOpType
AX = mybir.AxisListType

S = 2048
D = 1024
NT = 16
DIN = 3088
EPS = 1e-6
NEG = -30000.0


class Sem:
    def __init__(self, h, name):
        self.h = h
        self.count = 0
        self.name = name


class Res:
    def __init__(self, name, excl=False):
        self.name = name
        self.excl = excl
        self.w = None
        self.r = {}


class Eng:
    def __init__(self, name, h, sem):
        self.name = name
        self.h = h
        self.sem = sem
        self.seen = {}

    def wait(self, tok):
        if tok is None:
            return
        if tok[0] == "PENDING":
            if self.name == "pe":
                return
            raise RuntimeError("wait on pending PE token by " + self.name)
        s, v, _ = tok
        if self.seen.get(id(s), 0) >= v:
            return
        self.h.wait_ge(s.h, v)
        self.seen[id(s)] = v


class _Probe:
    def __init__(self):
        self.n = 0
        self.opname = ""
        self.fp32 = False
        self.accum = False

    def __getattr__(self, name):
        def f(*a, **k):
            out = k.get("out", a[0] if a else None)
            n = 1
            for d in out.shape[1:]:
                n *= d
            self.n = n
            self.opname = name
            self.call = (name, a, k)
            lt = k.get("lhsT", None)
            self.fp32 = lt is not None and lt.dtype == F32
            self.accum = k.get("accum_out", None) is not None
            return self
        return f

    def then_inc(self, *a, **k):
        return self

    def replay(self):
        name, a, k = self.call
        return lambda e: getattr(e, name)(*a, **k)


class Unit:
    __slots__ = ("kind", "eng", "fns", "reads", "writes", "dur", "busy", "idx", "args", "deps", "nsucc")


def _est(ename, pr):
    n = pr.n
    if ename == "pe":
        if pr.opname == "transpose":
            return 108.0
        return (max(64, n) / 2.4 + 6.0) * (4.0 if pr.fp32 else 1.0)
    if ename == "act":
        return 190.0 + n / 1.2 + (90.0 if pr.accum else 0.0)
    if ename == "dve":
        if pr.opname == "reciprocal":
            return 80.0 + 8.0 * n
        return 70.0 + n * 1.05
    if ename == "pool":
        if pr.opname == "tensor_tensor" and n <= 16:
            return 750.0
        return 150.0 + n * 2.3
    return 100.0


class FW:
    def __init__(self, nc, ndma_sems=16):
        self.rec = None
        self.cur_pe = None
        self.nc = nc
        self._ctx = []
        self.engs = {}
        for name, h in (("pe", nc.tensor), ("dve", nc.vector), ("act", nc.scalar),
                        ("pool", nc.gpsimd), ("sp", nc.sync)):
            self.engs[name] = Eng(name, h, self._sem("s_" + name))
        self.dsems = [self._sem("d%d" % i) for i in range(ndma_sems)]
        self.dnext = 0
        self.dsems_nb = [self._sem("n%d" % i) for i in range(8)]
        self.dnext_nb = 0
        self.dsems_sw = []
        self.pe_pending = []

    def _sem(self, name):
        cm = self.nc.semaphore(name)
        h = cm.__enter__()
        self._ctx.append(cm)
        return Sem(h, name)

    def barrier(self):
        was = self.rec is not None
        self.flush()
        self._barrier()
        if was:
            self.start_recording()

    def _barrier(self):
        assert not self.pe_pending
        toks = [(e.sem, e.sem.count, e.name) for e in self.engs.values() if e.sem.count]
        toks += [(s, s.count, "dma") for s in self.dsems if s.count]
        toks += [(s, s.count, "dma") for s, nb in self.dsems_sw if s.count and not nb]
        for e in self.engs.values():
            for t in toks:
                if t[2] != e.name:
                    e.wait(t)

    def sb(self, name, shape, dt):
        n = 1
        for d in shape[1:]:
            n *= d
        self.nbytes = getattr(self, "nbytes", 0) + n * (2 if dt == BF16 else 4)
        cm = self.nc.sbuf_tensor("sb_" + name, list(shape), dt)
        t = cm.__enter__()
        self._ctx.append(cm)
        return t

    def ps(self, name, shape, dt):
        cm = self.nc.psum_tensor(name, list(shape), dt)
        t = cm.__enter__()
        self._ctx.append(cm)
        return t

    def close(self):
        for cm in reversed(self._ctx):
            cm.__exit__(None, None, None)
        self._ctx = []

    def _acq(self, e, reads, writes):
        for r in reads:
            e.wait(r.w)
            if r.excl:
                for en, t in r.r.items():
                    if en != e.name:
                        e.wait(t)
        for w in writes:
            e.wait(w.w)
            for en, t in w.r.items():
                e.wait(t)

    def _rel(self, ename, tok, reads, writes):
        for r in reads:
            r.r[ename] = tok
        for w in writes:
            w.w = tok
            w.r = {}

    def start_recording(self):
        self.rec = []
        self.cur_pe = None

    def flush(self):
        if self.rec is None:
            return
        assert self.cur_pe is None
        units = self.rec
        self.rec = None
        n = len(units)
        lastw = {}
        readers = {}
        succ = [[] for _ in range(n)]
        for i, u in enumerate(units):
            u.idx = i
            deps = set()
            for r in u.reads:
                k = id(r)
                if r.excl:
                    if k in lastw:
                        deps.add(lastw[k])
                    deps.update(readers.get(k, ()))
                elif k in lastw:
                    deps.add(lastw[k])
            for w in u.writes:
                k = id(w)
                if k in lastw:
                    deps.add(lastw[k])
                deps.update(readers.get(k, ()))
            deps.discard(i)
            u.deps = deps
            for d in deps:
                succ[d].append(i)
            for r in u.reads:
                k = id(r)
                if r.excl:
                    lastw[k] = i
                    readers[k] = []
                else:
                    readers.setdefault(k, []).append(i)
            for w in u.writes:
                k = id(w)
                lastw[k] = i
                readers[k] = []
        LAT = float(os.environ.get("SCHED_LAT", "250"))
        blevel = [0.0] * n
        for i in range(n - 1, -1, -1):
            u = units[i]
            m = 0.0
            for j in succ[i]:
                v = blevel[j] + (LAT if units[j].eng != u.eng else 40.0)
                if v > m:
                    m = v
            blevel[i] = u.dur + m
        PEB = float(os.environ.get("SCHED_PEB", "0"))
        if PEB:
            for i in range(n):
                if units[i].eng == "pe":
                    blevel[i] += PEB
        ndep = [len(u.deps) for u in units]
        finish = [0.0] * n
        efree = {}
        ready = [i for i in range(n) if ndep[i] == 0]
        order = []
        SLACK = float(os.environ.get("SCHED_SLACK", "120"))
        while ready:
            ests = []
            mn = None
            for i in ready:
                u = units[i]
                st = efree.get(u.eng, 0.0)
                for d in u.deps:
                    f = finish[d] + (LAT if units[d].eng != u.eng else 40.0)
                    if f > st:
                        st = f
                ests.append(st)
                if mn is None or st < mn:
                    mn = st
            best = None
            bs = None
            bl = -1.0
            for i, st in zip(ready, ests):
                if st <= mn + SLACK and (blevel[i] > bl + 1e-9 or (abs(blevel[i] - bl) <= 1e-9 and i < best)):
                    bl = blevel[i]
                    best = i
                    bs = st
            ready.remove(best)
            u = units[best]
            if getattr(self, "diag", None) is not None and bs > efree.get(u.eng, 0.0) + 1.0:
                bd = max(u.deps, key=lambda d: finish[d] + (LAT if units[d].eng != u.eng else 40.0)) if u.deps else None
                if bd is not None:
                    ud = units[bd]
                    shared = [r.name for r in list(u.reads) + list(u.writes) if r in ud.reads or r in ud.writes]
                    key = (u.eng, shared[0] if shared else "?", ud.eng)
                    self.diag[key] = self.diag.get(key, 0.0) + bs - efree.get(u.eng, 0.0)
            efree[u.eng] = bs + u.busy
            finish[best] = bs + u.dur
            order.append(best)
            for j in succ[best]:
                ndep[j] -= 1
                if ndep[j] == 0:
                    ready.append(j)
        assert len(order) == n
        self.sched_span = getattr(self, "sched_span", 0.0) + max(finish) if n else 0.0
        for i in order:
            u = units[i]
            if u.kind == "op":
                self.op(u.eng, u.fns[0], u.reads, u.writes)
            elif u.kind == "mm":
                for j, (fn, rd, wr) in enumerate(u.fns):
                    self.mm(fn, rd, wr, last=(j == len(u.fns) - 1))
            else:
                self.dma(u.eng, u.args[0], u.args[1], u.reads, u.writes, nobar=u.args[2])

    def _record(self, kind, eng, fns, reads, writes, dur, busy, args=None):
        u = Unit()
        u.kind = kind
        u.eng = eng
        u.fns = fns
        u.reads = tuple(reads)
        u.writes = tuple(writes)
        u.dur = dur
        u.busy = busy
        u.args = args
        self.rec.append(u)

    def op(self, ename, fn, reads=(), writes=()):
        if self.rec is not None:
            pr = _Probe()
            fn(pr)
            d = _est(ename, pr)
            self._record("op", ename, [pr.replay()], reads, writes, d, d)
            return None
        e = self.engs[ename]
        self._acq(e, reads, writes)
        inst = fn(e.h)
        e.sem.count += 1
        inst.then_inc(e.sem.h, 1)
        self._rel(ename, (e.sem, e.sem.count, ename), reads, writes)
        return inst

    def mm(self, fn, reads=(), writes=(), last=False):
        if self.rec is not None:
            pr = _Probe()
            fn(pr)
            d = _est("pe", pr)
            if self.cur_pe is None:
                self.cur_pe = [[], [], [], 0.0]
            g = self.cur_pe
            g[0].append((pr.replay(), tuple(reads), tuple(writes)))
            for r in reads:
                if r not in g[1]:
                    g[1].append(r)
            for w in writes:
                if w not in g[2]:
                    g[2].append(w)
            g[3] += d
            if last:
                self.cur_pe = None
                self._record("mm", "pe", g[0], g[1], g[2], g[3] + 60.0, g[3])
            return None
        e = self.engs["pe"]
        self._acq(e, reads, writes)
        inst = fn(e.h)
        self.pe_pending.append((tuple(reads), tuple(writes)))
        if last:
            e.sem.count += 1
            inst.then_inc(e.sem.h, 1)
            tok = (e.sem, e.sem.count, "pe")
            for rd, wr in self.pe_pending:
                self._rel("pe", tok, rd, wr)
            self.pe_pending = []
        else:
            for w in writes:
                w.w = ("PENDING",)
                w.r = {}
            for r in reads:
                r.r["pe"] = ("PENDING",)
        return inst

    def dma(self, qname, out, in_, reads=(), writes=(), nobar=False):
        if self.rec is not None:
            n = 1
            for d_ in out.shape:
                n *= d_
            self._record("dma", qname, None, reads, writes, 2500.0 + n * 4 / 150.0, 120.0, (out, in_, nobar))
            return None
        e = self.engs[qname]
        self._acq(e, reads, writes)
        if qname == "pool":
            s = self._sem("w%d" % len(self.dsems_sw))
            self.dsems_sw.append((s, nobar))
        elif nobar:
            s = self.dsems_nb[self.dnext_nb]
            self.dnext_nb = (self.dnext_nb + 1) % len(self.dsems_nb)
        else:
            s = self.dsems[self.dnext]
            self.dnext = (self.dnext + 1) % len(self.dsems)
        if s.count:
            e.wait((s, s.count, "dma"))
        inst = e.h.dma_start(out=out, in_=in_)
        s.count += 16
        inst.then_inc(s.h, 16)
        tok = (s, s.count, "dma:" + s.name)
        self._rel("dma:" + s.name, tok, reads, writes)
        return tok


class T:
    def __init__(self, fw, name, shape, dt, view=None):
        self.t = fw.sb(name, shape, dt) if view is None else view
        self.r = Res(name)


class Arena:
    def __init__(self, t, nwords):
        self.t = t
        self.n = nwords
        self.o = 0

    def reset(self):
        self.o = 0

    def get(self, name, shape, dt):
        n = 1
        for d in shape[1:]:
            n *= d
        words = n if dt != BF16 else (n + 1) // 2
        assert self.o + words <= self.n, (name, self.o, words, self.n)
        v = self.t[:, self.o:self.o + words]
        self.o += words
        if dt != F32:
            v = v.bitcast(dt)
        if len(shape) == 3:
            v = v.rearrange("p (a b) -> p a b", b=shape[2])
        elif len(shape) == 4:
            v = v.rearrange("p (a b c) -> p a b c", b=shape[2], c=shape[3])
        return T(None, name, shape, dt, view=v)

    def __getitem__(self, k):
        return self.t[k]


def _consts():
    p = np.arange(128)
    cols = {}
    ident = np.eye(128, dtype=np.float32)
    cols["ident"] = ident
    k = p[:, None]
    q = p[None, :]
    mcur = np.where(k <= q, 0.0, NEG).astype(np.float32)
    mprev = np.where(k > q, 0.0, NEG).astype(np.float32)
    causal = (k <= q).astype(np.float32)
    global _CMASK
    _CMASK = np.ascontiguousarray(np.concatenate([mcur, mprev, causal], axis=1))
    cols["tri_in"] = causal * (-1.0 / 16.0)
    cols["tri_rev"] = (k > q).astype(np.float32) * (-1.0 / 16.0)
    cols["ncol"] = np.full((128, 2), -1.0 / 16.0, np.float32)
    h = np.arange(4, dtype=np.float32)
    log_g = np.log(1.0 - 2.0 ** (-5.0 - h)).astype(np.float32)
    i1 = (p[:, None] + 1).astype(np.float32)
    qdec = np.exp(log_g[None, :] * i1)
    kdec = np.exp(-log_g[None, :] * i1) / 8.0
    cols["dec8"] = np.concatenate([qdec, kdec], axis=1).astype(np.float32)
    cdec = np.exp(log_g * 128.0)
    sdec = np.zeros((128, 2), np.float32)
    for c in range(2):
        sdec[0:64, c] = cdec[2 * c]
        sdec[64:128, c] = cdec[2 * c + 1]
    cols["sdec"] = sdec
    bd64 = np.zeros((128, 128), np.float32)
    bd64[0:64, 0:64] = 1.0
    bd64[64:128, 64:128] = 1.0
    cols["bd64dec"] = np.concatenate([bd64 * sdec[:, 0:1], bd64 * sdec[:, 1:2]], axis=1)
    bd32 = np.zeros((128, 256), np.float32)
    for hh in range(4):
        bd32[32 * hh:32 * hh + 32, 64 * hh:64 * hh + 64] = 1.0
    cols["bd32"] = bd32
    gm = np.zeros((128, 2), np.float32)
    gm[0:16, 0] = 1.0
    gm[16, 1] = 1.0
    cols["gamask"] = gm
    fa = (10000.0 ** (-np.arange(0, 64, 2, dtype=np.float32) / 64.0)).astype(np.float32)
    fr = (1.0 / (10000.0 ** np.linspace(0.0, 1.0, 32, dtype=np.float32))).astype(np.float32)
    cols["freq"] = np.tile(np.concatenate([fa, fr])[None, :], (128, 1)).astype(np.float32)
    cols["neghalf"] = np.full((128, 8), -0.5, np.float32)
    cols["neghalf16"] = np.full((128, 16), -0.5, np.float32)
    off = {}
    o = 0
    parts = []
    for kname, v in cols.items():
        off[kname] = (o, v.shape[1])
        o += v.shape[1]
        parts.append(v.astype(np.float32))
    return np.ascontiguousarray(np.concatenate(parts, axis=1)), off


_CBLOB, _COFF = _consts()
NCONST = _CBLOB.shape[1]

_AQ = np.concatenate([np.arange(64 * hh, 64 * hh + 64) for hh in (0, 4, 1, 5, 2, 6, 3, 7)])
_r = lambda a, b: np.arange(a, b)
_PERM = np.concatenate([
    _r(3072, 3088),
    _AQ,
    _r(512, 640), _r(640, 768), _r(1280, 1536),
    _r(1536, 1792), _r(1792, 2048),
    _r(768, 1280),
    _r(2048, 2304), _r(2816, 3072),
    _r(2304, 2432), _r(2432, 2560), _r(2560, 2816),
])
NW = _PERM.shape[0]
_QK32 = np.concatenate([_r(1280, 1536), _r(2304, 2432), _r(1536, 1792), _r(2432, 2560)])
GOFF = 0
COFFS = [16 + 512 * i for i in range(6)]


def build(n_layers=2, n_tiles=NT, dbg=None, sched=True):
    nc = bass.Bass("TRN2", target_bir_lowering=False)
    fw = FW(nc)
    if sched:
        fw.start_recording()
    L = n_layers

    def din(name, shape, dt=F32):
        return nc.dram_tensor(name, list(shape), dt, kind="ExternalInput").ap()

    x_d = din("x", [S, D])
    c_d = din("c", [128, 8])
    pos_d = din("pos", [128, NT], I32)
    wmod_d = din("w_mod", [L, D, 3 * D])
    bmodF_d = din("bmodF", [L, 128, 24])
    bgate_d = din("bgate", [L, D])
    pregF_d = din("pregF", [L, 128, 8])
    postg_d = din("postg", [L, D])
    win_d = din("w_in", [L, D, NW])
    sink_d = din("sinks", [L, 8])
    gwe_d = din("gwe", [L, 128, 128])
    ggain_d = din("ggain", [L, 64])
    wout_d = din("w_out", [L, D, D])
    wqk_d = din("w_qk32", [L, D, 768])
    const_d = din("consts", [128, NCONST])
    cmask_d = din("cmask", [128, 384])
    out_d = nc.dram_tensor("out", [S, D], F32, kind="ExternalOutput").ap()

    xs = fw.sb("xs", [128, NT, D], F32)
    R_x = [Res("x%d" % t) for t in range(NT)]
    win = fw.sb("win", [128, 8, NW], BF16)
    R_winc = [[Res("win%d_%d" % (kc, hf)) for hf in range(2)] for kc in range(8)]
    wout = fw.sb("wout", [128, 8, D], BF16)
    R_woutc = [Res("wout%d" % kc) for kc in range(8)]
    cst = T(fw, "cst", [128, NCONST], F32)

    def C(name):
        o, n = _COFF[name]
        return cst.t[:, o:o + n]

    identb = T(fw, "identb", [128, 128], BF16)
    mcurb = T(fw, "mcurb", [128, 512], BF16)
    mprevb = T(fw, "mprevb", [128, 512], BF16)
    gpg = T(fw, "gpg", [128, D], F32)
    gmF = T(fw, "gmF", [128, 8], F32)
    shF = T(fw, "shF", [128, 8], F32)
    esink2 = T(fw, "esink2", [128, 8], F32)
    gwe = T(fw, "gwe", [128, 128], F32)
    ggain = T(fw, "ggain", [128, 64], F32)
    tabs = T(fw, "tabs", [128, 4, NT, 32], F32)
    small = T(fw, "small", [128, 64], F32)
    retS = T(fw, "retS", [128, 256], F32)
    retSb = T(fw, "retSb", [128, 256], BF16)
    glaS = T(fw, "glaS", [128, 256], F32)
    glaSb = T(fw, "glaSb", [128, 256], BF16)
    big32_ = T(fw, "big32", [128, 512], F32)
    big32 = [big32_, big32_]
    xn = T(fw, "xn", [128, D], BF16)
    hT = T(fw, "hT", [128, 8, 128], BF16)
    gaTe = T(fw, "gaTe", [128, 128], F32)
    nl = T(fw, "nl", [128, 128], F32)
    Eq = T(fw, "Eq", [128, 128], F32)
    Ek = T(fw, "Ek", [128, 128], F32)
    Er = T(fw, "Er", [128, 128], F32)
    aqk = T(fw, "aqk", [128, 640], BF16)
    rqk32 = T(fw, "rqk32", [128, 512], F32)
    gqk = T(fw, "gqk", [128, 256], BF16)
    scT = T(fw, "scT", [128, 512], BF16)
    stat = T(fw, "stat", [128, 32], F32)
    statA = T(fw, "statA", [128, 8], F32)
    h0F = T(fw, "h0F", [128, 16], F32)
    e0r = T(fw, "e0r", [128, 2], F32)
    s00 = T(fw, "s00", [128, 8], F32)
    ssq = [T(fw, "ssq%d" % i, [128, NT], F32) for i in range(n_layers)]
    rstdT = [T(fw, "rstd%d" % i, [128, NT], F32) for i in range(n_layers)]
    stat1 = T(fw, "stat1", [128, 8], F32)
    stat2 = T(fw, "stat2", [128, 8], F32)
    stat3 = T(fw, "stat3", [128, 8], F32)
    causalb = T(fw, "causalb", [128, 512], BF16)
    kT = [T(fw, "kT%d" % i, [128, 128], BF16) for i in range(3)]
    vext = [T(fw, "vext%d" % i, [128, 2, 65], BF16) for i in range(3)]
    ebl = [T(fw, "ebl%d" % i, [128, 2], F32) for i in range(2)]
    rqk = [T(fw, "rqk%d" % i, [128, 512], BF16) for i in range(2)]
    gkh = [T(fw, "gkh%d" % i, [128, 128], BF16) for i in range(2)]
    qbd = [[T(fw, "qbd%d%d" % (i, g), [128, 4, 128], BF16) for g in range(2)] for i in range(2)]
    rqbd = [[T(fw, "rqbd%d%d" % (i, c), [128, 2, 128], BF16) for c in range(2)] for i in range(2)]
    rqT = [T(fw, "rqT%d" % i, [128, 2, 128], BF16) for i in range(2)]
    rkT = [T(fw, "rkT%d" % i, [128, 2, 128], BF16) for i in range(2)]
    gqbd = [T(fw, "gqbd%d" % i, [128, 4, 128], BF16) for i in range(2)]
    gqT = [T(fw, "gqT%d" % i, [128, 128], BF16) for i in range(2)]
    gkT = [T(fw, "gkT%d" % i, [128, 128], BF16) for i in range(2)]
    rv = [T(fw, "rv%d" % i, [128, 256], BF16) for i in range(2)]
    gv = [T(fw, "gv%d" % i, [128, 256], BF16) for i in range(2)]
    sg1 = T(fw, "sg1", [128, D], BF16)
    rotA2 = T(fw, "rotA2", [128, 256], F32)
    rotB2 = T(fw, "rotB2", [128, 256], F32)
    AW = 5120
    arena_t = fw.sb("arena", [128, AW], F32)
    arA = Arena(arena_t, AW)
    wst = [arA.get("wst%d" % i, [128, D], F32) for i in range(2)]
    cb = arA.get("cb", [128, 8, 128], F32)
    ang = arA.get("ang", [128, NT, 32], F32)
    tu = arA.get("tu", [128, NT, 32], F32)
    tki = arA.get("tki", [128, NT, 32], I32)
    ty = arA.get("ty", [128, NT, 32], F32)
    arA2 = Arena(arena_t, AW)
    arA2.o = 3072
    wst_extra = [arA2.get("wst%d" % i, [128, D], F32) for i in (2, 3)]
    cmk = T(None, "cmk", [128, 384], F32, view=rqk32.t[:, 0:384])
    cmk.r = rqk32.r
    arB = Arena(arena_t, AW)
    rotA1 = arB.get("rotA", [128, 640], F32)
    rotB1 = arB.get("rotB", [128, 640], F32)
    th = arB.get("th", [128, 512], BF16)
    sg0 = arB.get("sg", [128, D], BF16)
    sg = [sg0, sg1]
    PT = [[arB.get("PT%d%d" % (g, b), [128, 512], BF16) for b in range(2)] for g in range(2)]
    an = arB.get("an", [128, 256], F32)
    rraw = arB.get("rraw", [128, 256], F32)
    rsq = arB.get("rsq", [128, 256], F32)
    mix = arB.get("mix", [128, D], BF16)
    mixT = arB.get("mixT", [128, 8, 128], BF16)
    dS = arB.get("dS", [128, 256], F32)

    pp = [fw.ps("pp%d" % i, [128, 512], F32) for i in range(2)]
    R_pp = [Res("pp%d" % i, excl=True) for i in range(2)]
    ptA = fw.ps("ptA", [128, 1024], BF16)
    R_tA = Res("ptA", excl=True)
    ptB = fw.ps("ptB", [128, 1024], BF16)
    R_tB = Res("ptB", excl=True)
    psc = [fw.ps("psc%d" % i, [128, 512], F32) for i in range(2)]
    R_sc = [Res("psc%d" % i, excl=True) for i in range(2)]
    po = [fw.ps("po%d" % i, [128, 512], F32) for i in range(2)]
    R_o = [Res("po%d" % i, excl=True) for i in range(2)]

    dve = lambda fn, reads=(), writes=(): fw.op("dve", fn, reads, writes)
    act = lambda fn, reads=(), writes=(): fw.op("act", fn, reads, writes)
    pool = lambda fn, reads=(), writes=(): fw.op("pool", fn, reads, writes)
    dbg_outs = {}

    def dump(name, ap, R, n):
        if dbg is None:
            return
        o = nc.dram_tensor("dbg_" + name, [128, n], F32, kind="ExternalOutput").ap()
        for a0 in range(0, n, 512):
            a1 = min(n, a0 + 512)
            dve(lambda e: e.tensor_copy(out=big32_.t[:, 0:a1 - a0], in_=ap[:, a0:a1]), [R], [big32_.r])
            fw.dma("sp", o[:, a0:a1], big32_.t[:, 0:a1 - a0], reads=[big32_.r])

    def load_weights(l, early=False):
        for kc in range(8):
            for hf in range(2):
                c0 = hf * (NW // 2)
                fw.dma("pool", win[:, kc, c0:c0 + NW // 2],
                       win_d[l, kc * 128:(kc + 1) * 128, c0:c0 + NW // 2], writes=[R_winc[kc][hf]], nobar=early)
        for kc in range(8):
            fw.dma("pool", wout[:, kc, :], wout_d[l, kc * 128:(kc + 1) * 128, :], writes=[R_woutc[kc]], nobar=True)

    fw.dma("sp", cst.t[:], const_d, writes=[cst.r])
    fw.dma("sp", xs[:, 0, :], x_d[0:128, :], writes=[R_x[0]])
    load_weights(0)
    if n_tiles > 1:
        fw.dma("sp", xs[:, 1, :], x_d[128:256, :], writes=[R_x[1]])
    dve(lambda e: e.tensor_copy(out=identb.t[:], in_=C("ident")), [cst.r], [identb.r])
    fw.dma("sp", cmk.t[:], cmask_d, writes=[cmk.r])
    for i_, dst_ in enumerate((mcurb, mprevb, causalb)):
        dve(lambda e: e.tensor_copy(out=dst_.t[:].rearrange("p (r q) -> p r q", q=128),
                                    in_=cmk.t[:, 128 * i_:128 * (i_ + 1)].unsqueeze(1).broadcast_to([128, 4, 128])),
            [cmk.r], [dst_.r])
    for i_ in range(2):
        for g in range(2):
            pool(lambda e: e.memset(qbd[i_][g].t[:], 0.0), [], [qbd[i_][g].r])
            pool(lambda e: e.memset(rqbd[i_][g].t[:], 0.0), [], [rqbd[i_][g].r])
        pool(lambda e: e.memset(gqbd[i_].t[:], 0.0), [], [gqbd[i_].r])
    for i_ in range(3):
        pool(lambda e: e.memset(vext[i_].t[:], 1.0), [], [vext[i_].r])

    posi = T(fw, "posi", [128, NT], I32)
    posf = T(fw, "posf", [128, NT], F32)
    fw.dma("sp", posi.t[:], pos_d, writes=[posi.r])
    dve(lambda e: e.tensor_copy(out=posf.t[:], in_=posi.t[:]), [posi.r], [posf.r])
    TWO_PI = float(2 * np.pi)
    PI = float(np.pi)
    C1_2PI = float(np.float32(2 * np.pi))
    C2_2PI = float(np.float32(2 * np.pi - C1_2PI))

    def wrap(dst, src, shift):
        dve(lambda e: e.tensor_scalar(out=dst.t[:], in0=src.t[:], scalar1=float(shift), scalar2=None,
                                      op0=ALU.add), [src.r], [dst.r])
        dve(lambda e: e.tensor_scalar(out=tu.t[:], in0=dst.t[:], scalar1=PI, scalar2=-TWO_PI,
                                      op0=ALU.is_gt, op1=ALU.mult), [dst.r], [tu.r])
        dve(lambda e: e.tensor_tensor(out=dst.t[:], in0=dst.t[:], in1=tu.t[:], op=ALU.add),
            [dst.r, tu.r], [dst.r])
        dve(lambda e: e.tensor_scalar(out=tu.t[:], in0=dst.t[:], scalar1=-PI, scalar2=TWO_PI,
                                      op0=ALU.is_lt, op1=ALU.mult), [dst.r], [tu.r])
        dve(lambda e: e.tensor_tensor(out=dst.t[:], in0=dst.t[:], in1=tu.t[:], op=ALU.add),
            [dst.r, tu.r], [dst.r])

    fo, _ = _COFF["freq"]
    for which in range(2):
        fr_ap = cst.t[:, fo + 32 * which: fo + 32 * which + 32].unsqueeze(1).broadcast_to([128, NT, 32])
        pos_ap = posf.t[:].unsqueeze(2).broadcast_to([128, NT, 32])
        dve(lambda e: e.tensor_tensor(out=ang.t[:], in0=pos_ap, in1=fr_ap, op=ALU.mult),
            [posf.r, cst.r], [ang.r])
        dve(lambda e: e.tensor_scalar(out=tu.t[:], in0=ang.t[:], scalar1=float(1.0 / TWO_PI), scalar2=None,
                                      op0=ALU.mult), [ang.r], [tu.r])
        dve(lambda e: e.tensor_copy(out=tki.t[:], in_=tu.t[:]), [tu.r], [tki.r])
        dve(lambda e: e.tensor_copy(out=tu.t[:], in_=tki.t[:]), [tki.r], [tu.r])
        dve(lambda e: e.scalar_tensor_tensor(out=ty.t[:], in0=tu.t[:], scalar=-C1_2PI, in1=ang.t[:],
                                             op0=ALU.mult, op1=ALU.add), [tu.r, ang.r], [ty.r])
        dve(lambda e: e.scalar_tensor_tensor(out=ty.t[:], in0=tu.t[:], scalar=-C2_2PI, in1=ty.t[:],
                                             op0=ALU.mult, op1=ALU.add), [tu.r, ty.r], [ty.r])
        wrap(ang, ty, 0.0)
        act(lambda e: e.activation(out=tabs.t[:, 2 * which + 1, :, :], in_=ang.t[:], func=AF.Sin),
            [ang.r], [tabs.r])
        wrap(ty, ang, PI / 2)
        act(lambda e: e.activation(out=tabs.t[:, 2 * which, :, :], in_=ty.t[:], func=AF.Sin),
            [ty.r], [tabs.r])

    c32 = T(fw, "c32", [128, 8], F32)
    cth = T(fw, "cth", [128, 8], F32)
    fw.dma("sp", c32.t[:], c_d, writes=[c32.r])
    act(lambda e: e.activation(out=cth.t[:], in_=c32.t[:], func=AF.Tanh, scale=0.5), [c32.r], [cth.r])
    dve(lambda e: e.scalar_tensor_tensor(out=cth.t[:], in0=cth.t[:], scalar=1.0, in1=c32.t[:],
                                         op0=ALU.add, op1=ALU.mult), [cth.r, c32.r], [cth.r])
    dve(lambda e: e.tensor_scalar(out=cth.t[:], in0=cth.t[:], scalar1=0.5, scalar2=None, op0=ALU.mult),
        [cth.r], [cth.r])

    out_toks = []

    for l in range(L):
        bmodF = T(fw, "bmodF%d" % l, [128, 24], F32)
        pregF = T(fw, "pregF%d" % l, [128, 8], F32)
        fw.dma("sp", bmodF.t[:], bmodF_d[l], writes=[bmodF.r])
        fw.dma("sp", pregF.t[:], pregF_d[l], writes=[pregF.r])
        fw.dma("sp", gwe.t[:], gwe_d[l], writes=[gwe.r])
        fw.dma("sp", ggain.t[:], ggain_d[l].partition_broadcast(128), writes=[ggain.r])
        fw.dma("sp", esink2.t[:], sink_d[l].partition_broadcast(128), writes=[esink2.r])
        act(lambda e: e.activation(out=esink2.t[:], in_=esink2.t[:], func=AF.Exp), [esink2.r], [esink2.r])
        dve(lambda e: e.tensor_scalar(out=esink2.t[:], in0=esink2.t[:], scalar1=2.0, scalar2=None,
                                      op0=ALU.mult), [esink2.r], [esink2.r])
        dve(lambda e: e.tensor_copy(out=cb.t[:], in_=cth.t[:].unsqueeze(2).broadcast_to([128, 8, 128])),
            [cth.r], [cb.r])
        banks = [(pp[0], R_pp[0]), (pp[1], R_pp[1]), (psc[0], R_sc[0]), (psc[1], R_sc[1]),
                 (po[0], R_o[0]), (po[1], R_o[1])]
        i = 0
        for kc in range(8):
            for third in range(3):
                stl = wst if l == 0 else wst + wst_extra
                st = stl[i % len(stl)]
                i += 1
                fw.dma("sp", st.t[:], wmod_d[l, kc * 128:(kc + 1) * 128, third * 1024:(third + 1) * 1024],
                       writes=[st.r])
                for hf in range(2):
                    bk, rb = banks[third * 2 + hf]
                    fw.mm(lambda e: e.matmul(bk[:, :], lhsT=cb.t[:, kc, :], rhs=st.t[:, hf * 512:(hf + 1) * 512],
                                             start=(kc == 0), stop=(kc == 7)),
                          reads=[cb.r, st.r], writes=[rb], last=True)
        for which, dst in ((0, shF), (1, small)):
            for hf in range(2):
                bk, rb = banks[which * 2 + hf]
                dve(lambda e: e.tensor_tensor(
                    out=big32_.t[:].rearrange("p (k n) -> p k n", n=128),
                    in0=bk[:, :].rearrange("p (k n) -> p k n", n=128),
                    in1=C("ident").unsqueeze(1).broadcast_to([128, 4, 128]), op=ALU.mult),
                    [rb, cst.r], [big32_.r])
                dve(lambda e: e.tensor_reduce(out=dst.t[:, 4 * hf:4 * hf + 4],
                                              in_=big32_.t[:].rearrange("p (k n) -> p k n", n=128),
                                              op=ALU.add, axis=AX.X), [big32_.r], [dst.r])
        dve(lambda e: e.tensor_tensor(out=shF.t[:], in0=shF.t[:], in1=bmodF.t[:, 0:8], op=ALU.add),
            [shF.r, bmodF.r], [shF.r])
        dve(lambda e: e.tensor_tensor(out=small.t[:, 0:8], in0=small.t[:, 0:8], in1=bmodF.t[:, 8:16], op=ALU.add),
            [small.r, bmodF.r], [small.r])
        dve(lambda e: e.scalar_tensor_tensor(out=gmF.t[:], in0=small.t[:, 0:8], scalar=1.0, in1=pregF.t[:],
                                             op0=ALU.add, op1=ALU.mult), [small.r, pregF.r], [gmF.r])
        fw.dma("sp", gpg.t[:], bgate_d[l].partition_broadcast(128), writes=[gpg.r])
        for hf in range(2):
            bk, rb = banks[4 + hf]
            dve(lambda e: e.tensor_tensor(out=gpg.t[:, hf * 512:(hf + 1) * 512], in0=bk[:, :],
                                          in1=gpg.t[:, hf * 512:(hf + 1) * 512], op=ALU.add),
                [rb, gpg.r], [gpg.r])
        for hf in range(2):
            fw.dma("sp", rqk32.t[:], postg_d[l, hf * 512:(hf + 1) * 512].partition_broadcast(128), writes=[rqk32.r])
            dve(lambda e: e.tensor_tensor(out=gpg.t[:, hf * 512:(hf + 1) * 512], in0=gpg.t[:, hf * 512:(hf + 1) * 512],
                                          in1=rqk32.t[:], op=ALU.mult), [gpg.r, rqk32.r], [gpg.r])
        nst = min(2, n_tiles) if l == 0 else n_tiles
        if l == 0:
            for t_ in range(nst):
                act(lambda e: e.activation(out=xn.t[:], in_=xs[:, t_, :], func=AF.Square,
                                           accum_out=ssq[0].t[:, t_:t_ + 1]), [R_x[t_]], [xn.r, ssq[0].r])
        dve(lambda e: e.tensor_scalar(out=rstdT[l].t[:, 0:nst], in0=ssq[l].t[:, 0:nst], scalar1=1.0 / D,
                                      scalar2=EPS, op0=ALU.mult, op1=ALU.add), [ssq[l].r], [rstdT[l].r])
        pool(lambda e: e.tensor_tensor(out=rstdT[l].t[:, 0:nst], in0=rstdT[l].t[:, 0:nst],
                                       in1=C("neghalf16")[:, 0:nst], op=ALU.pow), [rstdT[l].r, cst.r], [rstdT[l].r])
        dve(lambda e: e.tensor_scalar(out=e0r.t[:], in0=C("ident")[:, 0:2], scalar1=rstdT[l].t[:, 0:1], scalar2=None,
                                      op0=ALU.mult), [cst.r, rstdT[l].r], [e0r.r])
        for kc in range(8):
            fw.mm(lambda e: e.matmul(po[1][:, 2 * kc:2 * kc + 2], lhsT=xs[:, 0, kc * 128:(kc + 1) * 128], rhs=e0r.t[:],
                                     start=True, stop=True),
                  reads=[R_x[0], e0r.r], writes=[R_o[1]], last=(kc == 7))
        dve(lambda e: e.tensor_tensor(out=h0F.t[:, 0:8], in0=po[1][:, 0:16].rearrange("p (k two) -> p k two", two=2)[:, :, 0],
                                      in1=gmF.t[:], op=ALU.mult), [R_o[1], gmF.r], [h0F.r])
        dve(lambda e: e.tensor_tensor(out=h0F.t[:, 0:8], in0=h0F.t[:, 0:8], in1=shF.t[:], op=ALU.add),
            [h0F.r, shF.r], [h0F.r])
        dve(lambda e: e.tensor_copy(out=cb.t[:], in_=h0F.t[:, 0:8].unsqueeze(2).broadcast_to([128, 8, 128])),
            [h0F.r], [cb.r])
        for kc in range(8):
            st = wst[kc % 2]
            fw.dma("sp", st.t[:, 0:768], wqk_d[l, kc * 128:(kc + 1) * 128, :], writes=[st.r])
            for hf in range(2):
                fw.mm(lambda e: e.matmul(pp[hf][:, 0:384], lhsT=cb.t[:, kc, :], rhs=st.t[:, hf * 384:(hf + 1) * 384],
                                         start=(kc == 0), stop=(kc == 7)),
                      reads=[cb.r, st.r], writes=[R_pp[hf]], last=True)
        act(lambda e: e.activation(out=big32_.t[:, 0:384], in_=pp[0][:, 0:384], func=AF.Copy), [R_pp[0]], [big32_.r])
        dve(lambda e: e.tensor_tensor(out=big32_.t[:, 0:384], in0=big32_.t[:, 0:384], in1=pp[1][:, 0:384], op=ALU.mult),
            [big32_.r, R_pp[1]], [big32_.r])
        dve(lambda e: e.tensor_reduce(out=s00.t[:, 0:4], in_=big32_.t[:, 0:256].rearrange("p (h d) -> p h d", d=64),
                                      axis=AX.X, op=ALU.add), [big32_.r], [s00.r])
        dve(lambda e: e.tensor_reduce(out=s00.t[:, 4:8], in_=big32_.t[:, 256:384].rearrange("p (h d) -> p h d", d=32),
                                      axis=AX.X, op=ALU.add), [big32_.r], [s00.r])
        dve(lambda e: e.tensor_scalar(out=s00.t[:, 0:4], in0=s00.t[:, 0:4], scalar1=0.125, scalar2=None, op0=ALU.mult),
            [s00.r], [s00.r])
        dve(lambda e: e.tensor_scalar(out=s00.t[:, 4:8], in0=s00.t[:, 4:8], scalar1=float(32 ** -0.5), scalar2=None,
                                      op0=ALU.mult), [s00.r], [s00.r])
        pool(lambda e: e.memset(retS.t[:], 0.0), [], [retS.r])
        pool(lambda e: e.memset(retSb.t[:], 0.0), [], [retSb.r])
        pool(lambda e: e.memset(glaS.t[:], 0.0), [], [glaS.r])
        pool(lambda e: e.memset(glaSb.t[:], 0.0), [], [glaSb.r])

        fw.barrier()
        def genA(t):
            p2 = t % 2
            p3 = t % 3
            xt = xs[:, t, :]
            Rx = R_x[t]
            AB = [(pp[0], R_pp[0]), (pp[1], R_pp[1]), (psc[1], R_sc[1])]
            GB = (psc[0], R_sc[0])
            dve(lambda e: e.tensor_scalar(out=xn.t[:], in0=xt, scalar1=rstdT[l].t[:, t:t + 1], scalar2=None,
                                          op0=ALU.mult), [Rx, rstdT[l].r], [xn.r])
            for kc in range(8):
                fw.mm(lambda e: e.transpose(out=ptA[:, kc * 128:(kc + 1) * 128], in_=xn.t[:, kc * 128:(kc + 1) * 128],
                                            identity=identb.t[:]),
                      reads=[xn.r, identb.r], writes=[R_tA], last=(kc == 7))
            if l == 0 and t + 2 < n_tiles:
                t2 = t + 2
                act(lambda e: e.activation(out=xn.t[:], in_=xs[:, t2, :], func=AF.Square,
                                           accum_out=ssq[0].t[:, t2:t2 + 1]), [R_x[t2]], [xn.r, ssq[0].r])
                dve(lambda e: e.tensor_scalar(out=rstdT[0].t[:, t2:t2 + 1], in0=ssq[0].t[:, t2:t2 + 1], scalar1=1.0 / D,
                                              scalar2=EPS, op0=ALU.mult, op1=ALU.add), [ssq[0].r], [rstdT[0].r])
                pool(lambda e: e.tensor_tensor(out=rstdT[0].t[:, t2:t2 + 1], in0=rstdT[0].t[:, t2:t2 + 1],
                                               in1=C("neghalf")[:, 0:1], op=ALU.pow), [rstdT[0].r, cst.r], [rstdT[0].r])
            for kc in range(8):
                if kc < 4:
                    act(lambda e: e.activation(out=hT.t[:, kc, :], in_=ptA[:, kc * 128:(kc + 1) * 128],
                                               func=AF.Identity, scale=gmF.t[:, kc:kc + 1], bias=shF.t[:, kc:kc + 1]),
                        [R_tA, gmF.r, shF.r], [hT.r])
                else:
                    dve(lambda e: e.tensor_scalar(out=hT.t[:, kc, :], in0=ptA[:, kc * 128:(kc + 1) * 128],
                                                  scalar1=gmF.t[:, kc:kc + 1], scalar2=shF.t[:, kc:kc + 1],
                                                  op0=ALU.mult, op1=ALU.add),
                        [R_tA, gmF.r, shF.r], [hT.r])
            yield

            def proj(ci, bank):
                for kc in range(8):
                    fw.mm(lambda e: e.matmul(AB[bank][0][:, :], lhsT=hT.t[:, kc, :],
                                             rhs=win[:, kc, COFFS[ci]:COFFS[ci] + 512],
                                             start=(kc == 0), stop=(kc == 7)),
                          reads=[hT.r] + R_winc[kc], writes=[AB[bank][1]], last=(kc == 7))

            cA = tabs.t[:, 0, t, :]
            sA = tabs.t[:, 1, t, :]
            cR = tabs.t[:, 2, t, :]
            sR = tabs.t[:, 3, t, :]

            def rotary(src, R_src, nh, cos, sin, dst, dst_off, R_dst, final_eng):
                n = nh * 64
                rotA, rotB = (rotA1, rotB1) if nh == 8 else (rotA2, rotB2)
                s4 = src.rearrange("p (h two f) -> p h two f", two=2, f=32)
                a4 = rotA.t[:, 0:n].rearrange("p (h two f) -> p h two f", two=2, f=32)
                b4 = rotB.t[:, 0:n].rearrange("p (h two f) -> p h two f", two=2, f=32)
                d4 = dst[:, dst_off:dst_off + n].rearrange("p (h two f) -> p h two f", two=2, f=32)
                cos4 = cos.unsqueeze(1).unsqueeze(1).broadcast_to([128, nh, 2, 32])
                sin3 = sin.unsqueeze(1).broadcast_to([128, nh, 32])
                dve(lambda e: e.tensor_tensor(out=a4, in0=s4, in1=cos4, op=ALU.mult),
                    [R_src, tabs.r], [rotA.r])
                dve(lambda e: e.scalar_tensor_tensor(out=b4[:, :, 0, :], in0=s4[:, :, 1, :], scalar=-1.0, in1=sin3,
                                                     op0=ALU.mult, op1=ALU.mult), [R_src, tabs.r], [rotB.r])
                dve(lambda e: e.tensor_tensor(out=b4[:, :, 1, :], in0=s4[:, :, 0, :], in1=sin3, op=ALU.mult),
                    [R_src, tabs.r], [rotB.r])
                fw.op(final_eng, lambda e: e.tensor_tensor(out=d4, in0=a4, in1=b4, op=ALU.add),
                      [rotA.r, rotB.r], [R_dst])

            for kc in range(8):
                fw.mm(lambda e: e.matmul(GB[0][:, 0:128], lhsT=win[:, kc, 0:128], rhs=hT.t[:, kc, :],
                                         start=(kc == 0), stop=(kc == 7)),
                      reads=[hT.r] + R_winc[kc], writes=[GB[1]], last=(kc == 7))
            gmk = C("gamask")
            act(lambda e: e.activation(out=gaTe.t[:], in_=GB[0][:, 0:128], func=AF.Identity,
                                       scale=gmk[:, 0:1], bias=gmk[:, 1:2]), [GB[1], cst.r], [gaTe.r])
            proj(0, 0)
            fw.mm(lambda e: e.matmul(GB[0][:, 128:256], lhsT=gaTe.t[:], rhs=gwe.t[:], start=True, stop=True),
                  reads=[gaTe.r, gwe.r], writes=[GB[1]], last=True)
            act(lambda e: e.activation(out=nl.t[:], in_=GB[0][:, 128:256], func=AF.Exp, scale=-1.0),
                [GB[1]], [nl.r])
            act(lambda e: e.activation(out=nl.t[:], in_=nl.t[:], func=AF.Ln, bias=1.0), [nl.r], [nl.r])
            rotary(AB[0][0][:, 0:512], AB[0][1], 8, cA, sA, aqk.t, 0, aqk.r, "pool")
            yield
            proj(1, 1)
            rotary(AB[1][0][:, 0:128], AB[1][1], 2, cA, sA, aqk.t, 512, aqk.r, "pool")
            act(lambda e: e.activation(out=vext[p3].t[:, :, 0:64],
                                       in_=AB[1][0][:, 128:256].rearrange("p (g d) -> p g d", d=64), func=AF.Copy),
                [AB[1][1]], [vext[p3].r])
            rotary(AB[1][0][:, 256:512], AB[1][1], 4, cR, sR, rqk32.t, 0, rqk32.r, "pool")
            yield
            proj(2, 2)
            rotary(AB[2][0][:, 0:256], AB[2][1], 4, cR, sR, rqk32.t, 256, rqk32.r, "pool")
            act(lambda e: e.activation(out=rv[p2].t[:], in_=AB[2][0][:, 256:512], func=AF.Copy), [AB[2][1]], [rv[p2].r])
            pool(lambda e: e.tensor_tensor(out=rqk[p2].t[:].rearrange("p (h d) -> p h d", d=64),
                                           in0=rqk32.t[:].rearrange("p (h d) -> p h d", d=64),
                                           in1=C("dec8").unsqueeze(2).broadcast_to([128, 8, 64]), op=ALU.mult),
                 [rqk32.r, cst.r], [rqk[p2].r])
            yield
            for ci, bank in ((3, 0), (4, 1)):
                proj(ci, bank)
                o = (ci - 3) * 512
                act(lambda e: e.activation(out=th.t[:], in_=AB[bank][0][:, :], func=AF.Tanh, scale=0.5),
                    [AB[bank][1]], [th.r])
                dve(lambda e: e.scalar_tensor_tensor(out=sg[p2].t[:, o:o + 512], in0=th.t[:], scalar=1.0,
                                                     in1=AB[bank][0][:, :], op0=ALU.add, op1=ALU.mult),
                    [th.r, AB[bank][1]], [sg[p2].r])
                yield
            fw.mm(lambda e: e.matmul(GB[0][:, 128:256], lhsT=C("tri_in"), rhs=nl.t[:], start=True, stop=True),
                  reads=[cst.r, nl.r], writes=[GB[1]], last=False)
            fw.mm(lambda e: e.matmul(GB[0][:, 256:384], lhsT=C("tri_rev"), rhs=nl.t[:], start=True, stop=True),
                  reads=[cst.r, nl.r], writes=[GB[1]], last=False)
            fw.mm(lambda e: e.matmul(GB[0][:, 384:386], lhsT=nl.t[:], rhs=C("ncol"), start=True, stop=True),
                  reads=[cst.r, nl.r], writes=[GB[1]], last=True)
            act(lambda e: e.activation(out=Eq.t[:], in_=GB[0][:, 128:256], func=AF.Exp), [GB[1]], [Eq.r])
            act(lambda e: e.activation(out=Ek.t[:], in_=GB[0][:, 128:256], func=AF.Exp, scale=-1.0),
                [GB[1]], [Ek.r])
            act(lambda e: e.activation(out=Er.t[:], in_=GB[0][:, 256:384], func=AF.Exp), [GB[1]], [Er.r])
            act(lambda e: e.activation(out=ebl[p2].t[:], in_=GB[0][:, 384:386], func=AF.Exp), [GB[1]], [ebl[p2].r])
            proj(5, 2)
            dve(lambda e: e.scalar_tensor_tensor(out=gqk.t[:, 0:128], in0=AB[2][0][:, 0:128], scalar=float(32 ** -0.5),
                                                 in1=Eq.t[:], op0=ALU.mult, op1=ALU.mult),
                [AB[2][1], Eq.r], [gqk.r])
            dve(lambda e: e.tensor_tensor(out=gqk.t[:, 128:256], in0=AB[2][0][:, 128:256], in1=Ek.t[:], op=ALU.mult),
                [AB[2][1], Ek.r], [gqk.r])
            dve(lambda e: e.tensor_tensor(out=gkh[p2].t[:], in0=AB[2][0][:, 128:256], in1=Er.t[:], op=ALU.mult),
                [AB[2][1], Er.r], [gkh[p2].r])
            act(lambda e: e.activation(out=gv[p2].t[:], in_=AB[2][0][:, 256:512], func=AF.Copy), [AB[2][1]], [gv[p2].r])
            yield
            for j in range(5):
                fw.mm(lambda e: e.transpose(out=ptB[:, j * 128:(j + 1) * 128], in_=aqk.t[:, j * 128:(j + 1) * 128],
                                            identity=identb.t[:]),
                      reads=[aqk.r, identb.r], writes=[R_tB], last=(j == 4))
            src4 = ptB[:, 0:512].rearrange("p (c q) -> p c q", q=128)
            act(lambda e: e.activation(out=qbd[p2][0].t[0:64, :, :], in_=src4[0:64, :, :], func=AF.Copy),
                [R_tB], [qbd[p2][0].r])
            act(lambda e: e.activation(out=kT[p3].t[:], in_=ptB[:, 512:640], func=AF.Copy), [R_tB], [kT[p3].r])
            dve(lambda e: e.tensor_copy(out=qbd[p2][1].t[64:128, :, :], in_=src4[64:128, :, :]), [R_tB], [qbd[p2][1].r])
            yield
            for j in range(4):
                fw.mm(lambda e: e.transpose(out=ptB[:, j * 128:(j + 1) * 128], in_=rqk[p2].t[:, j * 128:(j + 1) * 128],
                                            identity=identb.t[:]),
                      reads=[rqk[p2].r, identb.r], writes=[R_tB], last=False)
            for j in range(2):
                fw.mm(lambda e: e.transpose(out=ptB[:, (4 + j) * 128:(5 + j) * 128],
                                            in_=gqk.t[:, j * 128:(j + 1) * 128], identity=identb.t[:]),
                      reads=[gqk.r, identb.r], writes=[R_tB], last=(j == 1))
            for c in range(2):
                act(lambda e: e.activation(out=rqbd[p2][c].t[0:64, 0, :], in_=ptB[0:64, c * 128:(c + 1) * 128],
                                           func=AF.Copy), [R_tB], [rqbd[p2][c].r])
            act(lambda e: e.activation(out=rkT[p2].t[:].rearrange("p c q -> p (c q)"), in_=ptB[:, 256:512],
                                       func=AF.Copy), [R_tB], [rkT[p2].r])
            act(lambda e: e.activation(out=gqT[p2].t[:], in_=ptB[:, 512:640], func=AF.Copy), [R_tB], [gqT[p2].r])
            for hh in (0, 2):
                act(lambda e: e.activation(out=gqbd[p2].t[32 * hh:32 * hh + 32, hh, :],
                                           in_=ptB[32 * hh:32 * hh + 32, 512:640], func=AF.Copy),
                    [R_tB], [gqbd[p2].r])
            for c in range(2):
                dve(lambda e: e.tensor_copy(out=rqbd[p2][c].t[64:128, 1, :], in_=ptB[64:128, c * 128:(c + 1) * 128]),
                    [R_tB], [rqbd[p2][c].r])
            dve(lambda e: e.tensor_copy(out=rqT[p2].t[:].rearrange("p c q -> p (c q)"), in_=ptB[:, 0:256]),
                [R_tB], [rqT[p2].r])
            for hh in (1, 3):
                dve(lambda e: e.tensor_copy(out=gqbd[p2].t[32 * hh:32 * hh + 32, hh, :],
                                            in_=ptB[32 * hh:32 * hh + 32, 512:640]), [R_tB], [gqbd[p2].r])
            dve(lambda e: e.tensor_copy(out=gkT[p2].t[:], in_=ptB[:, 640:768]), [R_tB], [gkT[p2].r])
            yield

        def genB(t):
            p2 = t % 2
            p3 = t % 3
            pv3 = (t - 1) % 3
            xt = xs[:, t, :]
            Rx = R_x[t]
            blocks = [(p3, mcurb)] + ([(pv3, mprevb)] if t > 0 else [])
            for g in range(2):
                for bi, (kp, mk) in enumerate(blocks):
                    bank = 0
                    fw.mm(lambda e: e.matmul(psc[bank][:, :], lhsT=kT[kp].t[:],
                                             rhs=qbd[p2][g].t[:].rearrange("p c q -> p (c q)"), start=True, stop=False),
                          reads=[kT[kp].r, qbd[p2][g].r], writes=[R_sc[bank]], last=False)
                    fw.mm(lambda e: e.matmul(psc[bank][:, :], lhsT=identb.t[:], rhs=mk.t[:], start=False, stop=True),
                          reads=[identb.r, mk.r], writes=[R_sc[bank]], last=True)
                    act(lambda e: e.activation(out=PT[g][bi].t[:], in_=psc[bank][:, :], func=AF.Exp, scale=0.125),
                        [R_sc[bank]], [PT[g][bi].r])
                yield
            for g in range(2):
                for c in range(4):
                    for bi, (kp, mk) in enumerate(blocks):
                        fw.mm(lambda e: e.matmul(po[g][:, c * 65:(c + 1) * 65], lhsT=PT[g][bi].t[:, c * 128:(c + 1) * 128],
                                                 rhs=vext[kp].t[:, g, :], start=(bi == 0), stop=(bi == len(blocks) - 1)),
                              reads=[PT[g][bi].r, vext[kp].r], writes=[R_o[g]],
                              last=(c == 3 and bi == len(blocks) - 1))
                o4 = po[g][:, 0:260].rearrange("p (c d) -> p c d", d=65)
                dve(lambda e: e.scalar_tensor_tensor(out=stat1.t[:, 0:4], in0=o4[:, :, 64], scalar=2.0,
                                                     in1=esink2.t[:, 4 * g:4 * g + 4], op0=ALU.mult, op1=ALU.add),
                    [R_o[g], esink2.r], [stat1.r])
                dve(lambda e: e.reciprocal(out=stat1.t[:, 4:8], in_=stat1.t[:, 0:4]), [stat1.r], [stat1.r])
                dve(lambda e: e.tensor_tensor(out=an.t[:].rearrange("p (c d) -> p c d", d=64), in0=o4[:, :, 0:64],
                                              in1=stat1.t[:, 4:8].unsqueeze(2).broadcast_to([128, 4, 64]),
                                              op=ALU.mult), [R_o[g], stat1.r], [an.r])
                pool(lambda e: e.tensor_tensor(out=mix.t[:, g * 256:(g + 1) * 256], in0=an.t[:],
                                               in1=sg[p2].t[:, g * 256:(g + 1) * 256], op=ALU.mult),
                     [an.r, sg[p2].r], [mix.r])
                yield

            def head_norm(bank, R_bank, off, gain, rraw, rsqt, rsq_r, stat2):
                act(lambda e: e.activation(out=rraw.t[:], in_=bank[:, 0:256], func=AF.Copy), [R_bank], [rraw.r])
                act(lambda e: e.activation(out=rsqt, in_=bank[:, 0:256], func=AF.Square), [R_bank], [rsq_r])
                dve(lambda e: e.tensor_reduce(out=stat2.t[:, 0:4], in_=rsqt.rearrange("p (h d) -> p h d", d=64),
                                              axis=AX.X, op=ALU.add), [rsq_r], [stat2.r])
                dve(lambda e: e.tensor_scalar(out=stat2.t[:, 0:4], in0=stat2.t[:, 0:4], scalar1=4.0 / 64.0,
                                              scalar2=4.0 * EPS, op0=ALU.mult, op1=ALU.add), [stat2.r], [stat2.r])
                pool(lambda e: e.tensor_tensor(out=stat2.t[:, 4:8], in0=stat2.t[:, 0:4], in1=C("neghalf")[:, 0:4],
                                               op=ALU.pow), [stat2.r, cst.r], [stat2.r])
                dve(lambda e: e.tensor_tensor(out=rraw.t[:].rearrange("p (h d) -> p h d", d=64),
                                              in0=rraw.t[:].rearrange("p (h d) -> p h d", d=64),
                                              in1=stat2.t[:, 4:8].unsqueeze(2).broadcast_to([128, 4, 64]), op=ALU.mult),
                    [rraw.r, stat2.r], [rraw.r])
                if gain is not None:
                    pool(lambda e: e.tensor_tensor(out=rraw.t[:].rearrange("p (h d) -> p h d", d=64),
                                                   in0=rraw.t[:].rearrange("p (h d) -> p h d", d=64),
                                                   in1=gain.t[:].unsqueeze(1).broadcast_to([128, 4, 64]), op=ALU.mult),
                         [rraw.r, gain.r], [rraw.r])
                pool(lambda e: e.tensor_tensor(out=mix.t[:, off:off + 256], in0=rraw.t[:], in1=sg[p2].t[:, off:off + 256],
                                               op=ALU.mult), [rraw.r, sg[p2].r], [mix.r])

            for c in range(2):
                fw.mm(lambda e: e.matmul(psc[0][:, c * 256:(c + 1) * 256], lhsT=rkT[p2].t[:, c, :],
                                         rhs=rqbd[p2][c].t[:].rearrange("p h q -> p (h q)"), start=True, stop=True),
                      reads=[rkT[p2].r, rqbd[p2][c].r], writes=[R_sc[0]], last=(c == 1))
            dve(lambda e: e.tensor_tensor(out=scT.t[:], in0=psc[0][:, :], in1=causalb.t[:], op=ALU.mult),
                [R_sc[0], causalb.r], [scT.r])
            if t == 0:
                dve(lambda e: e.tensor_copy(out=scT.t[0:1, :].rearrange("p (h i) -> p h i", i=128)[:, :, 0],
                                            in_=s00.t[0:1, 0:4]), [scT.r, s00.r], [scT.r])
            yield
            for hh in range(4):
                fw.mm(lambda e: e.matmul(po[0][:, hh * 64:(hh + 1) * 64], lhsT=scT.t[:, hh * 128:(hh + 1) * 128],
                                         rhs=rv[p2].t[:, hh * 64:(hh + 1) * 64], start=(hh == 0), stop=False),
                      reads=[scT.r, rv[p2].r], writes=[R_o[0]], last=False)
            for c in range(2):
                fw.mm(lambda e: e.matmul(po[0][:, c * 128:(c + 1) * 128], lhsT=rqT[p2].t[:, c, :],
                                         rhs=retSb.t[:, c * 128:(c + 1) * 128], start=False, stop=(c == 1)),
                      reads=[rqT[p2].r, retSb.r], writes=[R_o[0]], last=False)
            for c in range(2):
                fw.mm(lambda e: e.matmul(po[0][:, 256 + c * 128:256 + (c + 1) * 128],
                                         lhsT=rqk[p2].t[:, 256 + c * 128:256 + (c + 1) * 128],
                                         rhs=rv[p2].t[:, c * 128:(c + 1) * 128], start=True, stop=True),
                      reads=[rqk[p2].r, rv[p2].r], writes=[R_o[0]], last=(c == 1))
            head_norm(po[0], R_o[0], 512, None, rraw, rsq.t[:], rsq.r, stat2)
            dve(lambda e: e.tensor_tensor(out=dS.t[:], in0=po[0][:, 256:512], in1=C("bd64dec"), op=ALU.mult),
                [R_o[0], cst.r], [dS.r])
            pool(lambda e: e.tensor_tensor(out=retS.t[:].rearrange("p (c n) -> p c n", n=128),
                                           in0=retS.t[:].rearrange("p (c n) -> p c n", n=128),
                                           in1=C("sdec").unsqueeze(2).broadcast_to([128, 2, 128]), op=ALU.mult),
                 [retS.r, cst.r], [retS.r])
            pool(lambda e: e.tensor_tensor(out=retS.t[:], in0=retS.t[:], in1=dS.t[:], op=ALU.add),
                 [retS.r, dS.r], [retS.r])
            act(lambda e: e.activation(out=retSb.t[:], in_=retS.t[:], func=AF.Copy), [retS.r], [retSb.r])
            yield

            fw.mm(lambda e: e.matmul(psc[0][:, :], lhsT=gkT[p2].t[:], rhs=gqbd[p2].t[:].rearrange("p h q -> p (h q)"),
                                     start=True, stop=True),
                  reads=[gkT[p2].r, gqbd[p2].r], writes=[R_sc[0]], last=True)
            dve(lambda e: e.tensor_tensor(out=scT.t[:], in0=psc[0][:, :], in1=causalb.t[:], op=ALU.mult),
                [R_sc[0], causalb.r], [scT.r])
            if t == 0:
                dve(lambda e: e.tensor_copy(out=scT.t[0:1, :].rearrange("p (h i) -> p h i", i=128)[:, :, 0],
                                            in_=s00.t[0:1, 4:8]), [scT.r, s00.r], [scT.r])
            yield
            for hh in range(4):
                fw.mm(lambda e: e.matmul(po[1][:, hh * 64:(hh + 1) * 64], lhsT=scT.t[:, hh * 128:(hh + 1) * 128],
                                         rhs=gv[p2].t[:, hh * 64:(hh + 1) * 64], start=(hh == 0), stop=False),
                      reads=[scT.r, gv[p2].r], writes=[R_o[1]], last=False)
            fw.mm(lambda e: e.matmul(po[1][:, 0:256], lhsT=gqT[p2].t[:], rhs=glaSb.t[:], start=False, stop=True),
                  reads=[gqT[p2].r, glaSb.r], writes=[R_o[1]], last=False)
            fw.mm(lambda e: e.matmul(po[1][:, 256:512], lhsT=gkh[p2].t[:], rhs=gv[p2].t[:], start=True, stop=True),
                  reads=[gkh[p2].r, gv[p2].r], writes=[R_o[1]], last=True)
            head_norm(po[1], R_o[1], 768, ggain, an, big32_.t[:, 0:256], big32_.r, stat3)
            dve(lambda e: e.tensor_tensor(out=dS.t[:], in0=po[1][:, 256:512], in1=C("bd32"), op=ALU.mult),
                [R_o[1], cst.r], [dS.r])
            dve(lambda e: e.scalar_tensor_tensor(out=glaS.t[:], in0=glaS.t[:], scalar=ebl[p2].t[:, 0:1], in1=dS.t[:],
                                                 op0=ALU.mult, op1=ALU.add), [glaS.r, ebl[p2].r, dS.r], [glaS.r])
            act(lambda e: e.activation(out=glaSb.t[:], in_=glaS.t[:], func=AF.Copy), [glaS.r], [glaSb.r])
            yield

            if dbg is not None and l == 0 and t in dbg:
                dump("hT_%d" % t, hT.t[:].rearrange("p k q -> p (k q)"), hT.r, 1024)
                dump("aqk_%d" % t, aqk.t[:], aqk.r, 640)
                dump("rqk_%d" % t, rqk[p2].t[:], rqk[p2].r, 512)
                dump("gqk_%d" % t, gqk.t[:], gqk.r, 256)
                dump("gkh_%d" % t, gkh[p2].t[:], gkh[p2].r, 128)
                dump("sg_%d" % t, sg[p2].t[:], sg[p2].r, 1024)
                dump("nl_%d" % t, nl.t[:], nl.r, 128)
                dump("mix_%d" % t, mix.t[:], mix.r, 1024)
                dump("retS_%d" % t, retS.t[:], retS.r, 256)
                dump("glaS_%d" % t, glaS.t[:], glaS.r, 256)
            for j in range(8):
                fw.mm(lambda e: e.transpose(out=ptB[:, j * 128:(j + 1) * 128], in_=mix.t[:, j * 128:(j + 1) * 128],
                                            identity=identb.t[:]),
                      reads=[mix.r, identb.r], writes=[R_tB], last=(j == 7))
            act(lambda e: e.activation(out=mixT.t[:, 0:4, :].rearrange("p k q -> p (k q)"), in_=ptB[:, 0:512],
                                       func=AF.Copy), [R_tB], [mixT.r])
            dve(lambda e: e.tensor_copy(out=mixT.t[:, 4:8, :].rearrange("p k q -> p (k q)"), in_=ptB[:, 512:1024]),
                [R_tB], [mixT.r])
            yield
            for hf in range(2):
                for kc in range(8):
                    fw.mm(lambda e: e.matmul(po[hf][:, :], lhsT=mixT.t[:, kc, :],
                                             rhs=wout[:, kc, hf * 512:(hf + 1) * 512], start=(kc == 0), stop=(kc == 7)),
                          reads=[mixT.r, R_woutc[kc]], writes=[R_o[hf]], last=(kc == 7))
                yield
            for hf in range(2):
                act(lambda e: e.activation(out=mix.t[:, hf * 512:(hf + 1) * 512], in_=po[hf][:, :], func=AF.Square,
                                           accum_out=stat.t[:, 16 + hf:17 + hf]), [R_o[hf]], [mix.r, stat.r])
            dve(lambda e: e.tensor_tensor(out=stat.t[:, 18:19], in0=stat.t[:, 16:17], in1=stat.t[:, 17:18], op=ALU.add),
                [stat.r], [stat.r])
            dve(lambda e: e.tensor_scalar(out=stat.t[:, 19:20], in0=stat.t[:, 18:19], scalar1=1.0 / D, scalar2=EPS,
                                          op0=ALU.mult, op1=ALU.add), [stat.r], [stat.r])
            pool(lambda e: e.tensor_tensor(out=stat.t[:, 20:21], in0=stat.t[:, 19:20], in1=C("neghalf")[:, 0:1],
                                           op=ALU.pow), [stat.r, cst.r], [stat.r])
            for hf in range(2):
                dve(lambda e: e.scalar_tensor_tensor(out=big32[hf].t[:], in0=po[hf][:, :],
                                                     scalar=stat.t[:, 20:21], in1=gpg.t[:, hf * 512:(hf + 1) * 512],
                                                     op0=ALU.mult, op1=ALU.mult),
                    [R_o[hf], stat.r, gpg.r], [big32[hf].r])
                dve(lambda e: e.tensor_tensor(out=xt[:, hf * 512:(hf + 1) * 512], in0=xt[:, hf * 512:(hf + 1) * 512],
                                              in1=big32[hf].t[:], op=ALU.add), [Rx, big32[hf].r], [Rx])
            if l == L - 1:
                out_toks.append(fw.dma("sp", out_d[t * 128:(t + 1) * 128, :], xt, reads=[Rx]))
            else:
                act(lambda e: e.activation(out=mix.t[:], in_=xt, func=AF.Square, accum_out=ssq[l + 1].t[:, t:t + 1]),
                    [Rx], [mix.r, ssq[l + 1].r])
            yield

        if l == 0:
            for t_ in range(2, n_tiles):
                fw.dma("sp", xs[:, t_, :], x_d[t_ * 128:(t_ + 1) * 128, :], writes=[R_x[t_]], nobar=True)
        for _ in genA(0):
            pass
        for t in range(n_tiles):
            gb = genB(t)
            ga_ = genA(t + 1) if t + 1 < n_tiles else iter(())
            alive_a = alive_b = True
            while alive_a or alive_b:
                if alive_b:
                    try:
                        next(gb)
                    except StopIteration:
                        alive_b = False
                if alive_a:
                    try:
                        next(ga_)
                    except StopIteration:
                        alive_a = False
        if l + 1 < L:
            load_weights(l + 1, early=True)
        if dbg is not None and l == 0:
            dump("gmF", gmF.t[:], gmF.r, 8)
            dump("shF", shF.t[:], shF.r, 8)
            dump("gpg", gpg.t[:], gpg.r, 1024)
            dump("tabs", tabs.t[:].rearrange("p a t f -> p (a t f)"), tabs.r, 4 * NT * 32)
        fw.barrier()

    fw.flush()
    e = fw.engs["sp"]
    for s in fw.dsems + fw.dsems_nb + [x[0] for x in fw.dsems_sw]:
        if s.count:
            e.wait((s, s.count, "dma"))
    fw.close()
    return nc


_NC_CACHE = {}


def _prep_shared(w_mod, b_mod, pre_norm_gain, post_norm_gain, w_in, attn_sinks, gla_gate_w, gla_gate_b,
                 gla_norm_gain, w_out):
    L = w_in.shape[0]
    f = lambda a: np.ascontiguousarray(np.asarray(a, dtype=np.float32))
    b_mod = np.asarray(b_mod, np.float32)
    gwe = np.zeros((L, 128, 128), np.float32)
    gwe[:, 0:16, :] = np.asarray(gla_gate_w, np.float32)
    gwe[:, 16, :] = np.asarray(gla_gate_b, np.float32)
    return {
        "w_mod": f(w_mod),
        "bmodF": f(b_mod.reshape(L, 24, 128).transpose(0, 2, 1)),
        "bgate": f(b_mod[:, 2048:3072]),
        "pregF": f(np.asarray(pre_norm_gain, np.float32).reshape(L, 8, 128).transpose(0, 2, 1)),
        "postg": f(post_norm_gain),
        "w_in": f(np.asarray(w_in, np.float32)[:, :, _PERM]),
        "w_qk32": f(np.asarray(w_in, np.float32)[:, :, _QK32]),
        "sinks": f(attn_sinks),
        "gwe": gwe,
        "ggain": f(gla_norm_gain),
        "w_out": f(w_out),
        "consts": _CBLOB,
        "cmask": _CMASK,
    }


def kernel(x, c, positions, w_mod, b_mod, pre_norm_gain, post_norm_gain, w_in,
           attn_sinks, gla_gate_w, gla_gate_b, gla_norm_gain, w_out):
    x = np.asarray(x, np.float32)
    c = np.asarray(c, np.float32)
    positions = np.asarray(positions, np.int32)
    B = x.shape[0]
    shared = _prep_shared(w_mod, b_mod, pre_norm_gain, post_norm_gain, w_in, attn_sinks, gla_gate_w,
                          gla_gate_b, gla_norm_gain, w_out)
    if "nc" not in _NC_CACHE:
        _NC_CACHE["nc"] = build(n_layers=2)
    nc = _NC_CACHE["nc"]
    in_maps = []
    for b in range(B):
        m = dict(shared)
        m["x"] = np.ascontiguousarray(x[b])
        m["c"] = np.ascontiguousarray(c[b].reshape(8, 128).T)
        m["pos"] = np.ascontiguousarray(positions[b].reshape(NT, 128).T)
        in_maps.append(m)
    res = run_bass_kernel_spmd(nc, in_maps, core_ids=list(range(B)))
    return np.stack([np.asarray(r["out"], np.float32) for r in res.results], axis=0)
```

```python
import os
import numpy as np
import concourse.bass as bass
import concourse.mybir as mybir
from concourse.bass_utils import run_bass_kernel_spmd

F32 = mybir.dt.float32
BF16 = mybir.dt.bfloat16
I32 = mybir.dt.int32
AF = mybir.ActivationFunctionType
ALU = mybir.AluOpType
AX = mybir.AxisListType

S = 2048
D = 1024
NT = 16
DIN = 3088
EPS = 1e-6
NEG = -30000.0


class Sem:
    def __init__(self, h, name):
        self.h = h
        self.count = 0
        self.name = name


class Res:
    def __init__(self, name, excl=False):
        self.name = name
        self.excl = excl
        self.w = None
        self.r = {}


_SNAP = {}


class Eng:
    def __init__(self, name, h, sem):
        self.name = name
        self.h = h
        self.sem = sem
        self.seen = {}

    def wait(self, tok):
        if tok is None:
            return
        if tok[0] == "PENDING":
            if self.name == "pe":
                return
            raise RuntimeError("wait on pending PE token by " + self.name)
        s, v, _ = tok
        if self.seen.get(id(s), 0) < v:
            self.h.wait_ge(s.h, v)
            self.seen[id(s)] = v
        snap = _SNAP.get((id(s), v))
        if snap:
            for k, val in snap.items():
                if self.seen.get(k, 0) < val:
                    self.seen[k] = val


class _Probe:
    def __init__(self):
        self.n = 0
        self.opname = ""
        self.fp32 = False
        self.accum = False

    def __getattr__(self, name):
        def f(*a, **k):
            out = k.get("out", a[0] if a else None)
            n = 1
            for d in out.shape[1:]:
                n *= d
            self.n = n
            self.opname = name
            self.call = (name, a, k)
            lt = k.get("lhsT", None)
            self.fp32 = lt is not None and lt.dtype == F32
            self.accum = k.get("accum_out", None) is not None
            return self
        return f

    def then_inc(self, *a, **k):
        return self

    def replay(self):
        name, a, k = self.call
        return lambda e: getattr(e, name)(*a, **k)


class Unit:
    __slots__ = ("kind", "eng", "fns", "reads", "writes", "dur", "busy", "idx", "args", "deps", "nsucc")


def _est(ename, pr):
    n = pr.n
    if ename == "pe":
        if pr.opname == "transpose":
            return 108.0
        return (max(64, n) / 2.4 + 6.0) * (4.0 if pr.fp32 else 1.0)
    if ename == "act":
        return 190.0 + n / 1.2 + (90.0 if pr.accum else 0.0)
    if ename == "dve":
        if pr.opname == "reciprocal":
            return 80.0 + 8.0 * n
        return 70.0 + n * 1.05
    if ename == "pool":
        if pr.opname == "tensor_tensor" and n <= 16:
            return 750.0
        return 150.0 + n * 2.3
    return 100.0


class FW:
    def __init__(self, nc, ndma_sems=16):
        self.rec = None
        self.cur_pe = None
        self.nc = nc
        self._ctx = []
        self.engs = {}
        for name, h in (("pe", nc.tensor), ("dve", nc.vector), ("act", nc.scalar),
                        ("pool", nc.gpsimd), ("sp", nc.sync)):
            self.engs[name] = Eng(name, h, self._sem("s_" + name))
        self.dsems = [self._sem("d%d" % i) for i in range(ndma_sems)]
        self.dnext = 0
        self.dsems_nb = [self._sem("n%d" % i) for i in range(8)]
        self.dnext_nb = 0
        self.dsems_sw = []
        self.pe_pending = []

    def _sem(self, name):
        cm = self.nc.semaphore(name)
        h = cm.__enter__()
        self._ctx.append(cm)
        return Sem(h, name)

    def barrier(self):
        was = self.rec is not None
        self.flush()
        self._barrier()
        if was:
            self.start_recording()

    def _barrier(self):
        assert not self.pe_pending
        toks = [(e.sem, e.sem.count, e.name) for e in self.engs.values() if e.sem.count]
        toks += [(s, s.count, "dma") for s in self.dsems if s.count]
        toks += [(s, s.count, "dma") for s, nb in self.dsems_sw if s.count and not nb]
        for e in self.engs.values():
            for t in toks:
                if t[2] != e.name:
                    e.wait(t)

    def sb(self, name, shape, dt):
        n = 1
        for d in shape[1:]:
            n *= d
        self.nbytes = getattr(self, "nbytes", 0) + n * (2 if dt == BF16 else 4)
        cm = self.nc.sbuf_tensor("sb_" + name, list(shape), dt)
        t = cm.__enter__()
        self._ctx.append(cm)
        return t

    def ps(self, name, shape, dt):
        cm = self.nc.psum_tensor(name, list(shape), dt)
        t = cm.__enter__()
        self._ctx.append(cm)
        return t

    def close(self):
        for cm in reversed(self._ctx):
            cm.__exit__(None, None, None)
        self._ctx = []

    def _acq(self, e, reads, writes):
        for r in reads:
            e.wait(r.w)
            if r.excl:
                for en, t in r.r.items():
                    if en != e.name:
                        e.wait(t)
        for w in writes:
            e.wait(w.w)
            for en, t in w.r.items():
                e.wait(t)

    def _rel(self, ename, tok, reads, writes):
        for r in reads:
            r.r[ename] = tok
        for w in writes:
            w.w = tok
            w.r = {}

    def start_recording(self):
        self.rec = []
        self.cur_pe = None

    def flush(self):
        if self.rec is None:
            return
        assert self.cur_pe is None
        units = self.rec
        self.rec = None
        n = len(units)
        lastw = {}
        readers = {}
        succ = [[] for _ in range(n)]
        for i, u in enumerate(units):
            u.idx = i
            deps = set()
            for r in u.reads:
                k = id(r)
                if r.excl:
                    if k in lastw:
                        deps.add(lastw[k])
                    deps.update(readers.get(k, ()))
                elif k in lastw:
                    deps.add(lastw[k])
            for w in u.writes:
                k = id(w)
                if k in lastw:
                    deps.add(lastw[k])
                deps.update(readers.get(k, ()))
            deps.discard(i)
            u.deps = deps
            for d in deps:
                succ[d].append(i)
            for r in u.reads:
                k = id(r)
                if r.excl:
                    lastw[k] = i
                    readers[k] = []
                else:
                    readers.setdefault(k, []).append(i)
            for w in u.writes:
                k = id(w)
                lastw[k] = i
                readers[k] = []
        LAT = float(os.environ.get("SCHED_LAT", "250"))
        blevel = [0.0] * n
        for i in range(n - 1, -1, -1):
            u = units[i]
            m = 0.0
            for j in succ[i]:
                v = blevel[j] + (LAT if units[j].eng != u.eng else 40.0)
                if v > m:
                    m = v
            blevel[i] = u.dur + m
        PEB = float(os.environ.get("SCHED_PEB", "0"))
        if PEB:
            for i in range(n):
                if units[i].eng == "pe":
                    blevel[i] += PEB
        ndep = [len(u.deps) for u in units]
        finish = [0.0] * n
        efree = {}
        ready = [i for i in range(n) if ndep[i] == 0]
        order = []
        SLACK = float(os.environ.get("SCHED_SLACK", "120"))
        while ready:
            ests = []
            mn = None
            for i in ready:
                u = units[i]
                st = efree.get(u.eng, 0.0)
                for d in u.deps:
                    f = finish[d] + (LAT if units[d].eng != u.eng else 40.0)
                    if f > st:
                        st = f
                ests.append(st)
                if mn is None or st < mn:
                    mn = st
            best = None
            bs = None
            bl = -1.0
            for i, st in zip(ready, ests):
                if st <= mn + SLACK and (blevel[i] > bl + 1e-9 or (abs(blevel[i] - bl) <= 1e-9 and i < best)):
                    bl = blevel[i]
                    best = i
                    bs = st
            ready.remove(best)
            u = units[best]
            if getattr(self, "diag", None) is not None and bs > efree.get(u.eng, 0.0) + 1.0:
                bd = max(u.deps, key=lambda d: finish[d] + (LAT if units[d].eng != u.eng else 40.0)) if u.deps else None
                if bd is not None:
                    ud = units[bd]
                    shared = [r.name for r in list(u.reads) + list(u.writes) if r in ud.reads or r in ud.writes]
                    key = (u.eng, shared[0] if shared else "?", ud.eng)
                    self.diag[key] = self.diag.get(key, 0.0) + bs - efree.get(u.eng, 0.0)
            efree[u.eng] = bs + u.busy
            finish[best] = bs + u.dur
            order.append(best)
            for j in succ[best]:
                ndep[j] -= 1
                if ndep[j] == 0:
                    ready.append(j)
        assert len(order) == n
        self.sched_span = getattr(self, "sched_span", 0.0) + max(finish) if n else 0.0
        for i in order:
            u = units[i]
            if u.kind == "op":
                self.op(u.eng, u.fns[0], u.reads, u.writes)
            elif u.kind == "mm":
                for j, (fn, rd, wr) in enumerate(u.fns):
                    self.mm(fn, rd, wr, last=(j == len(u.fns) - 1))
            else:
                self.dma(u.eng, u.args[0], u.args[1], u.reads, u.writes, nobar=u.args[2])

    def _record(self, kind, eng, fns, reads, writes, dur, busy, args=None):
        u = Unit()
        u.kind = kind
        u.eng = eng
        u.fns = fns
        u.reads = tuple(reads)
        u.writes = tuple(writes)
        u.dur = dur
        u.busy = busy
        u.args = args
        self.rec.append(u)

    def op(self, ename, fn, reads=(), writes=()):
        if self.rec is not None:
            pr = _Probe()
            fn(pr)
            d = _est(ename, pr)
            self._record("op", ename, [pr.replay()], reads, writes, d, d)
            return None
        e = self.engs[ename]
        self._acq(e, reads, writes)
        inst = fn(e.h)
        e.sem.count += 1
        inst.then_inc(e.sem.h, 1)
        _SNAP[(id(e.sem), e.sem.count)] = dict(e.seen)
        self._rel(ename, (e.sem, e.sem.count, ename), reads, writes)
        return inst

    def mm(self, fn, reads=(), writes=(), last=False):
        if self.rec is not None:
            pr = _Probe()
            fn(pr)
            d = _est("pe", pr)
            if self.cur_pe is None:
                self.cur_pe = [[], [], [], 0.0]
            g = self.cur_pe
            g[0].append((pr.replay(), tuple(reads), tuple(writes)))
            for r in reads:
                if r not in g[1]:
                    g[1].append(r)
            for w in writes:
                if w not in g[2]:
                    g[2].append(w)
            g[3] += d
            if last:
                self.cur_pe = None
                self._record("mm", "pe", g[0], g[1], g[2], g[3] + 60.0, g[3])
            return None
        e = self.engs["pe"]
        self._acq(e, reads, writes)
        inst = fn(e.h)
        self.pe_pending.append((tuple(reads), tuple(writes)))
        if last:
            e.sem.count += 1
            inst.then_inc(e.sem.h, 1)
            _SNAP[(id(e.sem), e.sem.count)] = dict(e.seen)
            tok = (e.sem, e.sem.count, "pe")
            for rd, wr in self.pe_pending:
                self._rel("pe", tok, rd, wr)
            self.pe_pending = []
        else:
            for w in writes:
                w.w = ("PENDING",)
                w.r = {}
            for r in reads:
                r.r["pe"] = ("PENDING",)
        return inst

    def dma(self, qname, out, in_, reads=(), writes=(), nobar=False):
        if self.rec is not None:
            n = 1
            for d_ in out.shape:
                n *= d_
            self._record("dma", qname, None, reads, writes, 2500.0 + n * 4 / 150.0, 120.0, (out, in_, nobar))
            return None
        e = self.engs[qname]
        self._acq(e, reads, writes)
        if qname == "pool":
            s = self._sem("w%d" % len(self.dsems_sw))
            self.dsems_sw.append((s, nobar))
        elif nobar:
            s = self.dsems_nb[self.dnext_nb]
            self.dnext_nb = (self.dnext_nb + 1) % len(self.dsems_nb)
        else:
            s = self.dsems[self.dnext]
            self.dnext = (self.dnext + 1) % len(self.dsems)
        if s.count:
            e.wait((s, s.count, "dma"))
        inst = e.h.dma_start(out=out, in_=in_)
        s.count += 16
        inst.then_inc(s.h, 16)
        _SNAP[(id(s), s.count)] = dict(e.seen)
        tok = (s, s.count, "dma:" + s.name)
        self._rel("dma:" + s.name, tok, reads, writes)
        return tok


class T:
    def __init__(self, fw, name, shape, dt, view=None):
        self.t = fw.sb(name, shape, dt) if view is None else view
        self.r = Res(name)


class Arena:
    def __init__(self, t, nwords):
        self.t = t
        self.n = nwords
        self.o = 0

    def reset(self):
        self.o = 0

    def get(self, name, shape, dt):
        n = 1
        for d in shape[1:]:
            n *= d
        words = n if dt != BF16 else (n + 1) // 2
        assert self.o + words <= self.n, (name, self.o, words, self.n)
        v = self.t[:, self.o:self.o + words]
        self.o += words
        if dt != F32:
            v = v.bitcast(dt)
        if len(shape) == 3:
            v = v.rearrange("p (a b) -> p a b", b=shape[2])
        elif len(shape) == 4:
            v = v.rearrange("p (a b c) -> p a b c", b=shape[2], c=shape[3])
        return T(None, name, shape, dt, view=v)

    def __getitem__(self, k):
        return self.t[k]


def _consts():
    p = np.arange(128)
    cols = {}
    ident = np.eye(128, dtype=np.float32)
    cols["ident"] = ident
    k = p[:, None]
    q = p[None, :]
    mcur = np.where(k <= q, 0.0, NEG).astype(np.float32)
    mprev = np.where(k > q, 0.0, NEG).astype(np.float32)
    causal = (k <= q).astype(np.float32)
    global _CMASK
    _CMASK = np.ascontiguousarray(np.concatenate([mcur, mprev, causal], axis=1))
    cols["tri_in"] = causal * (-1.0 / 16.0)
    cols["tri_rev"] = (k > q).astype(np.float32) * (-1.0 / 16.0)
    cols["ncol"] = np.full((128, 2), -1.0 / 16.0, np.float32)
    h = np.arange(4, dtype=np.float32)
    log_g = np.log(1.0 - 2.0 ** (-5.0 - h)).astype(np.float32)
    i1 = (p[:, None] + 1).astype(np.float32)
    qdec = np.exp(log_g[None, :] * i1)
    kdec = np.exp(-log_g[None, :] * i1) / 8.0
    cols["dec8"] = np.concatenate([qdec, kdec], axis=1).astype(np.float32)
    cdec = np.exp(log_g * 128.0)
    sdec = np.zeros((128, 2), np.float32)
    for c in range(2):
        sdec[0:64, c] = cdec[2 * c]
        sdec[64:128, c] = cdec[2 * c + 1]
    cols["sdec"] = sdec
    bd64 = np.zeros((128, 128), np.float32)
    bd64[0:64, 0:64] = 1.0
    bd64[64:128, 64:128] = 1.0
    cols["bd64dec"] = np.concatenate([bd64 * sdec[:, 0:1], bd64 * sdec[:, 1:2]], axis=1)
    bd32 = np.zeros((128, 256), np.float32)
    for hh in range(4):
        bd32[32 * hh:32 * hh + 32, 64 * hh:64 * hh + 64] = 1.0
    cols["bd32"] = bd32
    gm = np.zeros((128, 2), np.float32)
    gm[0:16, 0] = 1.0
    gm[16, 1] = 1.0
    cols["gamask"] = gm
    fa = (10000.0 ** (-np.arange(0, 64, 2, dtype=np.float32) / 64.0)).astype(np.float32)
    fr = (1.0 / (10000.0 ** np.linspace(0.0, 1.0, 32, dtype=np.float32))).astype(np.float32)
    cols["freq"] = np.tile(np.concatenate([fa, fr])[None, :], (128, 1)).astype(np.float32)
    cols["neghalf"] = np.full((128, 8), -0.5, np.float32)
    cols["neghalf16"] = np.full((128, 16), -0.5, np.float32)
    off = {}
    o = 0
    parts = []
    for kname, v in cols.items():
        off[kname] = (o, v.shape[1])
        o += v.shape[1]
        parts.append(v.astype(np.float32))
    return np.ascontiguousarray(np.concatenate(parts, axis=1)), off


_CBLOB, _COFF = _consts()
NCONST = _CBLOB.shape[1]

_AQ = np.concatenate([np.arange(64 * hh, 64 * hh + 64) for hh in (0, 4, 1, 5, 2, 6, 3, 7)])
_r = lambda a, b: np.arange(a, b)
_PERM = np.concatenate([
    _r(3072, 3088),
    _AQ,
    _r(512, 640), _r(640, 768), _r(1280, 1536),
    _r(1536, 1792), _r(1792, 2048),
    _r(768, 1280),
    _r(2048, 2304), _r(2816, 3072),
    _r(2304, 2432), _r(2432, 2560), _r(2560, 2816),
])
NW = _PERM.shape[0]
_QK32 = np.concatenate([_r(1280, 1536), _r(2304, 2432), _r(1536, 1792), _r(2432, 2560)])
GOFF = 0
COFFS = [16 + 512 * i for i in range(6)]


def build(n_layers=2, n_tiles=NT, dbg=None, sched=True):
    nc = bass.Bass("TRN2", target_bir_lowering=False)
    _SNAP.clear()
    fw = FW(nc)
    if sched:
        fw.start_recording()
    L = n_layers

    def din(name, shape, dt=F32):
        return nc.dram_tensor(name, list(shape), dt, kind="ExternalInput").ap()

    x_d = din("x", [S, D])
    c_d = din("c", [128, 8])
    pos_d = din("pos", [128, NT], I32)
    wmod_d = din("w_mod", [L, D, 3 * D])
    bmodF_d = din("bmodF", [L, 128, 24])
    bgate_d = din("bgate", [L, D])
    pregF_d = din("pregF", [L, 128, 8])
    postg_d = din("postg", [L, D])
    win_d = din("w_in", [L, D, NW])
    sink_d = din("sinks", [L, 8])
    gwe_d = din("gwe", [L, 128, 128])
    ggain_d = din("ggain", [L, 64])
    wout_d = din("w_out", [L, D, D])
    wqk_d = din("w_qk32", [L, D, 768])
    const_d = din("consts", [128, NCONST])
    cmask_d = din("cmask", [128, 384])
    out_d = nc.dram_tensor("out", [S, D], F32, kind="ExternalOutput").ap()

    xs = fw.sb("xs", [128, NT, D], F32)
    R_x = [Res("x%d" % t) for t in range(NT)]
    win = fw.sb("win", [128, 8, NW], BF16)
    R_winc = [[Res("win%d_%d" % (kc, hf)) for hf in range(2)] for kc in range(8)]
    wout = fw.sb("wout", [128, 8, D], BF16)
    R_woutc = [Res("wout%d" % kc) for kc in range(8)]
    cst = T(fw, "cst", [128, NCONST], F32)

    def C(name):
        o, n = _COFF[name]
        return cst.t[:, o:o + n]

    identb = T(fw, "identb", [128, 128], BF16)
    mcurb = T(fw, "mcurb", [128, 512], BF16)
    mprevb = T(fw, "mprevb", [128, 512], BF16)
    gpg = T(fw, "gpg", [128, D], F32)
    gmF = T(fw, "gmF", [128, 8], F32)
    shF = T(fw, "shF", [128, 8], F32)
    esink2 = T(fw, "esink2", [128, 8], F32)
    gwe = T(fw, "gwe", [128, 128], F32)
    ggain = T(fw, "ggain", [128, 64], F32)
    tabs = T(fw, "tabs", [128, 4, NT, 32], F32)
    small = T(fw, "small", [128, 64], F32)
    retS = T(fw, "retS", [128, 256], F32)
    retSb = T(fw, "retSb", [128, 256], BF16)
    glaS = T(fw, "glaS", [128, 256], F32)
    glaSb = T(fw, "glaSb", [128, 256], BF16)
    big32_ = T(fw, "big32", [128, 512], F32)
    big32 = [big32_, big32_]
    xn = T(fw, "xn", [128, D], BF16)
    hT = T(fw, "hT", [128, 8, 128], BF16)
    gaTe = T(fw, "gaTe", [128, 128], F32)
    nl = T(fw, "nl", [128, 128], F32)
    Eq = T(fw, "Eq", [128, 128], F32)
    Ek = T(fw, "Ek", [128, 128], F32)
    Er = T(fw, "Er", [128, 128], F32)
    aqk = T(fw, "aqk", [128, 640], BF16)
    rqk32 = T(fw, "rqk32", [128, 512], F32)
    gqk = T(fw, "gqk", [128, 256], BF16)
    scT = T(fw, "scT", [128, 512], BF16)
    stat = T(fw, "stat", [128, 32], F32)
    statA = T(fw, "statA", [128, 8], F32)
    h0F = T(fw, "h0F", [128, 16], F32)
    e0r = T(fw, "e0r", [128, 2], F32)
    s00 = T(fw, "s00", [128, 8], F32)
    ssq = [T(fw, "ssq%d" % i, [128, NT], F32) for i in range(n_layers)]
    rstdT = [T(fw, "rstd%d" % i, [128, NT], F32) for i in range(n_layers)]
    stat1 = T(fw, "stat1", [128, 8], F32)
    stat2 = T(fw, "stat2", [128, 8], F32)
    stat3 = T(fw, "stat3", [128, 8], F32)
    causalb = T(fw, "causalb", [128, 512], BF16)
    kT = [T(fw, "kT%d" % i, [128, 128], BF16) for i in range(3)]
    vext = [T(fw, "vext%d" % i, [128, 2, 65], BF16) for i in range(3)]
    ebl = [T(fw, "ebl%d" % i, [128, 2], F32) for i in range(2)]
    rqk = [T(fw, "rqk%d" % i, [128, 512], BF16) for i in range(2)]
    gkh = [T(fw, "gkh%d" % i, [128, 128], BF16) for i in range(2)]
    qbd = [[T(fw, "qbd%d%d" % (i, g), [128, 4, 128], BF16) for g in range(2)] for i in range(2)]
    rqbd = [[T(fw, "rqbd%d%d" % (i, c), [128, 2, 128], BF16) for c in range(2)] for i in range(2)]
    rqT = [T(fw, "rqT%d" % i, [128, 2, 128], BF16) for i in range(2)]
    rkT = [T(fw, "rkT%d" % i, [128, 2, 128], BF16) for i in range(2)]
    gqbd = [T(fw, "gqbd%d" % i, [128, 4, 128], BF16) for i in range(2)]
    gqT = [T(fw, "gqT%d" % i, [128, 128], BF16) for i in range(2)]
    gkT = [T(fw, "gkT%d" % i, [128, 128], BF16) for i in range(2)]
    rv = [T(fw, "rv%d" % i, [128, 256], BF16) for i in range(2)]
    gv = [T(fw, "gv%d" % i, [128, 256], BF16) for i in range(2)]
    sg1 = T(fw, "sg1", [128, D], BF16)
    rotA2 = T(fw, "rotA2", [128, 256], F32)
    rotB2 = T(fw, "rotB2", [128, 256], F32)
    AW = 5120
    arena_t = fw.sb("arena", [128, AW], F32)
    arA = Arena(arena_t, AW)
    wst = [arA.get("wst%d" % i, [128, D], F32) for i in range(2)]
    cb = arA.get("cb", [128, 8, 128], F32)
    ang = arA.get("ang", [128, NT, 32], F32)
    tu = arA.get("tu", [128, NT, 32], F32)
    tki = arA.get("tki", [128, NT, 32], I32)
    ty = arA.get("ty", [128, NT, 32], F32)
    arA2 = Arena(arena_t, AW)
    arA2.o = 3072
    wst_extra = [arA2.get("wst%d" % i, [128, D], F32) for i in (2, 3)]
    cmk = T(None, "cmk", [128, 384], F32, view=rqk32.t[:, 0:384])
    cmk.r = rqk32.r
    arB = Arena(arena_t, AW)
    rotA1 = arB.get("rotA", [128, 640], F32)
    rotB1 = arB.get("rotB", [128, 640], F32)
    th = arB.get("th", [128, 512], BF16)
    sg0 = arB.get("sg", [128, D], BF16)
    sg = [sg0, sg1]
    PT = [[arB.get("PT%d%d" % (g, b), [128, 512], BF16) for b in range(2)] for g in range(2)]
    an = arB.get("an", [128, 256], F32)
    rraw = arB.get("rraw", [128, 256], F32)
    rsq = arB.get("rsq", [128, 256], F32)
    mix = arB.get("mix", [128, D], BF16)
    mixT = arB.get("mixT", [128, 8, 128], BF16)
    dS = arB.get("dS", [128, 256], F32)

    pp = [fw.ps("pp%d" % i, [128, 512], F32) for i in range(2)]
    R_pp = [Res("pp%d" % i, excl=True) for i in range(2)]
    ptA = fw.ps("ptA", [128, 1024], BF16)
    R_tA = Res("ptA", excl=True)
    ptB = fw.ps("ptB", [128, 1024], BF16)
    R_tB = Res("ptB", excl=True)
    psc = [fw.ps("psc%d" % i, [128, 512], F32) for i in range(2)]
    R_sc = [Res("psc%d" % i, excl=True) for i in range(2)]
    po = [fw.ps("po%d" % i, [128, 512], F32) for i in range(2)]
    R_o = [Res("po%d" % i, excl=True) for i in range(2)]

    dve = lambda fn, reads=(), writes=(): fw.op("dve", fn, reads, writes)
    act = lambda fn, reads=(), writes=(): fw.op("act", fn, reads, writes)
    pool = lambda fn, reads=(), writes=(): fw.op("pool", fn, reads, writes)
    dbg_outs = {}

    def dump(name, ap, R, n):
        if dbg is None:
            return
        o = nc.dram_tensor("dbg_" + name, [128, n], F32, kind="ExternalOutput").ap()
        for a0 in range(0, n, 512):
            a1 = min(n, a0 + 512)
            dve(lambda e: e.tensor_copy(out=big32_.t[:, 0:a1 - a0], in_=ap[:, a0:a1]), [R], [big32_.r])
            fw.dma("sp", o[:, a0:a1], big32_.t[:, 0:a1 - a0], reads=[big32_.r])

    def load_weights(l, early=False):
        for kc in range(8):
            for hf in range(2):
                c0 = hf * (NW // 2)
                fw.dma("pool", win[:, kc, c0:c0 + NW // 2],
                       win_d[l, kc * 128:(kc + 1) * 128, c0:c0 + NW // 2], writes=[R_winc[kc][hf]], nobar=early)
        for kc in range(8):
            fw.dma("pool", wout[:, kc, :], wout_d[l, kc * 128:(kc + 1) * 128, :], writes=[R_woutc[kc]], nobar=True)

    fw.dma("sp", cst.t[:], const_d, writes=[cst.r])
    fw.dma("sp", xs[:, 0, :], x_d[0:128, :], writes=[R_x[0]])
    load_weights(0)
    if n_tiles > 1:
        fw.dma("sp", xs[:, 1, :], x_d[128:256, :], writes=[R_x[1]])
    dve(lambda e: e.tensor_copy(out=identb.t[:], in_=C("ident")), [cst.r], [identb.r])
    fw.dma("sp", cmk.t[:], cmask_d, writes=[cmk.r])
    for i_, dst_ in enumerate((mcurb, mprevb, causalb)):
        dve(lambda e: e.tensor_copy(out=dst_.t[:].rearrange("p (r q) -> p r q", q=128),
                                    in_=cmk.t[:, 128 * i_:128 * (i_ + 1)].unsqueeze(1).broadcast_to([128, 4, 128])),
            [cmk.r], [dst_.r])
    for i_ in range(2):
        for g in range(2):
            pool(lambda e: e.memset(qbd[i_][g].t[:], 0.0), [], [qbd[i_][g].r])
            pool(lambda e: e.memset(rqbd[i_][g].t[:], 0.0), [], [rqbd[i_][g].r])
        pool(lambda e: e.memset(gqbd[i_].t[:], 0.0), [], [gqbd[i_].r])
    for i_ in range(3):
        pool(lambda e: e.memset(vext[i_].t[:], 1.0), [], [vext[i_].r])

    posi = T(fw, "posi", [128, NT], I32)
    posf = T(fw, "posf", [128, NT], F32)
    fw.dma("sp", posi.t[:], pos_d, writes=[posi.r])
    dve(lambda e: e.tensor_copy(out=posf.t[:], in_=posi.t[:]), [posi.r], [posf.r])
    TWO_PI = float(2 * np.pi)
    PI = float(np.pi)
    C1_2PI = float(np.float32(2 * np.pi))
    C2_2PI = float(np.float32(2 * np.pi - C1_2PI))

    def wrap(dst, src, shift):
        dve(lambda e: e.tensor_scalar(out=dst.t[:], in0=src.t[:], scalar1=float(shift), scalar2=None,
                                      op0=ALU.add), [src.r], [dst.r])
        dve(lambda e: e.tensor_scalar(out=tu.t[:], in0=dst.t[:], scalar1=PI, scalar2=-TWO_PI,
                                      op0=ALU.is_gt, op1=ALU.mult), [dst.r], [tu.r])
        dve(lambda e: e.tensor_tensor(out=dst.t[:], in0=dst.t[:], in1=tu.t[:], op=ALU.add),
            [dst.r, tu.r], [dst.r])
        dve(lambda e: e.tensor_scalar(out=tu.t[:], in0=dst.t[:], scalar1=-PI, scalar2=TWO_PI,
                                      op0=ALU.is_lt, op1=ALU.mult), [dst.r], [tu.r])
        dve(lambda e: e.tensor_tensor(out=dst.t[:], in0=dst.t[:], in1=tu.t[:], op=ALU.add),
            [dst.r, tu.r], [dst.r])

    fo, _ = _COFF["freq"]
    for which in range(2):
        fr_ap = cst.t[:, fo + 32 * which: fo + 32 * which + 32].unsqueeze(1).broadcast_to([128, NT, 32])
        pos_ap = posf.t[:].unsqueeze(2).broadcast_to([128, NT, 32])
        dve(lambda e: e.tensor_tensor(out=ang.t[:], in0=pos_ap, in1=fr_ap, op=ALU.mult),
            [posf.r, cst.r], [ang.r])
        dve(lambda e: e.tensor_scalar(out=tu.t[:], in0=ang.t[:], scalar1=float(1.0 / TWO_PI), scalar2=None,
                                      op0=ALU.mult), [ang.r], [tu.r])
        dve(lambda e: e.tensor_copy(out=tki.t[:], in_=tu.t[:]), [tu.r], [tki.r])
        dve(lambda e: e.tensor_copy(out=tu.t[:], in_=tki.t[:]), [tki.r], [tu.r])
        dve(lambda e: e.scalar_tensor_tensor(out=ty.t[:], in0=tu.t[:], scalar=-C1_2PI, in1=ang.t[:],
                                             op0=ALU.mult, op1=ALU.add), [tu.r, ang.r], [ty.r])
        dve(lambda e: e.scalar_tensor_tensor(out=ty.t[:], in0=tu.t[:], scalar=-C2_2PI, in1=ty.t[:],
                                             op0=ALU.mult, op1=ALU.add), [tu.r, ty.r], [ty.r])
        wrap(ang, ty, 0.0)
        act(lambda e: e.activation(out=tabs.t[:, 2 * which + 1, :, :], in_=ang.t[:], func=AF.Sin),
            [ang.r], [tabs.r])
        wrap(ty, ang, PI / 2)
        act(lambda e: e.activation(out=tabs.t[:, 2 * which, :, :], in_=ty.t[:], func=AF.Sin),
            [ty.r], [tabs.r])

    c32 = T(fw, "c32", [128, 8], F32)
    cth = T(fw, "cth", [128, 8], F32)
    fw.dma("sp", c32.t[:], c_d, writes=[c32.r])
    act(lambda e: e.activation(out=cth.t[:], in_=c32.t[:], func=AF.Tanh, scale=0.5), [c32.r], [cth.r])
    dve(lambda e: e.scalar_tensor_tensor(out=cth.t[:], in0=cth.t[:], scalar=1.0, in1=c32.t[:],
                                         op0=ALU.add, op1=ALU.mult), [cth.r, c32.r], [cth.r])
    dve(lambda e: e.tensor_scalar(out=cth.t[:], in0=cth.t[:], scalar1=0.5, scalar2=None, op0=ALU.mult),
        [cth.r], [cth.r])

    out_toks = []

    for l in range(L):
        bmodF = T(fw, "bmodF%d" % l, [128, 24], F32)
        pregF = T(fw, "pregF%d" % l, [128, 8], F32)
        fw.dma("sp", bmodF.t[:], bmodF_d[l], writes=[bmodF.r])
        fw.dma("sp", pregF.t[:], pregF_d[l], writes=[pregF.r])
        fw.dma("sp", gwe.t[:], gwe_d[l], writes=[gwe.r])
        fw.dma("sp", ggain.t[:], ggain_d[l].partition_broadcast(128), writes=[ggain.r])
        fw.dma("sp", esink2.t[:], sink_d[l].partition_broadcast(128), writes=[esink2.r])
        act(lambda e: e.activation(out=esink2.t[:], in_=esink2.t[:], func=AF.Exp), [esink2.r], [esink2.r])
        dve(lambda e: e.tensor_scalar(out=esink2.t[:], in0=esink2.t[:], scalar1=2.0, scalar2=None,
                                      op0=ALU.mult), [esink2.r], [esink2.r])
        dve(lambda e: e.tensor_copy(out=cb.t[:], in_=cth.t[:].unsqueeze(2).broadcast_to([128, 8, 128])),
            [cth.r], [cb.r])
        banks = [(pp[0], R_pp[0]), (pp[1], R_pp[1]), (psc[0], R_sc[0]), (psc[1], R_sc[1]),
                 (po[0], R_o[0]), (po[1], R_o[1])]
        i = 0
        for kc in range(8):
            for third in range(3):
                stl = wst if l == 0 else wst + wst_extra
                st = stl[i % len(stl)]
                i += 1
                fw.dma("sp", st.t[:], wmod_d[l, kc * 128:(kc + 1) * 128, third * 1024:(third + 1) * 1024],
                       writes=[st.r])
                for hf in range(2):
                    bk, rb = banks[third * 2 + hf]
                    fw.mm(lambda e: e.matmul(bk[:, :], lhsT=cb.t[:, kc, :], rhs=st.t[:, hf * 512:(hf + 1) * 512],
                                             start=(kc == 0), stop=(kc == 7)),
                          reads=[cb.r, st.r], writes=[rb], last=True)
        for which, dst in ((0, shF), (1, small)):
            for hf in range(2):
                bk, rb = banks[which * 2 + hf]
                dve(lambda e: e.tensor_tensor(
                    out=big32_.t[:].rearrange("p (k n) -> p k n", n=128),
                    in0=bk[:, :].rearrange("p (k n) -> p k n", n=128),
                    in1=C("ident").unsqueeze(1).broadcast_to([128, 4, 128]), op=ALU.mult),
                    [rb, cst.r], [big32_.r])
                dve(lambda e: e.tensor_reduce(out=dst.t[:, 4 * hf:4 * hf + 4],
                                              in_=big32_.t[:].rearrange("p (k n) -> p k n", n=128),
                                              op=ALU.add, axis=AX.X), [big32_.r], [dst.r])
        dve(lambda e: e.tensor_tensor(out=shF.t[:], in0=shF.t[:], in1=bmodF.t[:, 0:8], op=ALU.add),
            [shF.r, bmodF.r], [shF.r])
        dve(lambda e: e.tensor_tensor(out=small.t[:, 0:8], in0=small.t[:, 0:8], in1=bmodF.t[:, 8:16], op=ALU.add),
            [small.r, bmodF.r], [small.r])
        dve(lambda e: e.scalar_tensor_tensor(out=gmF.t[:], in0=small.t[:, 0:8], scalar=1.0, in1=pregF.t[:],
                                             op0=ALU.add, op1=ALU.mult), [small.r, pregF.r], [gmF.r])
        fw.dma("sp", gpg.t[:], bgate_d[l].partition_broadcast(128), writes=[gpg.r])
        for hf in range(2):
            bk, rb = banks[4 + hf]
            dve(lambda e: e.tensor_tensor(out=gpg.t[:, hf * 512:(hf + 1) * 512], in0=bk[:, :],
                                          in1=gpg.t[:, hf * 512:(hf + 1) * 512], op=ALU.add),
                [rb, gpg.r], [gpg.r])
        for hf in range(2):
            fw.dma("sp", rqk32.t[:], postg_d[l, hf * 512:(hf + 1) * 512].partition_broadcast(128), writes=[rqk32.r])
            dve(lambda e: e.tensor_tensor(out=gpg.t[:, hf * 512:(hf + 1) * 512], in0=gpg.t[:, hf * 512:(hf + 1) * 512],
                                          in1=rqk32.t[:], op=ALU.mult), [gpg.r, rqk32.r], [gpg.r])
        nst = min(2, n_tiles) if l == 0 else n_tiles
        if l == 0:
            for t_ in range(nst):
                act(lambda e: e.activation(out=xn.t[:], in_=xs[:, t_, :], func=AF.Square,
                                           accum_out=ssq[0].t[:, t_:t_ + 1]), [R_x[t_]], [xn.r, ssq[0].r])
        dve(lambda e: e.tensor_scalar(out=rstdT[l].t[:, 0:nst], in0=ssq[l].t[:, 0:nst], scalar1=1.0 / D,
                                      scalar2=EPS, op0=ALU.mult, op1=ALU.add), [ssq[l].r], [rstdT[l].r])
        pool(lambda e: e.tensor_tensor(out=rstdT[l].t[:, 0:nst], in0=rstdT[l].t[:, 0:nst],
                                       in1=C("neghalf16")[:, 0:nst], op=ALU.pow), [rstdT[l].r, cst.r], [rstdT[l].r])
        dve(lambda e: e.tensor_scalar(out=e0r.t[:], in0=C("ident")[:, 0:2], scalar1=rstdT[l].t[:, 0:1], scalar2=None,
                                      op0=ALU.mult), [cst.r, rstdT[l].r], [e0r.r])
        for kc in range(8):
            fw.mm(lambda e: e.matmul(po[1][:, 2 * kc:2 * kc + 2], lhsT=xs[:, 0, kc * 128:(kc + 1) * 128], rhs=e0r.t[:],
                                     start=True, stop=True),
                  reads=[R_x[0], e0r.r], writes=[R_o[1]], last=(kc == 7))
        dve(lambda e: e.tensor_tensor(out=h0F.t[:, 0:8], in0=po[1][:, 0:16].rearrange("p (k two) -> p k two", two=2)[:, :, 0],
                                      in1=gmF.t[:], op=ALU.mult), [R_o[1], gmF.r], [h0F.r])
        dve(lambda e: e.tensor_tensor(out=h0F.t[:, 0:8], in0=h0F.t[:, 0:8], in1=shF.t[:], op=ALU.add),
            [h0F.r, shF.r], [h0F.r])
        dve(lambda e: e.tensor_copy(out=cb.t[:], in_=h0F.t[:, 0:8].unsqueeze(2).broadcast_to([128, 8, 128])),
            [h0F.r], [cb.r])
        for kc in range(8):
            st = wst[kc % 2]
            fw.dma("sp", st.t[:, 0:768], wqk_d[l, kc * 128:(kc + 1) * 128, :], writes=[st.r])
            for hf in range(2):
                fw.mm(lambda e: e.matmul(pp[hf][:, 0:384], lhsT=cb.t[:, kc, :], rhs=st.t[:, hf * 384:(hf + 1) * 384],
                                         start=(kc == 0), stop=(kc == 7)),
                      reads=[cb.r, st.r], writes=[R_pp[hf]], last=True)
        act(lambda e: e.activation(out=big32_.t[:, 0:384], in_=pp[0][:, 0:384], func=AF.Copy), [R_pp[0]], [big32_.r])
        dve(lambda e: e.tensor_tensor(out=big32_.t[:, 0:384], in0=big32_.t[:, 0:384], in1=pp[1][:, 0:384], op=ALU.mult),
            [big32_.r, R_pp[1]], [big32_.r])
        dve(lambda e: e.tensor_reduce(out=s00.t[:, 0:4], in_=big32_.t[:, 0:256].rearrange("p (h d) -> p h d", d=64),
                                      axis=AX.X, op=ALU.add), [big32_.r], [s00.r])
        dve(lambda e: e.tensor_reduce(out=s00.t[:, 4:8], in_=big32_.t[:, 256:384].rearrange("p (h d) -> p h d", d=32),
                                      axis=AX.X, op=ALU.add), [big32_.r], [s00.r])
        dve(lambda e: e.tensor_scalar(out=s00.t[:, 0:4], in0=s00.t[:, 0:4], scalar1=0.125, scalar2=None, op0=ALU.mult),
            [s00.r], [s00.r])
        dve(lambda e: e.tensor_scalar(out=s00.t[:, 4:8], in0=s00.t[:, 4:8], scalar1=float(32 ** -0.5), scalar2=None,
                                      op0=ALU.mult), [s00.r], [s00.r])
        pool(lambda e: e.memset(retS.t[:], 0.0), [], [retS.r])
        pool(lambda e: e.memset(retSb.t[:], 0.0), [], [retSb.r])
        pool(lambda e: e.memset(glaS.t[:], 0.0), [], [glaS.r])
        pool(lambda e: e.memset(glaSb.t[:], 0.0), [], [glaSb.r])

        fw.barrier()
        def genA(t):
            p2 = t % 2
            p3 = t % 3
            xt = xs[:, t, :]
            Rx = R_x[t]
            AB = [(pp[0], R_pp[0]), (pp[1], R_pp[1]), (psc[1], R_sc[1])]
            GB = (psc[0], R_sc[0])
            dve(lambda e: e.tensor_scalar(out=xn.t[:], in0=xt, scalar1=rstdT[l].t[:, t:t + 1], scalar2=None,
                                          op0=ALU.mult), [Rx, rstdT[l].r], [xn.r])
            for kc in range(8):
                fw.mm(lambda e: e.transpose(out=ptA[:, kc * 128:(kc + 1) * 128], in_=xn.t[:, kc * 128:(kc + 1) * 128],
                                            identity=identb.t[:]),
                      reads=[xn.r, identb.r], writes=[R_tA], last=(kc == 7))
            if l == 0 and t + 2 < n_tiles:
                t2 = t + 2
                act(lambda e: e.activation(out=xn.t[:], in_=xs[:, t2, :], func=AF.Square,
                                           accum_out=ssq[0].t[:, t2:t2 + 1]), [R_x[t2]], [xn.r, ssq[0].r])
                dve(lambda e: e.tensor_scalar(out=rstdT[0].t[:, t2:t2 + 1], in0=ssq[0].t[:, t2:t2 + 1], scalar1=1.0 / D,
                                              scalar2=EPS, op0=ALU.mult, op1=ALU.add), [ssq[0].r], [rstdT[0].r])
                pool(lambda e: e.tensor_tensor(out=rstdT[0].t[:, t2:t2 + 1], in0=rstdT[0].t[:, t2:t2 + 1],
                                               in1=C("neghalf")[:, 0:1], op=ALU.pow), [rstdT[0].r, cst.r], [rstdT[0].r])
            for kc in range(8):
                if kc < 4:
                    act(lambda e: e.activation(out=hT.t[:, kc, :], in_=ptA[:, kc * 128:(kc + 1) * 128],
                                               func=AF.Identity, scale=gmF.t[:, kc:kc + 1], bias=shF.t[:, kc:kc + 1]),
                        [R_tA, gmF.r, shF.r], [hT.r])
                else:
                    dve(lambda e: e.tensor_scalar(out=hT.t[:, kc, :], in0=ptA[:, kc * 128:(kc + 1) * 128],
                                                  scalar1=gmF.t[:, kc:kc + 1], scalar2=shF.t[:, kc:kc + 1],
                                                  op0=ALU.mult, op1=ALU.add),
                        [R_tA, gmF.r, shF.r], [hT.r])
            yield

            def proj(ci, bank):
                for kc in range(8):
                    fw.mm(lambda e: e.matmul(AB[bank][0][:, :], lhsT=hT.t[:, kc, :],
                                             rhs=win[:, kc, COFFS[ci]:COFFS[ci] + 512],
                                             start=(kc == 0), stop=(kc == 7)),
                          reads=[hT.r] + R_winc[kc], writes=[AB[bank][1]], last=(kc == 7))

            cA = tabs.t[:, 0, t, :]
            sA = tabs.t[:, 1, t, :]
            cR = tabs.t[:, 2, t, :]
            sR = tabs.t[:, 3, t, :]

            def rotary(src, R_src, nh, cos, sin, dst, dst_off, R_dst, final_eng):
                n = nh * 64
                rotA, rotB = (rotA1, rotB1) if nh == 8 else (rotA2, rotB2)
                s4 = src.rearrange("p (h two f) -> p h two f", two=2, f=32)
                a4 = rotA.t[:, 0:n].rearrange("p (h two f) -> p h two f", two=2, f=32)
                b4 = rotB.t[:, 0:n].rearrange("p (h two f) -> p h two f", two=2, f=32)
                d4 = dst[:, dst_off:dst_off + n].rearrange("p (h two f) -> p h two f", two=2, f=32)
                cos4 = cos.unsqueeze(1).unsqueeze(1).broadcast_to([128, nh, 2, 32])
                sin3 = sin.unsqueeze(1).broadcast_to([128, nh, 32])
                dve(lambda e: e.tensor_tensor(out=a4, in0=s4, in1=cos4, op=ALU.mult),
                    [R_src, tabs.r], [rotA.r])
                dve(lambda e: e.scalar_tensor_tensor(out=b4[:, :, 0, :], in0=s4[:, :, 1, :], scalar=-1.0, in1=sin3,
                                                     op0=ALU.mult, op1=ALU.mult), [R_src, tabs.r], [rotB.r])
                dve(lambda e: e.tensor_tensor(out=b4[:, :, 1, :], in0=s4[:, :, 0, :], in1=sin3, op=ALU.mult),
                    [R_src, tabs.r], [rotB.r])
                fw.op(final_eng, lambda e: e.tensor_tensor(out=d4, in0=a4, in1=b4, op=ALU.add),
                      [rotA.r, rotB.r], [R_dst])

            for kc in range(8):
                fw.mm(lambda e: e.matmul(GB[0][:, 0:128], lhsT=win[:, kc, 0:128], rhs=hT.t[:, kc, :],
                                         start=(kc == 0), stop=(kc == 7)),
                      reads=[hT.r] + R_winc[kc], writes=[GB[1]], last=(kc == 7))
            gmk = C("gamask")
            act(lambda e: e.activation(out=gaTe.t[:], in_=GB[0][:, 0:128], func=AF.Identity,
                                       scale=gmk[:, 0:1], bias=gmk[:, 1:2]), [GB[1], cst.r], [gaTe.r])
            proj(0, 0)
            fw.mm(lambda e: e.matmul(GB[0][:, 128:256], lhsT=gaTe.t[:], rhs=gwe.t[:], start=True, stop=True),
                  reads=[gaTe.r, gwe.r], writes=[GB[1]], last=True)
            act(lambda e: e.activation(out=nl.t[:], in_=GB[0][:, 128:256], func=AF.Exp, scale=-1.0),
                [GB[1]], [nl.r])
            act(lambda e: e.activation(out=nl.t[:], in_=nl.t[:], func=AF.Ln, bias=1.0), [nl.r], [nl.r])
            rotary(AB[0][0][:, 0:512], AB[0][1], 8, cA, sA, aqk.t, 0, aqk.r, "pool")
            yield
            proj(1, 1)
            rotary(AB[1][0][:, 0:128], AB[1][1], 2, cA, sA, aqk.t, 512, aqk.r, "pool")
            act(lambda e: e.activation(out=vext[p3].t[:, :, 0:64],
                                       in_=AB[1][0][:, 128:256].rearrange("p (g d) -> p g d", d=64), func=AF.Copy),
                [AB[1][1]], [vext[p3].r])
            rotary(AB[1][0][:, 256:512], AB[1][1], 4, cR, sR, rqk32.t, 0, rqk32.r, "pool")
            yield
            proj(2, 2)
            rotary(AB[2][0][:, 0:256], AB[2][1], 4, cR, sR, rqk32.t, 256, rqk32.r, "pool")
            act(lambda e: e.activation(out=rv[p2].t[:], in_=AB[2][0][:, 256:512], func=AF.Copy), [AB[2][1]], [rv[p2].r])
            pool(lambda e: e.tensor_tensor(out=rqk[p2].t[:].rearrange("p (h d) -> p h d", d=64),
                                           in0=rqk32.t[:].rearrange("p (h d) -> p h d", d=64),
                                           in1=C("dec8").unsqueeze(2).broadcast_to([128, 8, 64]), op=ALU.mult),
                 [rqk32.r, cst.r], [rqk[p2].r])
            yield
            for ci, bank in ((3, 0), (4, 1)):
                proj(ci, bank)
                o = (ci - 3) * 512
                act(lambda e: e.activation(out=th.t[:], in_=AB[bank][0][:, :], func=AF.Tanh, scale=0.5),
                    [AB[bank][1]], [th.r])
                dve(lambda e: e.scalar_tensor_tensor(out=sg[p2].t[:, o:o + 512], in0=th.t[:], scalar=1.0,
                                                     in1=AB[bank][0][:, :], op0=ALU.add, op1=ALU.mult),
                    [th.r, AB[bank][1]], [sg[p2].r])
                yield
            fw.mm(lambda e: e.matmul(GB[0][:, 128:256], lhsT=C("tri_in"), rhs=nl.t[:], start=True, stop=True),
                  reads=[cst.r, nl.r], writes=[GB[1]], last=False)
            fw.mm(lambda e: e.matmul(GB[0][:, 256:384], lhsT=C("tri_rev"), rhs=nl.t[:], start=True, stop=True),
                  reads=[cst.r, nl.r], writes=[GB[1]], last=False)
            fw.mm(lambda e: e.matmul(GB[0][:, 384:386], lhsT=nl.t[:], rhs=C("ncol"), start=True, stop=True),
                  reads=[cst.r, nl.r], writes=[GB[1]], last=True)
            act(lambda e: e.activation(out=Eq.t[:], in_=GB[0][:, 128:256], func=AF.Exp), [GB[1]], [Eq.r])
            act(lambda e: e.activation(out=Ek.t[:], in_=GB[0][:, 128:256], func=AF.Exp, scale=-1.0),
                [GB[1]], [Ek.r])
            act(lambda e: e.activation(out=Er.t[:], in_=GB[0][:, 256:384], func=AF.Exp), [GB[1]], [Er.r])
            act(lambda e: e.activation(out=ebl[p2].t[:], in_=GB[0][:, 384:386], func=AF.Exp), [GB[1]], [ebl[p2].r])
            proj(5, 2)
            dve(lambda e: e.scalar_tensor_tensor(out=gqk.t[:, 0:128], in0=AB[2][0][:, 0:128], scalar=float(32 ** -0.5),
                                                 in1=Eq.t[:], op0=ALU.mult, op1=ALU.mult),
                [AB[2][1], Eq.r], [gqk.r])
            dve(lambda e: e.tensor_tensor(out=gqk.t[:, 128:256], in0=AB[2][0][:, 128:256], in1=Ek.t[:], op=ALU.mult),
                [AB[2][1], Ek.r], [gqk.r])
            dve(lambda e: e.tensor_tensor(out=gkh[p2].t[:], in0=AB[2][0][:, 128:256], in1=Er.t[:], op=ALU.mult),
                [AB[2][1], Er.r], [gkh[p2].r])
            act(lambda e: e.activation(out=gv[p2].t[:], in_=AB[2][0][:, 256:512], func=AF.Copy), [AB[2][1]], [gv[p2].r])
            yield
            for j in range(5):
                fw.mm(lambda e: e.transpose(out=ptB[:, j * 128:(j + 1) * 128], in_=aqk.t[:, j * 128:(j + 1) * 128],
                                            identity=identb.t[:]),
                      reads=[aqk.r, identb.r], writes=[R_tB], last=(j == 4))
            src4 = ptB[:, 0:512].rearrange("p (c q) -> p c q", q=128)
            act(lambda e: e.activation(out=qbd[p2][0].t[0:64, :, :], in_=src4[0:64, :, :], func=AF.Copy),
                [R_tB], [qbd[p2][0].r])
            act(lambda e: e.activation(out=kT[p3].t[:], in_=ptB[:, 512:640], func=AF.Copy), [R_tB], [kT[p3].r])
            dve(lambda e: e.tensor_copy(out=qbd[p2][1].t[64:128, :, :], in_=src4[64:128, :, :]), [R_tB], [qbd[p2][1].r])
            yield
            for j in range(4):
                fw.mm(lambda e: e.transpose(out=ptB[:, j * 128:(j + 1) * 128], in_=rqk[p2].t[:, j * 128:(j + 1) * 128],
                                            identity=identb.t[:]),
                      reads=[rqk[p2].r, identb.r], writes=[R_tB], last=False)
            for j in range(2):
                fw.mm(lambda e: e.transpose(out=ptB[:, (4 + j) * 128:(5 + j) * 128],
                                            in_=gqk.t[:, j * 128:(j + 1) * 128], identity=identb.t[:]),
                      reads=[gqk.r, identb.r], writes=[R_tB], last=(j == 1))
            for c in range(2):
                act(lambda e: e.activation(out=rqbd[p2][c].t[0:64, 0, :], in_=ptB[0:64, c * 128:(c + 1) * 128],
                                           func=AF.Copy), [R_tB], [rqbd[p2][c].r])
            act(lambda e: e.activation(out=rkT[p2].t[:].rearrange("p c q -> p (c q)"), in_=ptB[:, 256:512],
                                       func=AF.Copy), [R_tB], [rkT[p2].r])
            act(lambda e: e.activation(out=gqT[p2].t[:], in_=ptB[:, 512:640], func=AF.Copy), [R_tB], [gqT[p2].r])
            for hh in (0, 2):
                act(lambda e: e.activation(out=gqbd[p2].t[32 * hh:32 * hh + 32, hh, :],
                                           in_=ptB[32 * hh:32 * hh + 32, 512:640], func=AF.Copy),
                    [R_tB], [gqbd[p2].r])
            for c in range(2):
                dve(lambda e: e.tensor_copy(out=rqbd[p2][c].t[64:128, 1, :], in_=ptB[64:128, c * 128:(c + 1) * 128]),
                    [R_tB], [rqbd[p2][c].r])
            dve(lambda e: e.tensor_copy(out=rqT[p2].t[:].rearrange("p c q -> p (c q)"), in_=ptB[:, 0:256]),
                [R_tB], [rqT[p2].r])
            for hh in (1, 3):
                dve(lambda e: e.tensor_copy(out=gqbd[p2].t[32 * hh:32 * hh + 32, hh, :],
                                            in_=ptB[32 * hh:32 * hh + 32, 512:640]), [R_tB], [gqbd[p2].r])
            dve(lambda e: e.tensor_copy(out=gkT[p2].t[:], in_=ptB[:, 640:768]), [R_tB], [gkT[p2].r])
            yield

        def genB(t):
            p2 = t % 2
            p3 = t % 3
            pv3 = (t - 1) % 3
            xt = xs[:, t, :]
            Rx = R_x[t]
            blocks = [(p3, mcurb)] + ([(pv3, mprevb)] if t > 0 else [])
            for g in range(2):
                for bi, (kp, mk) in enumerate(blocks):
                    bank = 0
                    fw.mm(lambda e: e.matmul(psc[bank][:, :], lhsT=kT[kp].t[:],
                                             rhs=qbd[p2][g].t[:].rearrange("p c q -> p (c q)"), start=True, stop=False),
                          reads=[kT[kp].r, qbd[p2][g].r], writes=[R_sc[bank]], last=False)
                    fw.mm(lambda e: e.matmul(psc[bank][:, :], lhsT=identb.t[:], rhs=mk.t[:], start=False, stop=True),
                          reads=[identb.r, mk.r], writes=[R_sc[bank]], last=True)
                    act(lambda e: e.activation(out=PT[g][bi].t[:], in_=psc[bank][:, :], func=AF.Exp, scale=0.125),
                        [R_sc[bank]], [PT[g][bi].r])
                yield
            for g in range(2):
                for c in range(4):
                    for bi, (kp, mk) in enumerate(blocks):
                        fw.mm(lambda e: e.matmul(po[g][:, c * 65:(c + 1) * 65], lhsT=PT[g][bi].t[:, c * 128:(c + 1) * 128],
                                                 rhs=vext[kp].t[:, g, :], start=(bi == 0), stop=(bi == len(blocks) - 1)),
                              reads=[PT[g][bi].r, vext[kp].r], writes=[R_o[g]],
                              last=(c == 3 and bi == len(blocks) - 1))
                o4 = po[g][:, 0:260].rearrange("p (c d) -> p c d", d=65)
                dve(lambda e: e.scalar_tensor_tensor(out=stat1.t[:, 0:4], in0=o4[:, :, 64], scalar=2.0,
                                                     in1=esink2.t[:, 4 * g:4 * g + 4], op0=ALU.mult, op1=ALU.add),
                    [R_o[g], esink2.r], [stat1.r])
                dve(lambda e: e.reciprocal(out=stat1.t[:, 4:8], in_=stat1.t[:, 0:4]), [stat1.r], [stat1.r])
                dve(lambda e: e.tensor_tensor(out=an.t[:].rearrange("p (c d) -> p c d", d=64), in0=o4[:, :, 0:64],
                                              in1=stat1.t[:, 4:8].unsqueeze(2).broadcast_to([128, 4, 64]),
                                              op=ALU.mult), [R_o[g], stat1.r], [an.r])
                pool(lambda e: e.tensor_tensor(out=mix.t[:, g * 256:(g + 1) * 256], in0=an.t[:],
                                               in1=sg[p2].t[:, g * 256:(g + 1) * 256], op=ALU.mult),
                     [an.r, sg[p2].r], [mix.r])
                yield

            def head_norm(bank, R_bank, off, gain, rraw, rsqt, rsq_r, stat2):
                act(lambda e: e.activation(out=rraw.t[:], in_=bank[:, 0:256], func=AF.Copy), [R_bank], [rraw.r])
                act(lambda e: e.activation(out=rsqt, in_=bank[:, 0:256], func=AF.Square), [R_bank], [rsq_r])
                dve(lambda e: e.tensor_reduce(out=stat2.t[:, 0:4], in_=rsqt.rearrange("p (h d) -> p h d", d=64),
                                              axis=AX.X, op=ALU.add), [rsq_r], [stat2.r])
                dve(lambda e: e.tensor_scalar(out=stat2.t[:, 0:4], in0=stat2.t[:, 0:4], scalar1=4.0 / 64.0,
                                              scalar2=4.0 * EPS, op0=ALU.mult, op1=ALU.add), [stat2.r], [stat2.r])
                pool(lambda e: e.tensor_tensor(out=stat2.t[:, 4:8], in0=stat2.t[:, 0:4], in1=C("neghalf")[:, 0:4],
                                               op=ALU.pow), [stat2.r, cst.r], [stat2.r])
                dve(lambda e: e.tensor_tensor(out=rraw.t[:].rearrange("p (h d) -> p h d", d=64),
                                              in0=rraw.t[:].rearrange("p (h d) -> p h d", d=64),
                                              in1=stat2.t[:, 4:8].unsqueeze(2).broadcast_to([128, 4, 64]), op=ALU.mult),
                    [rraw.r, stat2.r], [rraw.r])
                if gain is not None:
                    pool(lambda e: e.tensor_tensor(out=rraw.t[:].rearrange("p (h d) -> p h d", d=64),
                                                   in0=rraw.t[:].rearrange("p (h d) -> p h d", d=64),
                                                   in1=gain.t[:].unsqueeze(1).broadcast_to([128, 4, 64]), op=ALU.mult),
                         [rraw.r, gain.r], [rraw.r])
                pool(lambda e: e.tensor_tensor(out=mix.t[:, off:off + 256], in0=rraw.t[:], in1=sg[p2].t[:, off:off + 256],
                                               op=ALU.mult), [rraw.r, sg[p2].r], [mix.r])

            for c in range(2):
                fw.mm(lambda e: e.matmul(psc[0][:, c * 256:(c + 1) * 256], lhsT=rkT[p2].t[:, c, :],
                                         rhs=rqbd[p2][c].t[:].rearrange("p h q -> p (h q)"), start=True, stop=True),
                      reads=[rkT[p2].r, rqbd[p2][c].r], writes=[R_sc[0]], last=(c == 1))
            dve(lambda e: e.tensor_tensor(out=scT.t[:], in0=psc[0][:, :], in1=causalb.t[:], op=ALU.mult),
                [R_sc[0], causalb.r], [scT.r])
            if t == 0:
                dve(lambda e: e.tensor_copy(out=scT.t[0:1, :].rearrange("p (h i) -> p h i", i=128)[:, :, 0],
                                            in_=s00.t[0:1, 0:4]), [scT.r, s00.r], [scT.r])
            yield
            for hh in range(4):
                fw.mm(lambda e: e.matmul(po[0][:, hh * 64:(hh + 1) * 64], lhsT=scT.t[:, hh * 128:(hh + 1) * 128],
                                         rhs=rv[p2].t[:, hh * 64:(hh + 1) * 64], start=(hh == 0), stop=False),
                      reads=[scT.r, rv[p2].r], writes=[R_o[0]], last=False)
            for c in range(2):
                fw.mm(lambda e: e.matmul(po[0][:, c * 128:(c + 1) * 128], lhsT=rqT[p2].t[:, c, :],
                                         rhs=retSb.t[:, c * 128:(c + 1) * 128], start=False, stop=(c == 1)),
                      reads=[rqT[p2].r, retSb.r], writes=[R_o[0]], last=False)
            for c in range(2):
                fw.mm(lambda e: e.matmul(po[0][:, 256 + c * 128:256 + (c + 1) * 128],
                                         lhsT=rqk[p2].t[:, 256 + c * 128:256 + (c + 1) * 128],
                                         rhs=rv[p2].t[:, c * 128:(c + 1) * 128], start=True, stop=True),
                      reads=[rqk[p2].r, rv[p2].r], writes=[R_o[0]], last=(c == 1))
            head_norm(po[0], R_o[0], 512, None, rraw, rsq.t[:], rsq.r, stat2)
            dve(lambda e: e.tensor_tensor(out=dS.t[:], in0=po[0][:, 256:512], in1=C("bd64dec"), op=ALU.mult),
                [R_o[0], cst.r], [dS.r])
            pool(lambda e: e.tensor_tensor(out=retS.t[:].rearrange("p (c n) -> p c n", n=128),
                                           in0=retS.t[:].rearrange("p (c n) -> p c n", n=128),
                                           in1=C("sdec").unsqueeze(2).broadcast_to([128, 2, 128]), op=ALU.mult),
                 [retS.r, cst.r], [retS.r])
            pool(lambda e: e.tensor_tensor(out=retS.t[:], in0=retS.t[:], in1=dS.t[:], op=ALU.add),
                 [retS.r, dS.r], [retS.r])
            act(lambda e: e.activation(out=retSb.t[:], in_=retS.t[:], func=AF.Copy), [retS.r], [retSb.r])
            yield

            fw.mm(lambda e: e.matmul(psc[0][:, :], lhsT=gkT[p2].t[:], rhs=gqbd[p2].t[:].rearrange("p h q -> p (h q)"),
                                     start=True, stop=True),
                  reads=[gkT[p2].r, gqbd[p2].r], writes=[R_sc[0]], last=True)
            dve(lambda e: e.tensor_tensor(out=scT.t[:], in0=psc[0][:, :], in1=causalb.t[:], op=ALU.mult),
                [R_sc[0], causalb.r], [scT.r])
            if t == 0:
                dve(lambda e: e.tensor_copy(out=scT.t[0:1, :].rearrange("p (h i) -> p h i", i=128)[:, :, 0],
                                            in_=s00.t[0:1, 4:8]), [scT.r, s00.r], [scT.r])
            yield
            for hh in range(4):
                fw.mm(lambda e: e.matmul(po[1][:, hh * 64:(hh + 1) * 64], lhsT=scT.t[:, hh * 128:(hh + 1) * 128],
                                         rhs=gv[p2].t[:, hh * 64:(hh + 1) * 64], start=(hh == 0), stop=False),
                      reads=[scT.r, gv[p2].r], writes=[R_o[1]], last=False)
            fw.mm(lambda e: e.matmul(po[1][:, 0:256], lhsT=gqT[p2].t[:], rhs=glaSb.t[:], start=False, stop=True),
                  reads=[gqT[p2].r, glaSb.r], writes=[R_o[1]], last=False)
            fw.mm(lambda e: e.matmul(po[1][:, 256:512], lhsT=gkh[p2].t[:], rhs=gv[p2].t[:], start=True, stop=True),
                  reads=[gkh[p2].r, gv[p2].r], writes=[R_o[1]], last=True)
            head_norm(po[1], R_o[1], 768, ggain, an, big32_.t[:, 0:256], big32_.r, stat3)
            dve(lambda e: e.tensor_tensor(out=dS.t[:], in0=po[1][:, 256:512], in1=C("bd32"), op=ALU.mult),
                [R_o[1], cst.r], [dS.r])
            dve(lambda e: e.scalar_tensor_tensor(out=glaS.t[:], in0=glaS.t[:], scalar=ebl[p2].t[:, 0:1], in1=dS.t[:],
                                                 op0=ALU.mult, op1=ALU.add), [glaS.r, ebl[p2].r, dS.r], [glaS.r])
            act(lambda e: e.activation(out=glaSb.t[:], in_=glaS.t[:], func=AF.Copy), [glaS.r], [glaSb.r])
            yield

            if dbg is not None and l == 0 and t in dbg:
                dump("hT_%d" % t, hT.t[:].rearrange("p k q -> p (k q)"), hT.r, 1024)
                dump("aqk_%d" % t, aqk.t[:], aqk.r, 640)
                dump("rqk_%d" % t, rqk[p2].t[:], rqk[p2].r, 512)
                dump("gqk_%d" % t, gqk.t[:], gqk.r, 256)
                dump("gkh_%d" % t, gkh[p2].t[:], gkh[p2].r, 128)
                dump("sg_%d" % t, sg[p2].t[:], sg[p2].r, 1024)
                dump("nl_%d" % t, nl.t[:], nl.r, 128)
                dump("mix_%d" % t, mix.t[:], mix.r, 1024)
                dump("retS_%d" % t, retS.t[:], retS.r, 256)
                dump("glaS_%d" % t, glaS.t[:], glaS.r, 256)
            for j in range(8):
                fw.mm(lambda e: e.transpose(out=ptB[:, j * 128:(j + 1) * 128], in_=mix.t[:, j * 128:(j + 1) * 128],
                                            identity=identb.t[:]),
                      reads=[mix.r, identb.r], writes=[R_tB], last=(j == 7))
            act(lambda e: e.activation(out=mixT.t[:, 0:4, :].rearrange("p k q -> p (k q)"), in_=ptB[:, 0:512],
                                       func=AF.Copy), [R_tB], [mixT.r])
            dve(lambda e: e.tensor_copy(out=mixT.t[:, 4:8, :].rearrange("p k q -> p (k q)"), in_=ptB[:, 512:1024]),
                [R_tB], [mixT.r])
            yield
            for hf in range(2):
                for kc in range(8):
                    fw.mm(lambda e: e.matmul(po[hf][:, :], lhsT=mixT.t[:, kc, :],
                                             rhs=wout[:, kc, hf * 512:(hf + 1) * 512], start=(kc == 0), stop=(kc == 7)),
                          reads=[mixT.r, R_woutc[kc]], writes=[R_o[hf]], last=(kc == 7))
                yield
            for hf in range(2):
                act(lambda e: e.activation(out=mix.t[:, hf * 512:(hf + 1) * 512], in_=po[hf][:, :], func=AF.Square,
                                           accum_out=stat.t[:, 16 + hf:17 + hf]), [R_o[hf]], [mix.r, stat.r])
            dve(lambda e: e.tensor_tensor(out=stat.t[:, 18:19], in0=stat.t[:, 16:17], in1=stat.t[:, 17:18], op=ALU.add),
                [stat.r], [stat.r])
            dve(lambda e: e.tensor_scalar(out=stat.t[:, 19:20], in0=stat.t[:, 18:19], scalar1=1.0 / D, scalar2=EPS,
                                          op0=ALU.mult, op1=ALU.add), [stat.r], [stat.r])
            pool(lambda e: e.tensor_tensor(out=stat.t[:, 20:21], in0=stat.t[:, 19:20], in1=C("neghalf")[:, 0:1],
                                           op=ALU.pow), [stat.r, cst.r], [stat.r])
            for hf in range(2):
                dve(lambda e: e.scalar_tensor_tensor(out=big32[hf].t[:], in0=po[hf][:, :],
                                                     scalar=stat.t[:, 20:21], in1=gpg.t[:, hf * 512:(hf + 1) * 512],
                                                     op0=ALU.mult, op1=ALU.mult),
                    [R_o[hf], stat.r, gpg.r], [big32[hf].r])
                dve(lambda e: e.tensor_tensor(out=xt[:, hf * 512:(hf + 1) * 512], in0=xt[:, hf * 512:(hf + 1) * 512],
                                              in1=big32[hf].t[:], op=ALU.add), [Rx, big32[hf].r], [Rx])
            if l == L - 1:
                out_toks.append(fw.dma("sp", out_d[t * 128:(t + 1) * 128, :], xt, reads=[Rx]))
            else:
                act(lambda e: e.activation(out=mix.t[:], in_=xt, func=AF.Square, accum_out=ssq[l + 1].t[:, t:t + 1]),
                    [Rx], [mix.r, ssq[l + 1].r])
            yield

        if l == 0:
            for t_ in range(2, n_tiles):
                fw.dma("sp", xs[:, t_, :], x_d[t_ * 128:(t_ + 1) * 128, :], writes=[R_x[t_]], nobar=True)
        for _ in genA(0):
            pass
        for t in range(n_tiles):
            gb = genB(t)
            ga_ = genA(t + 1) if t + 1 < n_tiles else iter(())
            alive_a = alive_b = True
            while alive_a or alive_b:
                if alive_b:
                    try:
                        next(gb)
                    except StopIteration:
                        alive_b = False
                if alive_a:
                    try:
                        next(ga_)
                    except StopIteration:
                        alive_a = False
        if l + 1 < L:
            load_weights(l + 1, early=True)
        if dbg is not None and l == 0:
            dump("gmF", gmF.t[:], gmF.r, 8)
            dump("shF", shF.t[:], shF.r, 8)
            dump("gpg", gpg.t[:], gpg.r, 1024)
            dump("tabs", tabs.t[:].rearrange("p a t f -> p (a t f)"), tabs.r, 4 * NT * 32)
        fw.barrier()

    fw.flush()
    e = fw.engs["sp"]
    for s in fw.dsems + fw.dsems_nb + [x[0] for x in fw.dsems_sw]:
        if s.count:
            e.wait((s, s.count, "dma"))
    fw.close()
    return nc


_NC_CACHE = {}


def _prep_shared(w_mod, b_mod, pre_norm_gain, post_norm_gain, w_in, attn_sinks, gla_gate_w, gla_gate_b,
                 gla_norm_gain, w_out):
    L = w_in.shape[0]
    f = lambda a: np.ascontiguousarray(np.asarray(a, dtype=np.float32))
    b_mod = np.asarray(b_mod, np.float32)
    gwe = np.zeros((L, 128, 128), np.float32)
    gwe[:, 0:16, :] = np.asarray(gla_gate_w, np.float32)
    gwe[:, 16, :] = np.asarray(gla_gate_b, np.float32)
    return {
        "w_mod": f(w_mod),
        "bmodF": f(b_mod.reshape(L, 24, 128).transpose(0, 2, 1)),
        "bgate": f(b_mod[:, 2048:3072]),
        "pregF": f(np.asarray(pre_norm_gain, np.float32).reshape(L, 8, 128).transpose(0, 2, 1)),
        "postg": f(post_norm_gain),
        "w_in": f(np.asarray(w_in, np.float32)[:, :, _PERM]),
        "w_qk32": f(np.asarray(w_in, np.float32)[:, :, _QK32]),
        "sinks": f(attn_sinks),
        "gwe": gwe,
        "ggain": f(gla_norm_gain),
        "w_out": f(w_out),
        "consts": _CBLOB,
        "cmask": _CMASK,
    }


def kernel(x, c, positions, w_mod, b_mod, pre_norm_gain, post_norm_gain, w_in,
           attn_sinks, gla_gate_w, gla_gate_b, gla_norm_gain, w_out):
    x = np.asarray(x, np.float32)
    c = np.asarray(c, np.float32)
    positions = np.asarray(positions, np.int32)
    B = x.shape[0]
    shared = _prep_shared(w_mod, b_mod, pre_norm_gain, post_norm_gain, w_in, attn_sinks, gla_gate_w,
                          gla_gate_b, gla_norm_gain, w_out)
    if "nc" not in _NC_CACHE:
        _NC_CACHE["nc"] = build(n_layers=2)
    nc = _NC_CACHE["nc"]
    in_maps = []
    for b in range(B):
        m = dict(shared)
        m["x"] = np.ascontiguousarray(x[b])
        m["c"] = np.ascontiguousarray(c[b].reshape(8, 128).T)
        m["pos"] = np.ascontiguousarray(positions[b].reshape(NT, 128).T)
        in_maps.append(m)
    res = run_bass_kernel_spmd(nc, in_maps, core_ids=list(range(B)))
    return np.stack([np.asarray(r["out"], np.float32) for r in res.results], axis=0)
```

```python
import os
import numpy as np
import concourse.bass as bass
import concourse.mybir as mybir
from concourse.bass_utils import run_bass_kernel_spmd

F32 = mybir.dt.float32
BF16 = mybir.dt.bfloat16
I32 = mybir.dt.int32
AF = mybir.ActivationFunctionType
ALU = mybir.AluOpType
AX = mybir.AxisListType

S = 2048
D = 1024
NT = 16
DIN = 3088
EPS = 1e-6
NEG = -30000.0


class Sem:
    def __init__(self, h, name):
        self.h = h
        self.count = 0
        self.name = name


class Res:
    def __init__(self, name, excl=False):
        self.name = name
        self.excl = excl
        self.w = None
        self.r = {}


_SNAP = {}


class Eng:
    def __init__(self, name, h, sem):
        self.name = name
        self.h = h
        self.sem = sem
        self.seen = {}

    def wait(self, tok):
        if tok is None:
            return
        if tok[0] == "PENDING":
            if self.name == "pe":
                return
            raise RuntimeError("wait on pending PE token by " + self.name)
        s, v, _ = tok
        if self.seen.get(id(s), 0) < v:
            self.h.wait_ge(s.h, v)
            self.seen[id(s)] = v
        snap = _SNAP.get((id(s), v))
        if snap:
            for k, val in snap.items():
                if self.seen.get(k, 0) < val:
                    self.seen[k] = val


class _Probe:
    def __init__(self):
        self.n = 0
        self.opname = ""
        self.fp32 = False
        self.accum = False

    def __getattr__(self, name):
        def f(*a, **k):
            out = k.get("out", a[0] if a else None)
            n = 1
            for d in out.shape[1:]:
                n *= d
            self.n = n
            self.opname = name
            self.call = (name, a, k)
            lt = k.get("lhsT", None)
            self.fp32 = lt is not None and lt.dtype == F32
            self.accum = k.get("accum_out", None) is not None
            return self
        return f

    def then_inc(self, *a, **k):
        return self

    def replay(self):
        name, a, k = self.call
        return lambda e: getattr(e, name)(*a, **k)


class Unit:
    __slots__ = ("kind", "eng", "fns", "reads", "writes", "dur", "busy", "idx", "args", "deps", "nsucc")


def _est(ename, pr):
    n = pr.n
    if ename == "pe":
        if pr.opname == "transpose":
            return 108.0
        return (max(64, n) / 2.4 + 6.0) * (4.0 if pr.fp32 else 1.0)
    if ename == "act":
        return 190.0 + n / 1.2 + (90.0 if pr.accum else 0.0)
    if ename == "dve":
        if pr.opname == "reciprocal":
            return 80.0 + 8.0 * n
        return 70.0 + n * 1.05
    if ename == "pool":
        if pr.opname == "tensor_tensor" and n <= 16:
            return 750.0
        return 150.0 + n * 2.3
    return 100.0


class FW:
    def __init__(self, nc, ndma_sems=16):
        self.rec = None
        self.cur_pe = None
        self.nc = nc
        self._ctx = []
        self.engs = {}
        for name, h in (("pe", nc.tensor), ("dve", nc.vector), ("act", nc.scalar),
                        ("pool", nc.gpsimd), ("sp", nc.sync)):
            self.engs[name] = Eng(name, h, self._sem("s_" + name))
        self.dsems = [self._sem("d%d" % i) for i in range(ndma_sems)]
        self.dnext = 0
        self.dsems_nb = [self._sem("n%d" % i) for i in range(8)]
        self.dnext_nb = 0
        self.dsems_sw = []
        self.pe_pending = []

    def _sem(self, name):
        cm = self.nc.semaphore(name)
        h = cm.__enter__()
        self._ctx.append(cm)
        return Sem(h, name)

    def barrier(self):
        was = self.rec is not None
        self.flush()
        self._barrier()
        if was:
            self.start_recording()

    def _barrier(self):
        assert not self.pe_pending
        toks = [(e.sem, e.sem.count, e.name) for e in self.engs.values() if e.sem.count]
        toks += [(s, s.count, "dma") for s in self.dsems if s.count]
        toks += [(s, s.count, "dma") for s, nb in self.dsems_sw if s.count and not nb]
        for e in self.engs.values():
            for t in toks:
                if t[2] != e.name:
                    e.wait(t)

    def sb(self, name, shape, dt):
        n = 1
        for d in shape[1:]:
            n *= d
        self.nbytes = getattr(self, "nbytes", 0) + n * (2 if dt == BF16 else 4)
        cm = self.nc.sbuf_tensor("sb_" + name, list(shape), dt)
        t = cm.__enter__()
        self._ctx.append(cm)
        return t

    def ps(self, name, shape, dt):
        cm = self.nc.psum_tensor(name, list(shape), dt)
        t = cm.__enter__()
        self._ctx.append(cm)
        return t

    def close(self):
        for cm in reversed(self._ctx):
            cm.__exit__(None, None, None)
        self._ctx = []

    def _acq(self, e, reads, writes):
        for r in reads:
            e.wait(r.w)
            if r.excl:
                for en, t in r.r.items():
                    if en != e.name:
                        e.wait(t)
        for w in writes:
            e.wait(w.w)
            for en, t in w.r.items():
                e.wait(t)

    def _rel(self, ename, tok, reads, writes):
        for r in reads:
            r.r[ename] = tok
        for w in writes:
            w.w = tok
            w.r = {}

    def start_recording(self):
        self.rec = []
        self.cur_pe = None

    def flush(self):
        if self.rec is None:
            return
        assert self.cur_pe is None
        units = self.rec
        self.rec = None
        n = len(units)
        lastw = {}
        readers = {}
        succ = [[] for _ in range(n)]
        for i, u in enumerate(units):
            u.idx = i
            deps = set()
            for r in u.reads:
                k = id(r)
                if r.excl:
                    if k in lastw:
                        deps.add(lastw[k])
                    deps.update(readers.get(k, ()))
                elif k in lastw:
                    deps.add(lastw[k])
            for w in u.writes:
                k = id(w)
                if k in lastw:
                    deps.add(lastw[k])
                deps.update(readers.get(k, ()))
            deps.discard(i)
            u.deps = deps
            for d in deps:
                succ[d].append(i)
            for r in u.reads:
                k = id(r)
                if r.excl:
                    lastw[k] = i
                    readers[k] = []
                else:
                    readers.setdefault(k, []).append(i)
            for w in u.writes:
                k = id(w)
                lastw[k] = i
                readers[k] = []
        LAT = float(os.environ.get("SCHED_LAT", "250"))
        blevel = [0.0] * n
        for i in range(n - 1, -1, -1):
            u = units[i]
            m = 0.0
            for j in succ[i]:
                v = blevel[j] + (LAT if units[j].eng != u.eng else 40.0)
                if v > m:
                    m = v
            blevel[i] = u.dur + m
        PEB = float(os.environ.get("SCHED_PEB", "0"))
        if PEB:
            for i in range(n):
                if units[i].eng == "pe":
                    blevel[i] += PEB
        ndep = [len(u.deps) for u in units]
        finish = [0.0] * n
        efree = {}
        ready = [i for i in range(n) if ndep[i] == 0]
        order = []
        SLACK = float(os.environ.get("SCHED_SLACK", "120"))
        while ready:
            ests = []
            mn = None
            for i in ready:
                u = units[i]
                st = efree.get(u.eng, 0.0)
                for d in u.deps:
                    f = finish[d] + (LAT if units[d].eng != u.eng else 40.0)
                    if f > st:
                        st = f
                ests.append(st)
                if mn is None or st < mn:
                    mn = st
            best = None
            bs = None
            bl = -1.0
            for i, st in zip(ready, ests):
                if st <= mn + SLACK and (blevel[i] > bl + 1e-9 or (abs(blevel[i] - bl) <= 1e-9 and i < best)):
                    bl = blevel[i]
                    best = i
                    bs = st
            ready.remove(best)
            u = units[best]
            if getattr(self, "diag", None) is not None and bs > efree.get(u.eng, 0.0) + 1.0:
                bd = max(u.deps, key=lambda d: finish[d] + (LAT if units[d].eng != u.eng else 40.0)) if u.deps else None
                if bd is not None:
                    ud = units[bd]
                    shared = [r.name for r in list(u.reads) + list(u.writes) if r in ud.reads or r in ud.writes]
                    key = (u.eng, shared[0] if shared else "?", ud.eng)
                    self.diag[key] = self.diag.get(key, 0.0) + bs - efree.get(u.eng, 0.0)
            efree[u.eng] = bs + u.busy
            finish[best] = bs + u.dur
            order.append(best)
            for j in succ[best]:
                ndep[j] -= 1
                if ndep[j] == 0:
                    ready.append(j)
        assert len(order) == n
        self.sched_span = getattr(self, "sched_span", 0.0) + max(finish) if n else 0.0
        for i in order:
            u = units[i]
            if u.kind == "op":
                self.op(u.eng, u.fns[0], u.reads, u.writes)
            elif u.kind == "mm":
                for j, (fn, rd, wr) in enumerate(u.fns):
                    self.mm(fn, rd, wr, last=(j == len(u.fns) - 1))
            else:
                self.dma(u.eng, u.args[0], u.args[1], u.reads, u.writes, nobar=u.args[2])

    def _record(self, kind, eng, fns, reads, writes, dur, busy, args=None):
        u = Unit()
        u.kind = kind
        u.eng = eng
        u.fns = fns
        u.reads = tuple(reads)
        u.writes = tuple(writes)
        u.dur = dur
        u.busy = busy
        u.args = args
        self.rec.append(u)

    def op(self, ename, fn, reads=(), writes=()):
        if self.rec is not None:
            pr = _Probe()
            fn(pr)
            d = _est(ename, pr)
            self._record("op", ename, [pr.replay()], reads, writes, d, d)
            return None
        e = self.engs[ename]
        self._acq(e, reads, writes)
        inst = fn(e.h)
        e.sem.count += 1
        inst.then_inc(e.sem.h, 1)
        _SNAP[(id(e.sem), e.sem.count)] = dict(e.seen)
        self._rel(ename, (e.sem, e.sem.count, ename), reads, writes)
        return inst

    def mm(self, fn, reads=(), writes=(), last=False):
        if self.rec is not None:
            pr = _Probe()
            fn(pr)
            d = _est("pe", pr)
            if self.cur_pe is None:
                self.cur_pe = [[], [], [], 0.0]
            g = self.cur_pe
            g[0].append((pr.replay(), tuple(reads), tuple(writes)))
            for r in reads:
                if r not in g[1]:
                    g[1].append(r)
            for w in writes:
                if w not in g[2]:
                    g[2].append(w)
            g[3] += d
            if last:
                self.cur_pe = None
                self._record("mm", "pe", g[0], g[1], g[2], g[3] + 60.0, g[3])
            return None
        e = self.engs["pe"]
        self._acq(e, reads, writes)
        inst = fn(e.h)
        self.pe_pending.append((tuple(reads), tuple(writes)))
        if last:
            e.sem.count += 1
            inst.then_inc(e.sem.h, 1)
            _SNAP[(id(e.sem), e.sem.count)] = dict(e.seen)
            tok = (e.sem, e.sem.count, "pe")
            for rd, wr in self.pe_pending:
                self._rel("pe", tok, rd, wr)
            self.pe_pending = []
        else:
            for w in writes:
                w.w = ("PENDING",)
                w.r = {}
            for r in reads:
                r.r["pe"] = ("PENDING",)
        return inst

    def dma(self, qname, out, in_, reads=(), writes=(), nobar=False):
        if self.rec is not None:
            n = 1
            for d_ in out.shape:
                n *= d_
            self._record("dma", qname, None, reads, writes, 2500.0 + n * 4 / 150.0, 120.0, (out, in_, nobar))
            return None
        e = self.engs[qname]
        self._acq(e, reads, writes)
        if qname == "pool":
            s = self._sem("w%d" % len(self.dsems_sw))
            self.dsems_sw.append((s, nobar))
        elif nobar:
            s = self.dsems_nb[self.dnext_nb]
            self.dnext_nb = (self.dnext_nb + 1) % len(self.dsems_nb)
        else:
            s = self.dsems[self.dnext]
            self.dnext = (self.dnext + 1) % len(self.dsems)
        if s.count:
            e.wait((s, s.count, "dma"))
        inst = e.h.dma_start(out=out, in_=in_)
        s.count += 16
        inst.then_inc(s.h, 16)
        _SNAP[(id(s), s.count)] = dict(e.seen)
        tok = (s, s.count, "dma:" + s.name)
        self._rel("dma:" + s.name, tok, reads, writes)
        return tok


class T:
    def __init__(self, fw, name, shape, dt, view=None):
        self.t = fw.sb(name, shape, dt) if view is None else view
        self.r = Res(name)


class Arena:
    def __init__(self, t, nwords):
        self.t = t
        self.n = nwords
        self.o = 0

    def reset(self):
        self.o = 0

    def get(self, name, shape, dt):
        n = 1
        for d in shape[1:]:
            n *= d
        words = n if dt != BF16 else (n + 1) // 2
        assert self.o + words <= self.n, (name, self.o, words, self.n)
        v = self.t[:, self.o:self.o + words]
        self.o += words
        if dt != F32:
            v = v.bitcast(dt)
        if len(shape) == 3:
            v = v.rearrange("p (a b) -> p a b", b=shape[2])
        elif len(shape) == 4:
            v = v.rearrange("p (a b c) -> p a b c", b=shape[2], c=shape[3])
        return T(None, name, shape, dt, view=v)

    def __getitem__(self, k):
        return self.t[k]


def _consts():
    p = np.arange(128)
    cols = {}
    ident = np.eye(128, dtype=np.float32)
    cols["ident"] = ident
    k = p[:, None]
    q = p[None, :]
    mcur = np.where(k <= q, 0.0, NEG).astype(np.float32)
    mprev = np.where(k > q, 0.0, NEG).astype(np.float32)
    causal = (k <= q).astype(np.float32)
    global _CMASK
    _CMASK = np.ascontiguousarray(np.concatenate([mcur, mprev, causal], axis=1))
    cols["tri_in"] = causal * (-1.0 / 16.0)
    cols["tri_rev"] = (k > q).astype(np.float32) * (-1.0 / 16.0)
    cols["ncol"] = np.full((128, 2), -1.0 / 16.0, np.float32)
    h = np.arange(4, dtype=np.float32)
    log_g = np.log(1.0 - 2.0 ** (-5.0 - h)).astype(np.float32)
    i1 = (p[:, None] + 1).astype(np.float32)
    qdec = np.exp(log_g[None, :] * i1)
    kdec = np.exp(-log_g[None, :] * i1) / 8.0
    cols["dec8"] = np.concatenate([qdec, kdec], axis=1).astype(np.float32)
    cdec = np.exp(log_g * 128.0)
    sdec = np.zeros((128, 2), np.float32)
    for c in range(2):
        sdec[0:64, c] = cdec[2 * c]
        sdec[64:128, c] = cdec[2 * c + 1]
    cols["sdec"] = sdec
    bd64 = np.zeros((128, 128), np.float32)
    bd64[0:64, 0:64] = 1.0
    bd64[64:128, 64:128] = 1.0
    cols["bd64dec"] = np.concatenate([bd64 * sdec[:, 0:1], bd64 * sdec[:, 1:2]], axis=1)
    bd32 = np.zeros((128, 256), np.float32)
    for hh in range(4):
        bd32[32 * hh:32 * hh + 32, 64 * hh:64 * hh + 64] = 1.0
    cols["bd32"] = bd32
    gm = np.zeros((128, 2), np.float32)
    gm[0:16, 0] = 1.0
    gm[16, 1] = 1.0
    cols["gamask"] = gm
    fa = (10000.0 ** (-np.arange(0, 64, 2, dtype=np.float32) / 64.0)).astype(np.float32)
    fr = (1.0 / (10000.0 ** np.linspace(0.0, 1.0, 32, dtype=np.float32))).astype(np.float32)
    cols["freq"] = np.tile(np.concatenate([fa, fr])[None, :], (128, 1)).astype(np.float32)
    cols["neghalf"] = np.full((128, 8), -0.5, np.float32)
    cols["neghalf16"] = np.full((128, 16), -0.5, np.float32)
    off = {}
    o = 0
    parts = []
    for kname, v in cols.items():
        off[kname] = (o, v.shape[1])
        o += v.shape[1]
        parts.append(v.astype(np.float32))
    return np.ascontiguousarray(np.concatenate(parts, axis=1)), off


_CBLOB, _COFF = _consts()
NCONST = _CBLOB.shape[1]

_AQ = np.concatenate([np.arange(64 * hh, 64 * hh + 64) for hh in (0, 4, 1, 5, 2, 6, 3, 7)])
_r = lambda a, b: np.arange(a, b)
_PERM = np.concatenate([
    _r(3072, 3088),
    _AQ,
    _r(512, 640), _r(640, 768), _r(1280, 1536),
    _r(1536, 1792), _r(1792, 2048),
    _r(768, 1280),
    _r(2048, 2304), _r(2816, 3072),
    _r(2304, 2432), _r(2432, 2560), _r(2560, 2816),
])
NW = _PERM.shape[0]
_QK32 = np.concatenate([_r(1280, 1536), _r(2304, 2432), _r(1536, 1792), _r(2432, 2560)])
GOFF = 0
COFFS = [16 + 512 * i for i in range(6)]


def build(n_layers=2, n_tiles=NT, dbg=None, sched=True):
    nc = bass.Bass("TRN2", target_bir_lowering=False)
    _SNAP.clear()
    fw = FW(nc)
    if sched:
        fw.start_recording()
    L = n_layers

    def din(name, shape, dt=F32):
        return nc.dram_tensor(name, list(shape), dt, kind="ExternalInput").ap()

    x_d = din("x", [S, D])
    c_d = din("c", [128, 8])
    pos_d = din("pos", [128, NT], I32)
    wmod_d = din("w_mod", [L, D, 3 * D])
    bmodF_d = din("bmodF", [L, 128, 24])
    bgate_d = din("bgate", [L, D])
    pregF_d = din("pregF", [L, 128, 8])
    postg_d = din("postg", [L, D])
    win_d = din("w_in", [L, D, NW])
    sink_d = din("sinks", [L, 8])
    gwe_d = din("gwe", [L, 128, 128])
    ggain_d = din("ggain", [L, 64])
    wout_d = din("w_out", [L, D, D])
    wqk_d = din("w_qk32", [L, D, 768])
    const_d = din("consts", [128, NCONST])
    cmask_d = din("cmask", [128, 384])
    out_d = nc.dram_tensor("out", [S, D], F32, kind="ExternalOutput").ap()

    xs = fw.sb("xs", [128, NT, D], F32)
    R_x = [Res("x%d" % t) for t in range(NT)]
    win = fw.sb("win", [128, 8, NW], BF16)
    R_winc = [[Res("win%d_%d" % (kc, hf)) for hf in range(2)] for kc in range(8)]
    wout = fw.sb("wout", [128, 8, D], BF16)
    R_woutc = [Res("wout%d" % kc) for kc in range(8)]
    cst = T(fw, "cst", [128, NCONST], F32)

    def C(name):
        o, n = _COFF[name]
        return cst.t[:, o:o + n]

    identb = T(fw, "identb", [128, 128], BF16)
    mcurb = T(fw, "mcurb", [128, 512], BF16)
    mprevb = T(fw, "mprevb", [128, 512], BF16)
    gpg = T(fw, "gpg", [128, D], F32)
    gmF = T(fw, "gmF", [128, 8], F32)
    shF = T(fw, "shF", [128, 8], F32)
    esink2 = T(fw, "esink2", [128, 8], F32)
    gwe = T(fw, "gwe", [128, 128], F32)
    ggain = T(fw, "ggain", [128, 64], F32)
    tabs = T(fw, "tabs", [128, 4, NT, 32], F32)
    small = T(fw, "small", [128, 64], F32)
    retS = T(fw, "retS", [128, 256], F32)
    retSb = T(fw, "retSb", [128, 256], BF16)
    glaS = T(fw, "glaS", [128, 256], F32)
    glaSb = T(fw, "glaSb", [128, 256], BF16)
    big32_ = T(fw, "big32", [128, 512], F32)
    big32 = [big32_, big32_]
    xn = T(fw, "xn", [128, D], BF16)
    hT = T(fw, "hT", [128, 8, 128], BF16)
    gaTe = T(fw, "gaTe", [128, 128], F32)
    nl = T(fw, "nl", [128, 128], F32)
    EqEr = T(fw, "EqEr", [128, 256], F32)
    Eq = T(None, "Eq", [128, 128], F32, view=EqEr.t[:, 0:128])
    Eq.r = EqEr.r
    Er = T(None, "Er", [128, 128], F32, view=EqEr.t[:, 128:256])
    Er.r = EqEr.r
    Ek = T(fw, "Ek", [128, 128], F32)
    aqk = T(fw, "aqk", [128, 640], BF16)
    rqk32 = T(fw, "rqk32", [128, 512], F32)
    gqk = T(fw, "gqk", [128, 256], BF16)
    scT = T(fw, "scT", [128, 512], BF16)
    stat = T(fw, "stat", [128, 32], F32)
    statA = T(fw, "statA", [128, 8], F32)
    h0F = T(fw, "h0F", [128, 16], F32)
    e0r = T(fw, "e0r", [128, 2], F32)
    s00 = T(fw, "s00", [128, 8], F32)
    ssq = [T(fw, "ssq%d" % i, [128, NT], F32) for i in range(n_layers)]
    rstdT = [T(fw, "rstd%d" % i, [128, NT], F32) for i in range(n_layers)]
    stat1 = T(fw, "stat1", [128, 8], F32)
    stat2 = T(fw, "stat2", [128, 8], F32)
    stat3 = T(fw, "stat3", [128, 8], F32)
    causalb = T(fw, "causalb", [128, 512], BF16)
    kT = [T(fw, "kT%d" % i, [128, 128], BF16) for i in range(3)]
    vext = [T(fw, "vext%d" % i, [128, 2, 65], BF16) for i in range(3)]
    ebl = [T(fw, "ebl%d" % i, [128, 2], F32) for i in range(2)]
    rqk = [T(fw, "rqk%d" % i, [128, 512], BF16) for i in range(2)]
    gkh = [T(fw, "gkh%d" % i, [128, 128], BF16) for i in range(2)]
    qbd = [[T(fw, "qbd%d%d" % (i, g), [128, 4, 128], BF16) for g in range(2)] for i in range(2)]
    rqbd = [[T(fw, "rqbd%d%d" % (i, c), [128, 2, 128], BF16) for c in range(2)] for i in range(2)]
    rqT = [T(fw, "rqT%d" % i, [128, 2, 128], BF16) for i in range(2)]
    rkT = [T(fw, "rkT%d" % i, [128, 2, 128], BF16) for i in range(2)]
    gqbd = [T(fw, "gqbd%d" % i, [128, 4, 128], BF16) for i in range(2)]
    gqT = [T(fw, "gqT%d" % i, [128, 128], BF16) for i in range(2)]
    gkT = [T(fw, "gkT%d" % i, [128, 128], BF16) for i in range(2)]
    rv = [T(fw, "rv%d" % i, [128, 256], BF16) for i in range(2)]
    gv = [T(fw, "gv%d" % i, [128, 256], BF16) for i in range(2)]
    sg1 = T(fw, "sg1", [128, D], BF16)
    rotA2 = T(fw, "rotA2", [128, 256], F32)
    rotB2 = T(fw, "rotB2", [128, 256], F32)
    AW = 5120
    arena_t = fw.sb("arena", [128, AW], F32)
    arA = Arena(arena_t, AW)
    wst = [arA.get("wst%d" % i, [128, D], F32) for i in range(2)]
    cb = arA.get("cb", [128, 8, 128], F32)
    ang = arA.get("ang", [128, NT, 32], F32)
    tu = arA.get("tu", [128, NT, 32], F32)
    tki = arA.get("tki", [128, NT, 32], I32)
    ty = arA.get("ty", [128, NT, 32], F32)
    arA2 = Arena(arena_t, AW)
    arA2.o = 3072
    wst_extra = [arA2.get("wst%d" % i, [128, D], F32) for i in (2, 3)]
    cmk = T(None, "cmk", [128, 384], F32, view=rqk32.t[:, 0:384])
    cmk.r = rqk32.r
    arB = Arena(arena_t, AW)
    rotA1 = arB.get("rotA", [128, 640], F32)
    rotB1 = arB.get("rotB", [128, 640], F32)
    th = arB.get("th", [128, 512], BF16)
    sg0 = arB.get("sg", [128, D], BF16)
    sg = [sg0, sg1]
    PT = [[arB.get("PT%d%d" % (g, b), [128, 512], BF16) for b in range(2)] for g in range(2)]
    an = arB.get("an", [128, 256], F32)
    rraw = arB.get("rraw", [128, 256], F32)
    rsq = arB.get("rsq", [128, 256], F32)
    mix = arB.get("mix", [128, D], BF16)
    mixT = arB.get("mixT", [128, 8, 128], BF16)
    dS = arB.get("dS", [128, 256], F32)

    pp = [fw.ps("pp%d" % i, [128, 512], F32) for i in range(2)]
    R_pp = [Res("pp%d" % i, excl=True) for i in range(2)]
    ptA = fw.ps("ptA", [128, 1024], BF16)
    R_tA = Res("ptA", excl=True)
    ptB = fw.ps("ptB", [128, 1024], BF16)
    R_tB = Res("ptB", excl=True)
    psc = [fw.ps("psc%d" % i, [128, 512], F32) for i in range(2)]
    R_sc = [Res("psc%d" % i, excl=True) for i in range(2)]
    po = [fw.ps("po%d" % i, [128, 512], F32) for i in range(2)]
    R_o = [Res("po%d" % i, excl=True) for i in range(2)]

    dve = lambda fn, reads=(), writes=(): fw.op("dve", fn, reads, writes)
    act = lambda fn, reads=(), writes=(): fw.op("act", fn, reads, writes)
    pool = lambda fn, reads=(), writes=(): fw.op("pool", fn, reads, writes)
    dbg_outs = {}

    def dump(name, ap, R, n):
        if dbg is None:
            return
        o = nc.dram_tensor("dbg_" + name, [128, n], F32, kind="ExternalOutput").ap()
        for a0 in range(0, n, 512):
            a1 = min(n, a0 + 512)
            dve(lambda e: e.tensor_copy(out=big32_.t[:, 0:a1 - a0], in_=ap[:, a0:a1]), [R], [big32_.r])
            fw.dma("sp", o[:, a0:a1], big32_.t[:, 0:a1 - a0], reads=[big32_.r])

    def load_weights(l, early=False):
        for kc in range(8):
            for hf in range(2):
                c0 = hf * (NW // 2)
                fw.dma("pool", win[:, kc, c0:c0 + NW // 2],
                       win_d[l, kc * 128:(kc + 1) * 128, c0:c0 + NW // 2], writes=[R_winc[kc][hf]], nobar=early)
        for kc in range(8):
            fw.dma("pool", wout[:, kc, :], wout_d[l, kc * 128:(kc + 1) * 128, :], writes=[R_woutc[kc]], nobar=True)

    fw.dma("sp", cst.t[:], const_d, writes=[cst.r])
    fw.dma("sp", xs[:, 0, :], x_d[0:128, :], writes=[R_x[0]])
    load_weights(0)
    if n_tiles > 1:
        fw.dma("sp", xs[:, 1, :], x_d[128:256, :], writes=[R_x[1]])
    dve(lambda e: e.tensor_copy(out=identb.t[:], in_=C("ident")), [cst.r], [identb.r])
    fw.dma("sp", cmk.t[:], cmask_d, writes=[cmk.r])
    for i_, dst_ in enumerate((mcurb, mprevb, causalb)):
        dve(lambda e: e.tensor_copy(out=dst_.t[:].rearrange("p (r q) -> p r q", q=128),
                                    in_=cmk.t[:, 128 * i_:128 * (i_ + 1)].unsqueeze(1).broadcast_to([128, 4, 128])),
            [cmk.r], [dst_.r])
    for i_ in range(2):
        for g in range(2):
            pool(lambda e: e.memset(qbd[i_][g].t[:], 0.0), [], [qbd[i_][g].r])
            pool(lambda e: e.memset(rqbd[i_][g].t[:], 0.0), [], [rqbd[i_][g].r])
        pool(lambda e: e.memset(gqbd[i_].t[:], 0.0), [], [gqbd[i_].r])
    for i_ in range(3):
        pool(lambda e: e.memset(vext[i_].t[:], 1.0), [], [vext[i_].r])

    posi = T(fw, "posi", [128, NT], I32)
    posf = T(fw, "posf", [128, NT], F32)
    fw.dma("sp", posi.t[:], pos_d, writes=[posi.r])
    dve(lambda e: e.tensor_copy(out=posf.t[:], in_=posi.t[:]), [posi.r], [posf.r])
    TWO_PI = float(2 * np.pi)
    PI = float(np.pi)
    C1_2PI = float(np.float32(2 * np.pi))
    C2_2PI = float(np.float32(2 * np.pi - C1_2PI))

    def wrap(dst, src, shift):
        dve(lambda e: e.tensor_scalar(out=dst.t[:], in0=src.t[:], scalar1=float(shift), scalar2=None,
                                      op0=ALU.add), [src.r], [dst.r])
        dve(lambda e: e.tensor_scalar(out=tu.t[:], in0=dst.t[:], scalar1=PI, scalar2=-TWO_PI,
                                      op0=ALU.is_gt, op1=ALU.mult), [dst.r], [tu.r])
        dve(lambda e: e.tensor_tensor(out=dst.t[:], in0=dst.t[:], in1=tu.t[:], op=ALU.add),
            [dst.r, tu.r], [dst.r])
        dve(lambda e: e.tensor_scalar(out=tu.t[:], in0=dst.t[:], scalar1=-PI, scalar2=TWO_PI,
                                      op0=ALU.is_lt, op1=ALU.mult), [dst.r], [tu.r])
        dve(lambda e: e.tensor_tensor(out=dst.t[:], in0=dst.t[:], in1=tu.t[:], op=ALU.add),
            [dst.r, tu.r], [dst.r])

    fo, _ = _COFF["freq"]
    for which in range(2):
        fr_ap = cst.t[:, fo + 32 * which: fo + 32 * which + 32].unsqueeze(1).broadcast_to([128, NT, 32])
        pos_ap = posf.t[:].unsqueeze(2).broadcast_to([128, NT, 32])
        dve(lambda e: e.tensor_tensor(out=ang.t[:], in0=pos_ap, in1=fr_ap, op=ALU.mult),
            [posf.r, cst.r], [ang.r])
        dve(lambda e: e.tensor_scalar(out=tu.t[:], in0=ang.t[:], scalar1=float(1.0 / TWO_PI), scalar2=None,
                                      op0=ALU.mult), [ang.r], [tu.r])
        dve(lambda e: e.tensor_copy(out=tki.t[:], in_=tu.t[:]), [tu.r], [tki.r])
        dve(lambda e: e.tensor_copy(out=tu.t[:], in_=tki.t[:]), [tki.r], [tu.r])
        dve(lambda e: e.scalar_tensor_tensor(out=ty.t[:], in0=tu.t[:], scalar=-C1_2PI, in1=ang.t[:],
                                             op0=ALU.mult, op1=ALU.add), [tu.r, ang.r], [ty.r])
        dve(lambda e: e.scalar_tensor_tensor(out=ty.t[:], in0=tu.t[:], scalar=-C2_2PI, in1=ty.t[:],
                                             op0=ALU.mult, op1=ALU.add), [tu.r, ty.r], [ty.r])
        wrap(ang, ty, 0.0)
        act(lambda e: e.activation(out=tabs.t[:, 2 * which + 1, :, :], in_=ang.t[:], func=AF.Sin),
            [ang.r], [tabs.r])
        wrap(ty, ang, PI / 2)
        act(lambda e: e.activation(out=tabs.t[:, 2 * which, :, :], in_=ty.t[:], func=AF.Sin),
            [ty.r], [tabs.r])

    c32 = T(fw, "c32", [128, 8], F32)
    cth = T(fw, "cth", [128, 8], F32)
    fw.dma("sp", c32.t[:], c_d, writes=[c32.r])
    act(lambda e: e.activation(out=cth.t[:], in_=c32.t[:], func=AF.Tanh, scale=0.5), [c32.r], [cth.r])
    dve(lambda e: e.scalar_tensor_tensor(out=cth.t[:], in0=cth.t[:], scalar=1.0, in1=c32.t[:],
                                         op0=ALU.add, op1=ALU.mult), [cth.r, c32.r], [cth.r])
    dve(lambda e: e.tensor_scalar(out=cth.t[:], in0=cth.t[:], scalar1=0.5, scalar2=None, op0=ALU.mult),
        [cth.r], [cth.r])

    out_toks = []

    for l in range(L):
        bmodF = T(fw, "bmodF%d" % l, [128, 24], F32)
        pregF = T(fw, "pregF%d" % l, [128, 8], F32)
        fw.dma("sp", bmodF.t[:], bmodF_d[l], writes=[bmodF.r])
        fw.dma("sp", pregF.t[:], pregF_d[l], writes=[pregF.r])
        fw.dma("sp", gwe.t[:], gwe_d[l], writes=[gwe.r])
        fw.dma("sp", ggain.t[:], ggain_d[l].partition_broadcast(128), writes=[ggain.r])
        fw.dma("sp", esink2.t[:], sink_d[l].partition_broadcast(128), writes=[esink2.r])
        act(lambda e: e.activation(out=esink2.t[:], in_=esink2.t[:], func=AF.Exp), [esink2.r], [esink2.r])
        dve(lambda e: e.tensor_scalar(out=esink2.t[:], in0=esink2.t[:], scalar1=2.0, scalar2=None,
                                      op0=ALU.mult), [esink2.r], [esink2.r])
        dve(lambda e: e.tensor_copy(out=cb.t[:], in_=cth.t[:].unsqueeze(2).broadcast_to([128, 8, 128])),
            [cth.r], [cb.r])
        banks = [(pp[0], R_pp[0]), (pp[1], R_pp[1]), (psc[0], R_sc[0]), (psc[1], R_sc[1]),
                 (po[0], R_o[0]), (po[1], R_o[1])]
        i = 0
        for kc in range(8):
            for third in range(3):
                stl = wst if l == 0 else wst + wst_extra
                st = stl[i % len(stl)]
                i += 1
                fw.dma("sp", st.t[:], wmod_d[l, kc * 128:(kc + 1) * 128, third * 1024:(third + 1) * 1024],
                       writes=[st.r])
                for hf in range(2):
                    bk, rb = banks[third * 2 + hf]
                    fw.mm(lambda e: e.matmul(bk[:, :], lhsT=cb.t[:, kc, :], rhs=st.t[:, hf * 512:(hf + 1) * 512],
                                             start=(kc == 0), stop=(kc == 7)),
                          reads=[cb.r, st.r], writes=[rb], last=True)
        for which, dst in ((0, shF), (1, small)):
            for hf in range(2):
                bk, rb = banks[which * 2 + hf]
                dve(lambda e: e.tensor_tensor(
                    out=big32_.t[:].rearrange("p (k n) -> p k n", n=128),
                    in0=bk[:, :].rearrange("p (k n) -> p k n", n=128),
                    in1=C("ident").unsqueeze(1).broadcast_to([128, 4, 128]), op=ALU.mult),
                    [rb, cst.r], [big32_.r])
                dve(lambda e: e.tensor_reduce(out=dst.t[:, 4 * hf:4 * hf + 4],
                                              in_=big32_.t[:].rearrange("p (k n) -> p k n", n=128),
                                              op=ALU.add, axis=AX.X), [big32_.r], [dst.r])
        dve(lambda e: e.tensor_tensor(out=shF.t[:], in0=shF.t[:], in1=bmodF.t[:, 0:8], op=ALU.add),
            [shF.r, bmodF.r], [shF.r])
        dve(lambda e: e.tensor_tensor(out=small.t[:, 0:8], in0=small.t[:, 0:8], in1=bmodF.t[:, 8:16], op=ALU.add),
            [small.r, bmodF.r], [small.r])
        dve(lambda e: e.scalar_tensor_tensor(out=gmF.t[:], in0=small.t[:, 0:8], scalar=1.0, in1=pregF.t[:],
                                             op0=ALU.add, op1=ALU.mult), [small.r, pregF.r], [gmF.r])
        fw.dma("sp", gpg.t[:], bgate_d[l].partition_broadcast(128), writes=[gpg.r])
        for hf in range(2):
            bk, rb = banks[4 + hf]
            dve(lambda e: e.tensor_tensor(out=gpg.t[:, hf * 512:(hf + 1) * 512], in0=bk[:, :],
                                          in1=gpg.t[:, hf * 512:(hf + 1) * 512], op=ALU.add),
                [rb, gpg.r], [gpg.r])
        for hf in range(2):
            fw.dma("sp", rqk32.t[:], postg_d[l, hf * 512:(hf + 1) * 512].partition_broadcast(128), writes=[rqk32.r])
            dve(lambda e: e.tensor_tensor(out=gpg.t[:, hf * 512:(hf + 1) * 512], in0=gpg.t[:, hf * 512:(hf + 1) * 512],
                                          in1=rqk32.t[:], op=ALU.mult), [gpg.r, rqk32.r], [gpg.r])
        nst = min(2, n_tiles) if l == 0 else n_tiles
        if l == 0:
            for t_ in range(nst):
                act(lambda e: e.activation(out=xn.t[:], in_=xs[:, t_, :], func=AF.Square,
                                           accum_out=ssq[0].t[:, t_:t_ + 1]), [R_x[t_]], [xn.r, ssq[0].r])
        dve(lambda e: e.tensor_scalar(out=rstdT[l].t[:, 0:nst], in0=ssq[l].t[:, 0:nst], scalar1=1.0 / D,
                                      scalar2=EPS, op0=ALU.mult, op1=ALU.add), [ssq[l].r], [rstdT[l].r])
        pool(lambda e: e.tensor_tensor(out=rstdT[l].t[:, 0:nst], in0=rstdT[l].t[:, 0:nst],
                                       in1=C("neghalf16")[:, 0:nst], op=ALU.pow), [rstdT[l].r, cst.r], [rstdT[l].r])
        dve(lambda e: e.tensor_scalar(out=e0r.t[:], in0=C("ident")[:, 0:2], scalar1=rstdT[l].t[:, 0:1], scalar2=None,
                                      op0=ALU.mult), [cst.r, rstdT[l].r], [e0r.r])
        for kc in range(8):
            fw.mm(lambda e: e.matmul(po[1][:, 2 * kc:2 * kc + 2], lhsT=xs[:, 0, kc * 128:(kc + 1) * 128], rhs=e0r.t[:],
                                     start=True, stop=True),
                  reads=[R_x[0], e0r.r], writes=[R_o[1]], last=(kc == 7))
        dve(lambda e: e.tensor_tensor(out=h0F.t[:, 0:8], in0=po[1][:, 0:16].rearrange("p (k two) -> p k two", two=2)[:, :, 0],
                                      in1=gmF.t[:], op=ALU.mult), [R_o[1], gmF.r], [h0F.r])
        dve(lambda e: e.tensor_tensor(out=h0F.t[:, 0:8], in0=h0F.t[:, 0:8], in1=shF.t[:], op=ALU.add),
            [h0F.r, shF.r], [h0F.r])
        dve(lambda e: e.tensor_copy(out=cb.t[:], in_=h0F.t[:, 0:8].unsqueeze(2).broadcast_to([128, 8, 128])),
            [h0F.r], [cb.r])
        for kc in range(8):
            st = wst[kc % 2]
            fw.dma("sp", st.t[:, 0:768], wqk_d[l, kc * 128:(kc + 1) * 128, :], writes=[st.r])
            for hf in range(2):
                fw.mm(lambda e: e.matmul(pp[hf][:, 0:384], lhsT=cb.t[:, kc, :], rhs=st.t[:, hf * 384:(hf + 1) * 384],
                                         start=(kc == 0), stop=(kc == 7)),
                      reads=[cb.r, st.r], writes=[R_pp[hf]], last=True)
        act(lambda e: e.activation(out=big32_.t[:, 0:384], in_=pp[0][:, 0:384], func=AF.Copy), [R_pp[0]], [big32_.r])
        dve(lambda e: e.tensor_tensor(out=big32_.t[:, 0:384], in0=big32_.t[:, 0:384], in1=pp[1][:, 0:384], op=ALU.mult),
            [big32_.r, R_pp[1]], [big32_.r])
        dve(lambda e: e.tensor_reduce(out=s00.t[:, 0:4], in_=big32_.t[:, 0:256].rearrange("p (h d) -> p h d", d=64),
                                      axis=AX.X, op=ALU.add), [big32_.r], [s00.r])
        dve(lambda e: e.tensor_reduce(out=s00.t[:, 4:8], in_=big32_.t[:, 256:384].rearrange("p (h d) -> p h d", d=32),
                                      axis=AX.X, op=ALU.add), [big32_.r], [s00.r])
        dve(lambda e: e.tensor_scalar(out=s00.t[:, 0:4], in0=s00.t[:, 0:4], scalar1=0.125, scalar2=None, op0=ALU.mult),
            [s00.r], [s00.r])
        dve(lambda e: e.tensor_scalar(out=s00.t[:, 4:8], in0=s00.t[:, 4:8], scalar1=float(32 ** -0.5), scalar2=None,
                                      op0=ALU.mult), [s00.r], [s00.r])
        pool(lambda e: e.memset(retS.t[:], 0.0), [], [retS.r])
        pool(lambda e: e.memset(retSb.t[:], 0.0), [], [retSb.r])
        pool(lambda e: e.memset(glaS.t[:], 0.0), [], [glaS.r])
        pool(lambda e: e.memset(glaSb.t[:], 0.0), [], [glaSb.r])

        fw.barrier()
        def genA(t):
            p2 = t % 2
            p3 = t % 3
            xt = xs[:, t, :]
            Rx = R_x[t]
            AB = [(pp[0], R_pp[0]), (pp[1], R_pp[1]), (psc[1], R_sc[1])]
            GB = (psc[0], R_sc[0])
            dve(lambda e: e.tensor_scalar(out=xn.t[:], in0=xt, scalar1=rstdT[l].t[:, t:t + 1], scalar2=None,
                                          op0=ALU.mult), [Rx, rstdT[l].r], [xn.r])
            for kc in range(8):
                fw.mm(lambda e: e.transpose(out=ptA[:, kc * 128:(kc + 1) * 128], in_=xn.t[:, kc * 128:(kc + 1) * 128],
                                            identity=identb.t[:]),
                      reads=[xn.r, identb.r], writes=[R_tA], last=(kc == 7))
            if l == 0 and t + 2 < n_tiles:
                t2 = t + 2
                act(lambda e: e.activation(out=xn.t[:], in_=xs[:, t2, :], func=AF.Square,
                                           accum_out=ssq[0].t[:, t2:t2 + 1]), [R_x[t2]], [xn.r, ssq[0].r])
                dve(lambda e: e.tensor_scalar(out=rstdT[0].t[:, t2:t2 + 1], in0=ssq[0].t[:, t2:t2 + 1], scalar1=1.0 / D,
                                              scalar2=EPS, op0=ALU.mult, op1=ALU.add), [ssq[0].r], [rstdT[0].r])
                pool(lambda e: e.tensor_tensor(out=rstdT[0].t[:, t2:t2 + 1], in0=rstdT[0].t[:, t2:t2 + 1],
                                               in1=C("neghalf")[:, 0:1], op=ALU.pow), [rstdT[0].r, cst.r], [rstdT[0].r])
            for kc in range(8):
                if kc < 4:
                    act(lambda e: e.activation(out=hT.t[:, kc, :], in_=ptA[:, kc * 128:(kc + 1) * 128],
                                               func=AF.Identity, scale=gmF.t[:, kc:kc + 1], bias=shF.t[:, kc:kc + 1]),
                        [R_tA, gmF.r, shF.r], [hT.r])
                else:
                    dve(lambda e: e.tensor_scalar(out=hT.t[:, kc, :], in0=ptA[:, kc * 128:(kc + 1) * 128],
                                                  scalar1=gmF.t[:, kc:kc + 1], scalar2=shF.t[:, kc:kc + 1],
                                                  op0=ALU.mult, op1=ALU.add),
                        [R_tA, gmF.r, shF.r], [hT.r])
            yield

            def proj(ci, bank):
                for kc in range(8):
                    fw.mm(lambda e: e.matmul(AB[bank][0][:, :], lhsT=hT.t[:, kc, :],
                                             rhs=win[:, kc, COFFS[ci]:COFFS[ci] + 512],
                                             start=(kc == 0), stop=(kc == 7)),
                          reads=[hT.r] + R_winc[kc], writes=[AB[bank][1]], last=(kc == 7))

            cA = tabs.t[:, 0, t, :]
            sA = tabs.t[:, 1, t, :]
            cR = tabs.t[:, 2, t, :]
            sR = tabs.t[:, 3, t, :]

            def rotary(src, R_src, nh, cos, sin, dst, dst_off, R_dst, final_eng):
                n = nh * 64
                rotA, rotB = (rotA1, rotB1) if nh == 8 else (rotA2, rotB2)
                s4 = src.rearrange("p (h two f) -> p h two f", two=2, f=32)
                a4 = rotA.t[:, 0:n].rearrange("p (h two f) -> p h two f", two=2, f=32)
                b4 = rotB.t[:, 0:n].rearrange("p (h two f) -> p h two f", two=2, f=32)
                d4 = dst[:, dst_off:dst_off + n].rearrange("p (h two f) -> p h two f", two=2, f=32)
                cos4 = cos.unsqueeze(1).unsqueeze(1).broadcast_to([128, nh, 2, 32])
                sin3 = sin.unsqueeze(1).broadcast_to([128, nh, 32])
                dve(lambda e: e.tensor_tensor(out=a4, in0=s4, in1=cos4, op=ALU.mult),
                    [R_src, tabs.r], [rotA.r])
                dve(lambda e: e.scalar_tensor_tensor(out=b4[:, :, 0, :], in0=s4[:, :, 1, :], scalar=-1.0, in1=sin3,
                                                     op0=ALU.mult, op1=ALU.mult), [R_src, tabs.r], [rotB.r])
                dve(lambda e: e.tensor_tensor(out=b4[:, :, 1, :], in0=s4[:, :, 0, :], in1=sin3, op=ALU.mult),
                    [R_src, tabs.r], [rotB.r])
                fw.op(final_eng, lambda e: e.tensor_tensor(out=d4, in0=a4, in1=b4, op=ALU.add),
                      [rotA.r, rotB.r], [R_dst])

            for kc in range(8):
                fw.mm(lambda e: e.matmul(GB[0][:, 0:128], lhsT=win[:, kc, 0:128], rhs=hT.t[:, kc, :],
                                         start=(kc == 0), stop=(kc == 7)),
                      reads=[hT.r] + R_winc[kc], writes=[GB[1]], last=(kc == 7))
            gmk = C("gamask")
            act(lambda e: e.activation(out=gaTe.t[:], in_=GB[0][:, 0:128], func=AF.Identity,
                                       scale=gmk[:, 0:1], bias=gmk[:, 1:2]), [GB[1], cst.r], [gaTe.r])
            proj(0, 0)
            fw.mm(lambda e: e.matmul(GB[0][:, 128:256], lhsT=gaTe.t[:], rhs=gwe.t[:], start=True, stop=True),
                  reads=[gaTe.r, gwe.r], writes=[GB[1]], last=True)
            act(lambda e: e.activation(out=nl.t[:], in_=GB[0][:, 128:256], func=AF.Exp, scale=-1.0),
                [GB[1]], [nl.r])
            act(lambda e: e.activation(out=nl.t[:], in_=nl.t[:], func=AF.Ln, bias=1.0), [nl.r], [nl.r])
            rotary(AB[0][0][:, 0:512], AB[0][1], 8, cA, sA, aqk.t, 0, aqk.r, "pool")
            yield
            proj(1, 1)
            rotary(AB[1][0][:, 0:128], AB[1][1], 2, cA, sA, aqk.t, 512, aqk.r, "pool")
            act(lambda e: e.activation(out=vext[p3].t[:, :, 0:64],
                                       in_=AB[1][0][:, 128:256].rearrange("p (g d) -> p g d", d=64), func=AF.Copy),
                [AB[1][1]], [vext[p3].r])
            rotary(AB[1][0][:, 256:512], AB[1][1], 4, cR, sR, rqk32.t, 0, rqk32.r, "pool")
            yield
            proj(2, 2)
            rotary(AB[2][0][:, 0:256], AB[2][1], 4, cR, sR, rqk32.t, 256, rqk32.r, "pool")
            act(lambda e: e.activation(out=rv[p2].t[:], in_=AB[2][0][:, 256:512], func=AF.Copy), [AB[2][1]], [rv[p2].r])
            pool(lambda e: e.tensor_tensor(out=rqk[p2].t[:].rearrange("p (h d) -> p h d", d=64),
                                           in0=rqk32.t[:].rearrange("p (h d) -> p h d", d=64),
                                           in1=C("dec8").unsqueeze(2).broadcast_to([128, 8, 64]), op=ALU.mult),
                 [rqk32.r, cst.r], [rqk[p2].r])
            yield
            for ci, bank in ((3, 0), (4, 1)):
                proj(ci, bank)
                o = (ci - 3) * 512
                act(lambda e: e.activation(out=th.t[:], in_=AB[bank][0][:, :], func=AF.Tanh, scale=0.5),
                    [AB[bank][1]], [th.r])
                dve(lambda e: e.scalar_tensor_tensor(out=sg[p2].t[:, o:o + 512], in0=th.t[:], scalar=1.0,
                                                     in1=AB[bank][0][:, :], op0=ALU.add, op1=ALU.mult),
                    [th.r, AB[bank][1]], [sg[p2].r])
                yield
            fw.mm(lambda e: e.matmul(GB[0][:, 128:256], lhsT=C("tri_in"), rhs=nl.t[:], start=True, stop=True),
                  reads=[cst.r, nl.r], writes=[GB[1]], last=False)
            fw.mm(lambda e: e.matmul(GB[0][:, 256:384], lhsT=C("tri_rev"), rhs=nl.t[:], start=True, stop=True),
                  reads=[cst.r, nl.r], writes=[GB[1]], last=False)
            fw.mm(lambda e: e.matmul(GB[0][:, 384:386], lhsT=nl.t[:], rhs=C("ncol"), start=True, stop=True),
                  reads=[cst.r, nl.r], writes=[GB[1]], last=True)
            act(lambda e: e.activation(out=EqEr.t[:], in_=GB[0][:, 128:384], func=AF.Exp), [GB[1]], [EqEr.r])
            act(lambda e: e.activation(out=Ek.t[:], in_=GB[0][:, 128:256], func=AF.Exp, scale=-1.0),
                [GB[1]], [Ek.r])
            act(lambda e: e.activation(out=ebl[p2].t[:], in_=GB[0][:, 384:386], func=AF.Exp), [GB[1]], [ebl[p2].r])
            proj(5, 2)
            dve(lambda e: e.scalar_tensor_tensor(out=gqk.t[:, 0:128], in0=AB[2][0][:, 0:128], scalar=float(32 ** -0.5),
                                                 in1=Eq.t[:], op0=ALU.mult, op1=ALU.mult),
                [AB[2][1], Eq.r], [gqk.r])
            dve(lambda e: e.tensor_tensor(out=gqk.t[:, 128:256], in0=AB[2][0][:, 128:256], in1=Ek.t[:], op=ALU.mult),
                [AB[2][1], Ek.r], [gqk.r])
            dve(lambda e: e.tensor_tensor(out=gkh[p2].t[:], in0=AB[2][0][:, 128:256], in1=Er.t[:], op=ALU.mult),
                [AB[2][1], Er.r], [gkh[p2].r])
            act(lambda e: e.activation(out=gv[p2].t[:], in_=AB[2][0][:, 256:512], func=AF.Copy), [AB[2][1]], [gv[p2].r])
            yield
            for j in range(5):
                fw.mm(lambda e: e.transpose(out=ptB[:, j * 128:(j + 1) * 128], in_=aqk.t[:, j * 128:(j + 1) * 128],
                                            identity=identb.t[:]),
                      reads=[aqk.r, identb.r], writes=[R_tB], last=(j == 4))
            src4 = ptB[:, 0:512].rearrange("p (c q) -> p c q", q=128)
            act(lambda e: e.activation(out=qbd[p2][0].t[0:64, :, :], in_=src4[0:64, :, :], func=AF.Copy),
                [R_tB], [qbd[p2][0].r])
            act(lambda e: e.activation(out=kT[p3].t[:], in_=ptB[:, 512:640], func=AF.Copy), [R_tB], [kT[p3].r])
            dve(lambda e: e.tensor_copy(out=qbd[p2][1].t[64:128, :, :], in_=src4[64:128, :, :]), [R_tB], [qbd[p2][1].r])
            yield
            for j in range(4):
                fw.mm(lambda e: e.transpose(out=ptB[:, j * 128:(j + 1) * 128], in_=rqk[p2].t[:, j * 128:(j + 1) * 128],
                                            identity=identb.t[:]),
                      reads=[rqk[p2].r, identb.r], writes=[R_tB], last=False)
            for j in range(2):
                fw.mm(lambda e: e.transpose(out=ptB[:, (4 + j) * 128:(5 + j) * 128],
                                            in_=gqk.t[:, j * 128:(j + 1) * 128], identity=identb.t[:]),
                      reads=[gqk.r, identb.r], writes=[R_tB], last=(j == 1))
            for c in range(2):
                act(lambda e: e.activation(out=rqbd[p2][c].t[0:64, 0, :], in_=ptB[0:64, c * 128:(c + 1) * 128],
                                           func=AF.Copy), [R_tB], [rqbd[p2][c].r])
            act(lambda e: e.activation(out=rkT[p2].t[:].rearrange("p c q -> p (c q)"), in_=ptB[:, 256:512],
                                       func=AF.Copy), [R_tB], [rkT[p2].r])
            act(lambda e: e.activation(out=gqT[p2].t[:], in_=ptB[:, 512:640], func=AF.Copy), [R_tB], [gqT[p2].r])
            for hh in (0, 2):
                act(lambda e: e.activation(out=gqbd[p2].t[32 * hh:32 * hh + 32, hh, :],
                                           in_=ptB[32 * hh:32 * hh + 32, 512:640], func=AF.Copy),
                    [R_tB], [gqbd[p2].r])
            for c in range(2):
                dve(lambda e: e.tensor_copy(out=rqbd[p2][c].t[64:128, 1, :], in_=ptB[64:128, c * 128:(c + 1) * 128]),
                    [R_tB], [rqbd[p2][c].r])
            dve(lambda e: e.tensor_copy(out=rqT[p2].t[:].rearrange("p c q -> p (c q)"), in_=ptB[:, 0:256]),
                [R_tB], [rqT[p2].r])
            for hh in (1, 3):
                dve(lambda e: e.tensor_copy(out=gqbd[p2].t[32 * hh:32 * hh + 32, hh, :],
                                            in_=ptB[32 * hh:32 * hh + 32, 512:640]), [R_tB], [gqbd[p2].r])
            dve(lambda e: e.tensor_copy(out=gkT[p2].t[:], in_=ptB[:, 640:768]), [R_tB], [gkT[p2].r])
            yield

        def genB(t):
            p2 = t % 2
            p3 = t % 3
            pv3 = (t - 1) % 3
            xt = xs[:, t, :]
            Rx = R_x[t]
            blocks = [(p3, mcurb)] + ([(pv3, mprevb)] if t > 0 else [])
            for g in range(2):
                for bi, (kp, mk) in enumerate(blocks):
                    bank = 0
                    fw.mm(lambda e: e.matmul(psc[bank][:, :], lhsT=kT[kp].t[:],
                                             rhs=qbd[p2][g].t[:].rearrange("p c q -> p (c q)"), start=True, stop=False),
                          reads=[kT[kp].r, qbd[p2][g].r], writes=[R_sc[bank]], last=False)
                    fw.mm(lambda e: e.matmul(psc[bank][:, :], lhsT=identb.t[:], rhs=mk.t[:], start=False, stop=True),
                          reads=[identb.r, mk.r], writes=[R_sc[bank]], last=True)
                    act(lambda e: e.activation(out=PT[g][bi].t[:], in_=psc[bank][:, :], func=AF.Exp, scale=0.125),
                        [R_sc[bank]], [PT[g][bi].r])
                yield
            for g in range(2):
                for c in range(4):
                    for bi, (kp, mk) in enumerate(blocks):
                        fw.mm(lambda e: e.matmul(po[g][:, c * 65:(c + 1) * 65], lhsT=PT[g][bi].t[:, c * 128:(c + 1) * 128],
                                                 rhs=vext[kp].t[:, g, :], start=(bi == 0), stop=(bi == len(blocks) - 1)),
                              reads=[PT[g][bi].r, vext[kp].r], writes=[R_o[g]],
                              last=(c == 3 and bi == len(blocks) - 1))
                o4 = po[g][:, 0:260].rearrange("p (c d) -> p c d", d=65)
                dve(lambda e: e.scalar_tensor_tensor(out=stat1.t[:, 0:4], in0=o4[:, :, 64], scalar=2.0,
                                                     in1=esink2.t[:, 4 * g:4 * g + 4], op0=ALU.mult, op1=ALU.add),
                    [R_o[g], esink2.r], [stat1.r])
                dve(lambda e: e.reciprocal(out=stat1.t[:, 4:8], in_=stat1.t[:, 0:4]), [stat1.r], [stat1.r])
                dve(lambda e: e.tensor_tensor(out=an.t[:].rearrange("p (c d) -> p c d", d=64), in0=o4[:, :, 0:64],
                                              in1=stat1.t[:, 4:8].unsqueeze(2).broadcast_to([128, 4, 64]),
                                              op=ALU.mult), [R_o[g], stat1.r], [an.r])
                pool(lambda e: e.tensor_tensor(out=mix.t[:, g * 256:(g + 1) * 256], in0=an.t[:],
                                               in1=sg[p2].t[:, g * 256:(g + 1) * 256], op=ALU.mult),
                     [an.r, sg[p2].r], [mix.r])
                yield

            def head_norm(bank, R_bank, off, gain, rraw, rsqt, rsq_r, stat2):
                act(lambda e: e.activation(out=rraw.t[:], in_=bank[:, 0:256], func=AF.Copy), [R_bank], [rraw.r])
                act(lambda e: e.activation(out=rsqt, in_=bank[:, 0:256], func=AF.Square), [R_bank], [rsq_r])
                dve(lambda e: e.tensor_reduce(out=stat2.t[:, 0:4], in_=rsqt.rearrange("p (h d) -> p h d", d=64),
                                              axis=AX.X, op=ALU.add), [rsq_r], [stat2.r])
                dve(lambda e: e.tensor_scalar(out=stat2.t[:, 0:4], in0=stat2.t[:, 0:4], scalar1=4.0 / 64.0,
                                              scalar2=4.0 * EPS, op0=ALU.mult, op1=ALU.add), [stat2.r], [stat2.r])
                pool(lambda e: e.tensor_tensor(out=stat2.t[:, 4:8], in0=stat2.t[:, 0:4], in1=C("neghalf")[:, 0:4],
                                               op=ALU.pow), [stat2.r, cst.r], [stat2.r])
                dve(lambda e: e.tensor_tensor(out=rraw.t[:].rearrange("p (h d) -> p h d", d=64),
                                              in0=rraw.t[:].rearrange("p (h d) -> p h d", d=64),
                                              in1=stat2.t[:, 4:8].unsqueeze(2).broadcast_to([128, 4, 64]), op=ALU.mult),
                    [rraw.r, stat2.r], [rraw.r])
                if gain is not None:
                    pool(lambda e: e.tensor_tensor(out=rraw.t[:].rearrange("p (h d) -> p h d", d=64),
                                                   in0=rraw.t[:].rearrange("p (h d) -> p h d", d=64),
                                                   in1=gain.t[:].unsqueeze(1).broadcast_to([128, 4, 64]), op=ALU.mult),
                         [rraw.r, gain.r], [rraw.r])
                pool(lambda e: e.tensor_tensor(out=mix.t[:, off:off + 256], in0=rraw.t[:], in1=sg[p2].t[:, off:off + 256],
                                               op=ALU.mult), [rraw.r, sg[p2].r], [mix.r])

            for c in range(2):
                fw.mm(lambda e: e.matmul(psc[0][:, c * 256:(c + 1) * 256], lhsT=rkT[p2].t[:, c, :],
                                         rhs=rqbd[p2][c].t[:].rearrange("p h q -> p (h q)"), start=True, stop=True),
                      reads=[rkT[p2].r, rqbd[p2][c].r], writes=[R_sc[0]], last=(c == 1))
            dve(lambda e: e.tensor_tensor(out=scT.t[:], in0=psc[0][:, :], in1=causalb.t[:], op=ALU.mult),
                [R_sc[0], causalb.r], [scT.r])
            if t == 0:
                dve(lambda e: e.tensor_copy(out=scT.t[0:1, :].rearrange("p (h i) -> p h i", i=128)[:, :, 0],
                                            in_=s00.t[0:1, 0:4]), [scT.r, s00.r], [scT.r])
            yield
            for hh in range(4):
                fw.mm(lambda e: e.matmul(po[0][:, hh * 64:(hh + 1) * 64], lhsT=scT.t[:, hh * 128:(hh + 1) * 128],
                                         rhs=rv[p2].t[:, hh * 64:(hh + 1) * 64], start=(hh == 0), stop=False),
                      reads=[scT.r, rv[p2].r], writes=[R_o[0]], last=False)
            for c in range(2):
                fw.mm(lambda e: e.matmul(po[0][:, c * 128:(c + 1) * 128], lhsT=rqT[p2].t[:, c, :],
                                         rhs=retSb.t[:, c * 128:(c + 1) * 128], start=False, stop=(c == 1)),
                      reads=[rqT[p2].r, retSb.r], writes=[R_o[0]], last=False)
            for c in range(2):
                fw.mm(lambda e: e.matmul(po[0][:, 256 + c * 128:256 + (c + 1) * 128],
                                         lhsT=rqk[p2].t[:, 256 + c * 128:256 + (c + 1) * 128],
                                         rhs=rv[p2].t[:, c * 128:(c + 1) * 128], start=True, stop=True),
                      reads=[rqk[p2].r, rv[p2].r], writes=[R_o[0]], last=(c == 1))
            head_norm(po[0], R_o[0], 512, None, rraw, rsq.t[:], rsq.r, stat2)
            dve(lambda e: e.tensor_tensor(out=dS.t[:], in0=po[0][:, 256:512], in1=C("bd64dec"), op=ALU.mult),
                [R_o[0], cst.r], [dS.r])
            pool(lambda e: e.tensor_tensor(out=retS.t[:].rearrange("p (c n) -> p c n", n=128),
                                           in0=retS.t[:].rearrange("p (c n) -> p c n", n=128),
                                           in1=C("sdec").unsqueeze(2).broadcast_to([128, 2, 128]), op=ALU.mult),
                 [retS.r, cst.r], [retS.r])
            pool(lambda e: e.tensor_tensor(out=retS.t[:], in0=retS.t[:], in1=dS.t[:], op=ALU.add),
                 [retS.r, dS.r], [retS.r])
            act(lambda e: e.activation(out=retSb.t[:], in_=retS.t[:], func=AF.Copy), [retS.r], [retSb.r])
            yield

            fw.mm(lambda e: e.matmul(psc[0][:, :], lhsT=gkT[p2].t[:], rhs=gqbd[p2].t[:].rearrange("p h q -> p (h q)"),
                                     start=True, stop=True),
                  reads=[gkT[p2].r, gqbd[p2].r], writes=[R_sc[0]], last=True)
            dve(lambda e: e.tensor_tensor(out=scT.t[:], in0=psc[0][:, :], in1=causalb.t[:], op=ALU.mult),
                [R_sc[0], causalb.r], [scT.r])
            if t == 0:
                dve(lambda e: e.tensor_copy(out=scT.t[0:1, :].rearrange("p (h i) -> p h i", i=128)[:, :, 0],
                                            in_=s00.t[0:1, 4:8]), [scT.r, s00.r], [scT.r])
            yield
            for hh in range(4):
                fw.mm(lambda e: e.matmul(po[1][:, hh * 64:(hh + 1) * 64], lhsT=scT.t[:, hh * 128:(hh + 1) * 128],
                                         rhs=gv[p2].t[:, hh * 64:(hh + 1) * 64], start=(hh == 0), stop=False),
                      reads=[scT.r, gv[p2].r], writes=[R_o[1]], last=False)
            fw.mm(lambda e: e.matmul(po[1][:, 0:256], lhsT=gqT[p2].t[:], rhs=glaSb.t[:], start=False, stop=True),
                  reads=[gqT[p2].r, glaSb.r], writes=[R_o[1]], last=False)
            fw.mm(lambda e: e.matmul(po[1][:, 256:512], lhsT=gkh[p2].t[:], rhs=gv[p2].t[:], start=True, stop=True),
                  reads=[gkh[p2].r, gv[p2].r], writes=[R_o[1]], last=True)
            head_norm(po[1], R_o[1], 768, ggain, an, big32_.t[:, 0:256], big32_.r, stat3)
            dve(lambda e: e.tensor_tensor(out=dS.t[:], in0=po[1][:, 256:512], in1=C("bd32"), op=ALU.mult),
                [R_o[1], cst.r], [dS.r])
            dve(lambda e: e.scalar_tensor_tensor(out=glaS.t[:], in0=glaS.t[:], scalar=ebl[p2].t[:, 0:1], in1=dS.t[:],
                                                 op0=ALU.mult, op1=ALU.add), [glaS.r, ebl[p2].r, dS.r], [glaS.r])
            act(lambda e: e.activation(out=glaSb.t[:], in_=glaS.t[:], func=AF.Copy), [glaS.r], [glaSb.r])
            yield

            if dbg is not None and l == 0 and t in dbg:
                dump("hT_%d" % t, hT.t[:].rearrange("p k q -> p (k q)"), hT.r, 1024)
                dump("aqk_%d" % t, aqk.t[:], aqk.r, 640)
                dump("rqk_%d" % t, rqk[p2].t[:], rqk[p2].r, 512)
                dump("gqk_%d" % t, gqk.t[:], gqk.r, 256)
                dump("gkh_%d" % t, gkh[p2].t[:], gkh[p2].r, 128)
                dump("sg_%d" % t, sg[p2].t[:], sg[p2].r, 1024)
                dump("nl_%d" % t, nl.t[:], nl.r, 128)
                dump("mix_%d" % t, mix.t[:], mix.r, 1024)
                dump("retS_%d" % t, retS.t[:], retS.r, 256)
                dump("glaS_%d" % t, glaS.t[:], glaS.r, 256)
            for j in range(8):
                fw.mm(lambda e: e.transpose(out=ptB[:, j * 128:(j + 1) * 128], in_=mix.t[:, j * 128:(j + 1) * 128],
                                            identity=identb.t[:]),
                      reads=[mix.r, identb.r], writes=[R_tB], last=(j == 7))
            act(lambda e: e.activation(out=mixT.t[:, 0:4, :].rearrange("p k q -> p (k q)"), in_=ptB[:, 0:512],
                                       func=AF.Copy), [R_tB], [mixT.r])
            dve(lambda e: e.tensor_copy(out=mixT.t[:, 4:8, :].rearrange("p k q -> p (k q)"), in_=ptB[:, 512:1024]),
                [R_tB], [mixT.r])
            yield
            for hf in range(2):
                for kc in range(8):
                    fw.mm(lambda e: e.matmul(po[hf][:, :], lhsT=mixT.t[:, kc, :],
                                             rhs=wout[:, kc, hf * 512:(hf + 1) * 512], start=(kc == 0), stop=(kc == 7)),
                          reads=[mixT.r, R_woutc[kc]], writes=[R_o[hf]], last=(kc == 7))
                yield
            for hf in range(2):
                act(lambda e: e.activation(out=mix.t[:, hf * 512:(hf + 1) * 512], in_=po[hf][:, :], func=AF.Square,
                                           accum_out=stat.t[:, 16 + hf:17 + hf]), [R_o[hf]], [mix.r, stat.r])
            dve(lambda e: e.tensor_tensor(out=stat.t[:, 18:19], in0=stat.t[:, 16:17], in1=stat.t[:, 17:18], op=ALU.add),
                [stat.r], [stat.r])
            dve(lambda e: e.tensor_scalar(out=stat.t[:, 19:20], in0=stat.t[:, 18:19], scalar1=1.0 / D, scalar2=EPS,
                                          op0=ALU.mult, op1=ALU.add), [stat.r], [stat.r])
            pool(lambda e: e.tensor_tensor(out=stat.t[:, 20:21], in0=stat.t[:, 19:20], in1=C("neghalf")[:, 0:1],
                                           op=ALU.pow), [stat.r, cst.r], [stat.r])
            for hf in range(2):
                dve(lambda e: e.scalar_tensor_tensor(out=big32[hf].t[:], in0=po[hf][:, :],
                                                     scalar=stat.t[:, 20:21], in1=gpg.t[:, hf * 512:(hf + 1) * 512],
                                                     op0=ALU.mult, op1=ALU.mult),
                    [R_o[hf], stat.r, gpg.r], [big32[hf].r])
                dve(lambda e: e.tensor_tensor(out=xt[:, hf * 512:(hf + 1) * 512], in0=xt[:, hf * 512:(hf + 1) * 512],
                                              in1=big32[hf].t[:], op=ALU.add), [Rx, big32[hf].r], [Rx])
            if l == L - 1:
                out_toks.append(fw.dma("sp", out_d[t * 128:(t + 1) * 128, :], xt, reads=[Rx]))
            else:
                act(lambda e: e.activation(out=mix.t[:], in_=xt, func=AF.Square, accum_out=ssq[l + 1].t[:, t:t + 1]),
                    [Rx], [mix.r, ssq[l + 1].r])
            yield

        if l == 0:
            for t_ in range(2, n_tiles):
                fw.dma("sp", xs[:, t_, :], x_d[t_ * 128:(t_ + 1) * 128, :], writes=[R_x[t_]], nobar=True)
        for _ in genA(0):
            pass
        for t in range(n_tiles):
            gb = genB(t)
            ga_ = genA(t + 1) if t + 1 < n_tiles else iter(())
            alive_a = alive_b = True
            while alive_a or alive_b:
                if alive_b:
                    try:
                        next(gb)
                    except StopIteration:
                        alive_b = False
                if alive_a:
                    try:
                        next(ga_)
                    except StopIteration:
                        alive_a = False
        if l + 1 < L:
            load_weights(l + 1, early=True)
        if dbg is not None and l == 0:
            dump("gmF", gmF.t[:], gmF.r, 8)
            dump("shF", shF.t[:], shF.r, 8)
            dump("gpg", gpg.t[:], gpg.r, 1024)
            dump("tabs", tabs.t[:].rearrange("p a t f -> p (a t f)"), tabs.r, 4 * NT * 32)
        fw.barrier()

    fw.flush()
    e = fw.engs["sp"]
    for s in fw.dsems + fw.dsems_nb + [x[0] for x in fw.dsems_sw]:
        if s.count:
            e.wait((s, s.count, "dma"))
    fw.close()
    return nc


_NC_CACHE = {}


def _prep_shared(w_mod, b_mod, pre_norm_gain, post_norm_gain, w_in, attn_sinks, gla_gate_w, gla_gate_b,
                 gla_norm_gain, w_out):
    L = w_in.shape[0]
    f = lambda a: np.ascontiguousarray(np.asarray(a, dtype=np.float32))
    b_mod = np.asarray(b_mod, np.float32)
    gwe = np.zeros((L, 128, 128), np.float32)
    gwe[:, 0:16, :] = np.asarray(gla_gate_w, np.float32)
    gwe[:, 16, :] = np.asarray(gla_gate_b, np.float32)
    return {
        "w_mod": f(w_mod),
        "bmodF": f(b_mod.reshape(L, 24, 128).transpose(0, 2, 1)),
        "bgate": f(b_mod[:, 2048:3072]),
        "pregF": f(np.asarray(pre_norm_gain, np.float32).reshape(L, 8, 128).transpose(0, 2, 1)),
        "postg": f(post_norm_gain),
        "w_in": f(np.asarray(w_in, np.float32)[:, :, _PERM]),
        "w_qk32": f(np.asarray(w_in, np.float32)[:, :, _QK32]),
        "sinks": f(attn_sinks),
        "gwe": gwe,
        "ggain": f(gla_norm_gain),
        "w_out": f(w_out),
        "consts": _CBLOB,
        "cmask": _CMASK,
    }


def kernel(x, c, positions, w_mod, b_mod, pre_norm_gain, post_norm_gain, w_in,
           attn_sinks, gla_gate_w, gla_gate_b, gla_norm_gain, w_out):
    x = np.asarray(x, np.float32)
    c = np.asarray(c, np.float32)
    positions = np.asarray(positions, np.int32)
    B = x.shape[0]
    shared = _prep_shared(w_mod, b_mod, pre_norm_gain, post_norm_gain, w_in, attn_sinks, gla_gate_w,
                          gla_gate_b, gla_norm_gain, w_out)
    if "nc" not in _NC_CACHE:
        _NC_CACHE["nc"] = build(n_layers=2)
    nc = _NC_CACHE["nc"]
    in_maps = []
    for b in range(B):
        m = dict(shared)
        m["x"] = np.ascontiguousarray(x[b])
        m["c"] = np.ascontiguousarray(c[b].reshape(8, 128).T)
        m["pos"] = np.ascontiguousarray(positions[b].reshape(NT, 128).T)
        in_maps.append(m)
    res = run_bass_kernel_spmd(nc, in_maps, core_ids=list(range(B)))
    return np.stack([np.asarray(r["out"], np.float32) for r in res.results], axis=0)
```

```python
import os
import numpy as np
import concourse.bass as bass
import concourse.mybir as mybir
from concourse.bass_utils import run_bass_kernel_spmd

F32 = mybir.dt.float32
BF16 = mybir.dt.bfloat16
I32 = mybir.dt.int32
AF = mybir.ActivationFunctionType
ALU = mybir.AluOpType
AX = mybir.AxisListType

S = 2048
D = 1024
NT = 16
DIN = 3088
EPS = 1e-6
NEG = -30000.0


class Sem:
    def __init__(self, h, name):
        self.h = h
        self.count = 0
        self.name = name


class Res:
    def __init__(self, name, excl=False):
        self.name = name
        self.excl = excl
        self.w = None
        self.r = {}


_SNAP = {}


class Eng:
    def __init__(self, name, h, sem):
        self.name = name
        self.h = h
        self.sem = sem
        self.seen = {}

    def wait(self, tok):
        if tok is None:
            return
        if tok[0] == "PENDING":
            if self.name == "pe":
                return
            raise RuntimeError("wait on pending PE token by " + self.name)
        s, v, _ = tok
        if self.seen.get(id(s), 0) < v:
            self.h.wait_ge(s.h, v)
            self.seen[id(s)] = v
        snap = _SNAP.get((id(s), v))
        if snap:
            for k, val in snap.items():
                if self.seen.get(k, 0) < val:
                    self.seen[k] = val


class _Probe:
    def __init__(self):
        self.n = 0
        self.opname = ""
        self.fp32 = False
        self.accum = False

    def __getattr__(self, name):
        def f(*a, **k):
            out = k.get("out", a[0] if a else None)
            n = 1
            for d in out.shape[1:]:
                n *= d
            self.n = n
            self.opname = name
            self.call = (name, a, k)
            lt = k.get("lhsT", None)
            self.fp32 = lt is not None and lt.dtype == F32
            self.accum = k.get("accum_out", None) is not None
            return self
        return f

    def then_inc(self, *a, **k):
        return self

    def replay(self):
        name, a, k = self.call
        return lambda e: getattr(e, name)(*a, **k)


class Unit:
    __slots__ = ("kind", "eng", "fns", "reads", "writes", "dur", "busy", "idx", "args", "deps", "nsucc")


def _est(ename, pr):
    n = pr.n
    if ename == "pe":
        if pr.opname == "transpose":
            return 108.0
        return (max(64, n) / 2.4 + 6.0) * (4.0 if pr.fp32 else 1.0)
    if ename == "act":
        return 190.0 + n / 1.2 + (90.0 if pr.accum else 0.0)
    if ename == "dve":
        if pr.opname == "reciprocal":
            return 80.0 + 8.0 * n
        return 70.0 + n * 1.05
    if ename == "pool":
        if pr.opname == "tensor_tensor" and n <= 16:
            return 750.0
        return 150.0 + n * 2.3
    return 100.0


class FW:
    def __init__(self, nc, ndma_sems=16):
        self.rec = None
        self.cur_pe = None
        self.nc = nc
        self._ctx = []
        self.engs = {}
        for name, h in (("pe", nc.tensor), ("dve", nc.vector), ("act", nc.scalar),
                        ("pool", nc.gpsimd), ("sp", nc.sync)):
            self.engs[name] = Eng(name, h, self._sem("s_" + name))
        self.dsems = [self._sem("d%d" % i) for i in range(ndma_sems)]
        self.dnext = 0
        self.dsems_nb = [self._sem("n%d" % i) for i in range(8)]
        self.dnext_nb = 0
        self.dsems_sw = []
        self.pe_pending = []

    def _sem(self, name):
        cm = self.nc.semaphore(name)
        h = cm.__enter__()
        self._ctx.append(cm)
        return Sem(h, name)

    def barrier(self):
        was = self.rec is not None
        self.flush()
        self._barrier()
        if was:
            self.start_recording()

    def _barrier(self):
        assert not self.pe_pending
        toks = [(e.sem, e.sem.count, e.name) for e in self.engs.values() if e.sem.count]
        toks += [(s, s.count, "dma") for s in self.dsems if s.count]
        toks += [(s, s.count, "dma") for s, nb in self.dsems_sw if s.count and not nb]
        for e in self.engs.values():
            for t in toks:
                if t[2] != e.name:
                    e.wait(t)

    def sb(self, name, shape, dt):
        n = 1
        for d in shape[1:]:
            n *= d
        self.nbytes = getattr(self, "nbytes", 0) + n * (2 if dt == BF16 else 4)
        cm = self.nc.sbuf_tensor("sb_" + name, list(shape), dt)
        t = cm.__enter__()
        self._ctx.append(cm)
        return t

    def ps(self, name, shape, dt):
        cm = self.nc.psum_tensor(name, list(shape), dt)
        t = cm.__enter__()
        self._ctx.append(cm)
        return t

    def close(self):
        for cm in reversed(self._ctx):
            cm.__exit__(None, None, None)
        self._ctx = []

    def _acq(self, e, reads, writes):
        for r in reads:
            e.wait(r.w)
            if r.excl:
                for en, t in r.r.items():
                    if en != e.name:
                        e.wait(t)
        for w in writes:
            e.wait(w.w)
            for en, t in w.r.items():
                e.wait(t)

    def _rel(self, ename, tok, reads, writes):
        for r in reads:
            r.r[ename] = tok
        for w in writes:
            w.w = tok
            w.r = {}

    def start_recording(self):
        self.rec = []
        self.cur_pe = None

    def flush(self):
        if self.rec is None:
            return
        assert self.cur_pe is None
        units = self.rec
        self.rec = None
        n = len(units)
        lastw = {}
        readers = {}
        succ = [[] for _ in range(n)]
        for i, u in enumerate(units):
            u.idx = i
            deps = set()
            for r in u.reads:
                k = id(r)
                if r.excl:
                    if k in lastw:
                        deps.add(lastw[k])
                    deps.update(readers.get(k, ()))
                elif k in lastw:
                    deps.add(lastw[k])
            for w in u.writes:
                k = id(w)
                if k in lastw:
                    deps.add(lastw[k])
                deps.update(readers.get(k, ()))
            deps.discard(i)
            u.deps = deps
            for d in deps:
                succ[d].append(i)
            for r in u.reads:
                k = id(r)
                if r.excl:
                    lastw[k] = i
                    readers[k] = []
                else:
                    readers.setdefault(k, []).append(i)
            for w in u.writes:
                k = id(w)
                lastw[k] = i
                readers[k] = []
        LAT = float(os.environ.get("SCHED_LAT", "250"))
        blevel = [0.0] * n
        for i in range(n - 1, -1, -1):
            u = units[i]
            m = 0.0
            for j in succ[i]:
                v = blevel[j] + (LAT if units[j].eng != u.eng else 40.0)
                if v > m:
                    m = v
            blevel[i] = u.dur + m
        PEB = float(os.environ.get("SCHED_PEB", "0"))
        if PEB:
            for i in range(n):
                if units[i].eng == "pe":
                    blevel[i] += PEB
        ndep = [len(u.deps) for u in units]
        finish = [0.0] * n
        efree = {}
        ready = [i for i in range(n) if ndep[i] == 0]
        order = []
        SLACK = float(os.environ.get("SCHED_SLACK", "120"))
        while ready:
            ests = []
            mn = None
            for i in ready:
                u = units[i]
                st = efree.get(u.eng, 0.0)
                for d in u.deps:
                    f = finish[d] + (LAT if units[d].eng != u.eng else 40.0)
                    if f > st:
                        st = f
                ests.append(st)
                if mn is None or st < mn:
                    mn = st
            best = None
            bs = None
            bl = -1.0
            for i, st in zip(ready, ests):
                if st <= mn + SLACK and (blevel[i] > bl + 1e-9 or (abs(blevel[i] - bl) <= 1e-9 and i < best)):
                    bl = blevel[i]
                    best = i
                    bs = st
            ready.remove(best)
            u = units[best]
            if getattr(self, "diag", None) is not None and bs > efree.get(u.eng, 0.0) + 1.0:
                bd = max(u.deps, key=lambda d: finish[d] + (LAT if units[d].eng != u.eng else 40.0)) if u.deps else None
                if bd is not None:
                    ud = units[bd]
                    shared = [r.name for r in list(u.reads) + list(u.writes) if r in ud.reads or r in ud.writes]
                    key = (u.eng, shared[0] if shared else "?", ud.eng)
                    self.diag[key] = self.diag.get(key, 0.0) + bs - efree.get(u.eng, 0.0)
            efree[u.eng] = bs + u.busy
            finish[best] = bs + u.dur
            order.append(best)
            for j in succ[best]:
                ndep[j] -= 1
                if ndep[j] == 0:
                    ready.append(j)
        assert len(order) == n
        self.sched_span = getattr(self, "sched_span", 0.0) + max(finish) if n else 0.0
        for i in order:
            u = units[i]
            if u.kind == "op":
                self.op(u.eng, u.fns[0], u.reads, u.writes)
            elif u.kind == "mm":
                for j, (fn, rd, wr) in enumerate(u.fns):
                    self.mm(fn, rd, wr, last=(j == len(u.fns) - 1))
            else:
                self.dma(u.eng, u.args[0], u.args[1], u.reads, u.writes, nobar=u.args[2])

    def _record(self, kind, eng, fns, reads, writes, dur, busy, args=None):
        u = Unit()
        u.kind = kind
        u.eng = eng
        u.fns = fns
        u.reads = tuple(reads)
        u.writes = tuple(writes)
        u.dur = dur
        u.busy = busy
        u.args = args
        self.rec.append(u)

    def op(self, ename, fn, reads=(), writes=()):
        if self.rec is not None:
            pr = _Probe()
            fn(pr)
            d = _est(ename, pr)
            self._record("op", ename, [pr.replay()], reads, writes, d, d)
            return None
        e = self.engs[ename]
        self._acq(e, reads, writes)
        inst = fn(e.h)
        e.sem.count += 1
        inst.then_inc(e.sem.h, 1)
        _SNAP[(id(e.sem), e.sem.count)] = dict(e.seen)
        self._rel(ename, (e.sem, e.sem.count, ename), reads, writes)
        return inst

    def mm(self, fn, reads=(), writes=(), last=False):
        if self.rec is not None:
            pr = _Probe()
            fn(pr)
            d = _est("pe", pr)
            if self.cur_pe is None:
                self.cur_pe = [[], [], [], 0.0]
            g = self.cur_pe
            g[0].append((pr.replay(), tuple(reads), tuple(writes)))
            for r in reads:
                if r not in g[1]:
                    g[1].append(r)
            for w in writes:
                if w not in g[2]:
                    g[2].append(w)
            g[3] += d
            if last:
                self.cur_pe = None
                self._record("mm", "pe", g[0], g[1], g[2], g[3] + 60.0, g[3])
            return None
        e = self.engs["pe"]
        self._acq(e, reads, writes)
        inst = fn(e.h)
        self.pe_pending.append((tuple(reads), tuple(writes)))
        if last:
            e.sem.count += 1
            inst.then_inc(e.sem.h, 1)
            _SNAP[(id(e.sem), e.sem.count)] = dict(e.seen)
            tok = (e.sem, e.sem.count, "pe")
            for rd, wr in self.pe_pending:
                self._rel("pe", tok, rd, wr)
            self.pe_pending = []
        else:
            for w in writes:
                w.w = ("PENDING",)
                w.r = {}
            for r in reads:
                r.r["pe"] = ("PENDING",)
        return inst

    def dma(self, qname, out, in_, reads=(), writes=(), nobar=False):
        if self.rec is not None:
            n = 1
            for d_ in out.shape:
                n *= d_
            self._record("dma", qname, None, reads, writes, 2500.0 + n * 4 / 150.0, 120.0, (out, in_, nobar))
            return None
        e = self.engs[qname]
        self._acq(e, reads, writes)
        if qname == "pool":
            s = self._sem("w%d" % len(self.dsems_sw))
            self.dsems_sw.append((s, nobar))
        elif nobar:
            s = self.dsems_nb[self.dnext_nb]
            self.dnext_nb = (self.dnext_nb + 1) % len(self.dsems_nb)
        else:
            s = self.dsems[self.dnext]
            self.dnext = (self.dnext + 1) % len(self.dsems)
        if s.count:
            e.wait((s, s.count, "dma"))
        inst = e.h.dma_start(out=out, in_=in_)
        s.count += 16
        inst.then_inc(s.h, 16)
        _SNAP[(id(s), s.count)] = dict(e.seen)
        tok = (s, s.count, "dma:" + s.name)
        self._rel("dma:" + s.name, tok, reads, writes)
        return tok


class T:
    def __init__(self, fw, name, shape, dt, view=None):
        self.t = fw.sb(name, shape, dt) if view is None else view
        self.r = Res(name)


class Arena:
    def __init__(self, t, nwords):
        self.t = t
        self.n = nwords
        self.o = 0

    def reset(self):
        self.o = 0

    def get(self, name, shape, dt):
        n = 1
        for d in shape[1:]:
            n *= d
        words = n if dt != BF16 else (n + 1) // 2
        assert self.o + words <= self.n, (name, self.o, words, self.n)
        v = self.t[:, self.o:self.o + words]
        self.o += words
        if dt != F32:
            v = v.bitcast(dt)
        if len(shape) == 3:
            v = v.rearrange("p (a b) -> p a b", b=shape[2])
        elif len(shape) == 4:
            v = v.rearrange("p (a b c) -> p a b c", b=shape[2], c=shape[3])
        return T(None, name, shape, dt, view=v)

    def __getitem__(self, k):
        return self.t[k]


def _consts():
    p = np.arange(128)
    cols = {}
    ident = np.eye(128, dtype=np.float32)
    cols["ident"] = ident
    k = p[:, None]
    q = p[None, :]
    mcur = np.where(k <= q, 0.0, NEG).astype(np.float32)
    mprev = np.where(k > q, 0.0, NEG).astype(np.float32)
    causal = (k <= q).astype(np.float32)
    global _CMASK
    _CMASK = np.ascontiguousarray(np.concatenate([mcur, mprev, causal], axis=1))
    cols["tri_in"] = causal * (-1.0 / 16.0)
    cols["tri_rev"] = (k > q).astype(np.float32) * (-1.0 / 16.0)
    cols["ncol"] = np.full((128, 2), -1.0 / 16.0, np.float32)
    h = np.arange(4, dtype=np.float32)
    log_g = np.log(1.0 - 2.0 ** (-5.0 - h)).astype(np.float32)
    i1 = (p[:, None] + 1).astype(np.float32)
    qdec = np.exp(log_g[None, :] * i1)
    kdec = np.exp(-log_g[None, :] * i1) / 8.0
    cols["dec8"] = np.concatenate([qdec, kdec], axis=1).astype(np.float32)
    cdec = np.exp(log_g * 128.0)
    sdec = np.zeros((128, 2), np.float32)
    for c in range(2):
        sdec[0:64, c] = cdec[2 * c]
        sdec[64:128, c] = cdec[2 * c + 1]
    cols["sdec"] = sdec
    bd64 = np.zeros((128, 128), np.float32)
    bd64[0:64, 0:64] = 1.0
    bd64[64:128, 64:128] = 1.0
    cols["bd64dec"] = np.concatenate([bd64 * sdec[:, 0:1], bd64 * sdec[:, 1:2]], axis=1)
    bd32 = np.zeros((128, 256), np.float32)
    for hh in range(4):
        bd32[32 * hh:32 * hh + 32, 64 * hh:64 * hh + 64] = 1.0
    cols["bd32"] = bd32
    gm = np.zeros((128, 2), np.float32)
    gm[0:16, 0] = 1.0
    gm[16, 1] = 1.0
    cols["gamask"] = gm
    fa = (10000.0 ** (-np.arange(0, 64, 2, dtype=np.float32) / 64.0)).astype(np.float32)
    fr = (1.0 / (10000.0 ** np.linspace(0.0, 1.0, 32, dtype=np.float32))).astype(np.float32)
    cols["freq"] = np.tile(np.concatenate([fa, fr])[None, :], (128, 1)).astype(np.float32)
    cols["neghalf"] = np.full((128, 8), -0.5, np.float32)
    cols["neghalf16"] = np.full((128, 16), -0.5, np.float32)
    off = {}
    o = 0
    parts = []
    for kname, v in cols.items():
        off[kname] = (o, v.shape[1])
        o += v.shape[1]
        parts.append(v.astype(np.float32))
    return np.ascontiguousarray(np.concatenate(parts, axis=1)), off


_CBLOB, _COFF = _consts()
NCONST = _CBLOB.shape[1]

_AQ = np.concatenate([np.arange(64 * hh, 64 * hh + 64) for hh in (0, 4, 1, 5, 2, 6, 3, 7)])
_r = lambda a, b: np.arange(a, b)
_PERM = np.concatenate([
    _r(3072, 3088),
    _AQ,
    _r(512, 640), _r(640, 768), _r(1280, 1536),
    _r(1536, 1792), _r(1792, 2048),
    _r(768, 1280),
    _r(2048, 2304), _r(2816, 3072),
    _r(2304, 2432), _r(2432, 2560), _r(2560, 2816),
])
NW = _PERM.shape[0]
_QK32 = np.concatenate([_r(1280, 1536), _r(2304, 2432), _r(1536, 1792), _r(2432, 2560)])
GOFF = 0
COFFS = [16 + 512 * i for i in range(6)]


def build(n_layers=2, n_tiles=NT, dbg=None, sched=True):
    nc = bass.Bass("TRN2", target_bir_lowering=False)
    _SNAP.clear()
    fw = FW(nc)
    if sched:
        fw.start_recording()
    L = n_layers

    def din(name, shape, dt=F32):
        return nc.dram_tensor(name, list(shape), dt, kind="ExternalInput").ap()

    x_d = din("x", [S, D])
    c_d = din("c", [128, 8])
    pos_d = din("pos", [128, NT], I32)
    wmod_d = din("w_mod", [L, D, 3 * D])
    bmodF_d = din("bmodF", [L, 128, 24])
    bgate_d = din("bgate", [L, D])
    pregF_d = din("pregF", [L, 128, 8])
    postg_d = din("postg", [L, D])
    win_d = din("w_in", [L, D, NW])
    sink_d = din("sinks", [L, 8])
    gwe_d = din("gwe", [L, 128, 128])
    ggain_d = din("ggain", [L, 64])
    wout_d = din("w_out", [L, D, D])
    wqk_d = din("w_qk32", [L, D, 768])
    const_d = din("consts", [128, NCONST])
    cmask_d = din("cmask", [128, 384])
    out_d = nc.dram_tensor("out", [S, D], F32, kind="ExternalOutput").ap()

    xs = fw.sb("xs", [128, NT, D], F32)
    R_x = [Res("x%d" % t) for t in range(NT)]
    win = fw.sb("win", [128, 8, NW], BF16)
    R_winc = [[Res("win%d_%d" % (kc, hf)) for hf in range(2)] for kc in range(8)]
    wout = fw.sb("wout", [128, 8, D], BF16)
    R_woutc = [Res("wout%d" % kc) for kc in range(8)]
    cst = T(fw, "cst", [128, NCONST], F32)

    def C(name):
        o, n = _COFF[name]
        return cst.t[:, o:o + n]

    identb = T(fw, "identb", [128, 128], BF16)
    mcurb = T(fw, "mcurb", [128, 512], BF16)
    mprevb = T(fw, "mprevb", [128, 512], BF16)
    gpg = T(fw, "gpg", [128, D], F32)
    gmF = T(fw, "gmF", [128, 8], F32)
    shF = T(fw, "shF", [128, 8], F32)
    esink2 = T(fw, "esink2", [128, 8], F32)
    gwe = T(fw, "gwe", [128, 128], F32)
    ggain = T(fw, "ggain", [128, 64], F32)
    tabs = T(fw, "tabs", [128, 4, NT, 32], F32)
    small = T(fw, "small", [128, 64], F32)
    retS = T(fw, "retS", [128, 256], F32)
    retSb = T(fw, "retSb", [128, 256], BF16)
    glaS = T(fw, "glaS", [128, 256], F32)
    glaSb = T(fw, "glaSb", [128, 256], BF16)
    big32_ = T(fw, "big32", [128, 512], F32)
    big32 = [big32_, big32_]
    xn = T(fw, "xn", [128, D], BF16)
    hT = T(fw, "hT", [128, 8, 128], BF16)
    gaTe = T(fw, "gaTe", [128, 128], F32)
    nl = T(fw, "nl", [128, 128], F32)
    Eq = T(fw, "Eq", [128, 128], F32)
    Ek = T(fw, "Ek", [128, 128], F32)
    Er = T(fw, "Er", [128, 128], F32)
    aqk = T(fw, "aqk", [128, 640], BF16)
    rqk32 = T(fw, "rqk32", [128, 512], F32)
    gqk = T(fw, "gqk", [128, 256], BF16)
    scT = T(fw, "scT", [128, 512], BF16)
    stat = T(fw, "stat", [128, 32], F32)
    statA = T(fw, "statA", [128, 8], F32)
    h0F = T(fw, "h0F", [128, 16], F32)
    e0r = T(fw, "e0r", [128, 2], F32)
    s00 = T(fw, "s00", [128, 8], F32)
    ssq = [T(fw, "ssq%d" % i, [128, NT], F32) for i in range(n_layers)]
    rstdT = [T(fw, "rstd%d" % i, [128, NT], F32) for i in range(n_layers)]
    stat1 = T(fw, "stat1", [128, 8], F32)
    stat2 = T(fw, "stat2", [128, 8], F32)
    stat3 = T(fw, "stat3", [128, 8], F32)
    causalb = T(fw, "causalb", [128, 512], BF16)
    kT = [T(fw, "kT%d" % i, [128, 128], BF16) for i in range(3)]
    vext = [T(fw, "vext%d" % i, [128, 2, 65], BF16) for i in range(3)]
    ebl = [T(fw, "ebl%d" % i, [128, 2], F32) for i in range(2)]
    rqk = [T(fw, "rqk%d" % i, [128, 512], BF16) for i in range(2)]
    gkh = [T(fw, "gkh%d" % i, [128, 128], BF16) for i in range(2)]
    qbd = [[T(fw, "qbd%d%d" % (i, g), [128, 4, 128], BF16) for g in range(2)] for i in range(2)]
    rqbd = [[T(fw, "rqbd%d%d" % (i, c), [128, 2, 128], BF16) for c in range(2)] for i in range(2)]
    rqT = [T(fw, "rqT%d" % i, [128, 2, 128], BF16) for i in range(2)]
    rkT = [T(fw, "rkT%d" % i, [128, 2, 128], BF16) for i in range(2)]
    gqbd = [T(fw, "gqbd%d" % i, [128, 4, 128], BF16) for i in range(2)]
    gqT = [T(fw, "gqT%d" % i, [128, 128], BF16) for i in range(2)]
    gkT = [T(fw, "gkT%d" % i, [128, 128], BF16) for i in range(2)]
    rv = [T(fw, "rv%d" % i, [128, 256], BF16) for i in range(2)]
    gv = [T(fw, "gv%d" % i, [128, 256], BF16) for i in range(2)]
    sg1 = T(fw, "sg1", [128, D], BF16)
    rotA2 = T(fw, "rotA2", [128, 256], F32)
    rotB2 = T(fw, "rotB2", [128, 256], F32)
    AW = 5120
    arena_t = fw.sb("arena", [128, AW], F32)
    arA = Arena(arena_t, AW)
    wst = [arA.get("wst%d" % i, [128, D], F32) for i in range(2)]
    cb = arA.get("cb", [128, 8, 128], F32)
    ang = arA.get("ang", [128, NT, 32], F32)
    tu = arA.get("tu", [128, NT, 32], F32)
    tki = arA.get("tki", [128, NT, 32], I32)
    ty = arA.get("ty", [128, NT, 32], F32)
    arA2 = Arena(arena_t, AW)
    arA2.o = 3072
    wst_extra = [arA2.get("wst%d" % i, [128, D], F32) for i in (2, 3)]
    cmk = T(None, "cmk", [128, 384], F32, view=rqk32.t[:, 0:384])
    cmk.r = rqk32.r
    arB = Arena(arena_t, AW)
    rotA1 = arB.get("rotA", [128, 640], F32)
    rotB1 = arB.get("rotB", [128, 640], F32)
    th = arB.get("th", [128, 512], BF16)
    sg0 = arB.get("sg", [128, D], BF16)
    sg = [sg0, sg1]
    PT = [[arB.get("PT%d%d" % (g, b), [128, 512], BF16) for b in range(2)] for g in range(2)]
    an = arB.get("an", [128, 256], F32)
    rraw = arB.get("rraw", [128, 256], F32)
    rsq = arB.get("rsq", [128, 256], F32)
    mix = arB.get("mix", [128, D], BF16)
    mixT = arB.get("mixT", [128, 8, 128], BF16)
    dS = arB.get("dS", [128, 256], F32)

    pp = [fw.ps("pp%d" % i, [128, 512], F32) for i in range(2)]
    R_pp = [Res("pp%d" % i, excl=True) for i in range(2)]
    ptA = fw.ps("ptA", [128, 1024], BF16)
    R_tA = Res("ptA", excl=True)
    ptB = fw.ps("ptB", [128, 1024], BF16)
    R_tB = Res("ptB", excl=True)
    psc = [fw.ps("psc%d" % i, [128, 512], F32) for i in range(2)]
    R_sc = [Res("psc%d" % i, excl=True) for i in range(2)]
    po = [fw.ps("po%d" % i, [128, 512], F32) for i in range(2)]
    R_o = [Res("po%d" % i, excl=True) for i in range(2)]

    dve = lambda fn, reads=(), writes=(): fw.op("dve", fn, reads, writes)
    act = lambda fn, reads=(), writes=(): fw.op("act", fn, reads, writes)
    pool = lambda fn, reads=(), writes=(): fw.op("pool", fn, reads, writes)
    dbg_outs = {}

    def dump(name, ap, R, n):
        if dbg is None:
            return
        o = nc.dram_tensor("dbg_" + name, [128, n], F32, kind="ExternalOutput").ap()
        for a0 in range(0, n, 512):
            a1 = min(n, a0 + 512)
            dve(lambda e: e.tensor_copy(out=big32_.t[:, 0:a1 - a0], in_=ap[:, a0:a1]), [R], [big32_.r])
            fw.dma("sp", o[:, a0:a1], big32_.t[:, 0:a1 - a0], reads=[big32_.r])

    def load_weights(l, early=False):
        for kc in range(8):
            for hf in range(2):
                c0 = hf * (NW // 2)
                fw.dma("pool", win[:, kc, c0:c0 + NW // 2],
                       win_d[l, kc * 128:(kc + 1) * 128, c0:c0 + NW // 2], writes=[R_winc[kc][hf]], nobar=early)
        for kc in range(8):
            fw.dma("pool", wout[:, kc, :], wout_d[l, kc * 128:(kc + 1) * 128, :], writes=[R_woutc[kc]], nobar=True)

    fw.dma("sp", cst.t[:], const_d, writes=[cst.r])
    fw.dma("sp", xs[:, 0, :], x_d[0:128, :], writes=[R_x[0]])
    load_weights(0)
    if n_tiles > 1:
        fw.dma("sp", xs[:, 1, :], x_d[128:256, :], writes=[R_x[1]])
    dve(lambda e: e.tensor_copy(out=identb.t[:], in_=C("ident")), [cst.r], [identb.r])
    fw.dma("sp", cmk.t[:], cmask_d, writes=[cmk.r])
    for i_, dst_ in enumerate((mcurb, mprevb, causalb)):
        dve(lambda e: e.tensor_copy(out=dst_.t[:].rearrange("p (r q) -> p r q", q=128),
                                    in_=cmk.t[:, 128 * i_:128 * (i_ + 1)].unsqueeze(1).broadcast_to([128, 4, 128])),
            [cmk.r], [dst_.r])
    for i_ in range(2):
        for g in range(2):
            pool(lambda e: e.memset(qbd[i_][g].t[:], 0.0), [], [qbd[i_][g].r])
            pool(lambda e: e.memset(rqbd[i_][g].t[:], 0.0), [], [rqbd[i_][g].r])
        pool(lambda e: e.memset(gqbd[i_].t[:], 0.0), [], [gqbd[i_].r])
    for i_ in range(3):
        pool(lambda e: e.memset(vext[i_].t[:], 1.0), [], [vext[i_].r])

    posi = T(fw, "posi", [128, NT], I32)
    posf = T(fw, "posf", [128, NT], F32)
    fw.dma("sp", posi.t[:], pos_d, writes=[posi.r])
    dve(lambda e: e.tensor_copy(out=posf.t[:], in_=posi.t[:]), [posi.r], [posf.r])
    TWO_PI = float(2 * np.pi)
    PI = float(np.pi)
    C1_2PI = float(np.float32(2 * np.pi))
    C2_2PI = float(np.float32(2 * np.pi - C1_2PI))

    def wrap(dst, src, shift):
        dve(lambda e: e.tensor_scalar(out=dst.t[:], in0=src.t[:], scalar1=float(shift), scalar2=None,
                                      op0=ALU.add), [src.r], [dst.r])
        dve(lambda e: e.tensor_scalar(out=tu.t[:], in0=dst.t[:], scalar1=PI, scalar2=-TWO_PI,
                                      op0=ALU.is_gt, op1=ALU.mult), [dst.r], [tu.r])
        dve(lambda e: e.tensor_tensor(out=dst.t[:], in0=dst.t[:], in1=tu.t[:], op=ALU.add),
            [dst.r, tu.r], [dst.r])
        dve(lambda e: e.tensor_scalar(out=tu.t[:], in0=dst.t[:], scalar1=-PI, scalar2=TWO_PI,
                                      op0=ALU.is_lt, op1=ALU.mult), [dst.r], [tu.r])
        dve(lambda e: e.tensor_tensor(out=dst.t[:], in0=dst.t[:], in1=tu.t[:], op=ALU.add),
            [dst.r, tu.r], [dst.r])

    fo, _ = _COFF["freq"]
    for which in range(2):
        fr_ap = cst.t[:, fo + 32 * which: fo + 32 * which + 32].unsqueeze(1).broadcast_to([128, NT, 32])
        pos_ap = posf.t[:].unsqueeze(2).broadcast_to([128, NT, 32])
        dve(lambda e: e.tensor_tensor(out=ang.t[:], in0=pos_ap, in1=fr_ap, op=ALU.mult),
            [posf.r, cst.r], [ang.r])
        dve(lambda e: e.tensor_scalar(out=tu.t[:], in0=ang.t[:], scalar1=float(1.0 / TWO_PI), scalar2=None,
                                      op0=ALU.mult), [ang.r], [tu.r])
        dve(lambda e: e.tensor_copy(out=tki.t[:], in_=tu.t[:]), [tu.r], [tki.r])
        dve(lambda e: e.tensor_copy(out=tu.t[:], in_=tki.t[:]), [tki.r], [tu.r])
        dve(lambda e: e.scalar_tensor_tensor(out=ty.t[:], in0=tu.t[:], scalar=-C1_2PI, in1=ang.t[:],
                                             op0=ALU.mult, op1=ALU.add), [tu.r, ang.r], [ty.r])
        dve(lambda e: e.scalar_tensor_tensor(out=ty.t[:], in0=tu.t[:], scalar=-C2_2PI, in1=ty.t[:],
                                             op0=ALU.mult, op1=ALU.add), [tu.r, ty.r], [ty.r])
        wrap(ang, ty, 0.0)
        act(lambda e: e.activation(out=tabs.t[:, 2 * which + 1, :, :], in_=ang.t[:], func=AF.Sin),
            [ang.r], [tabs.r])
        wrap(ty, ang, PI / 2)
        act(lambda e: e.activation(out=tabs.t[:, 2 * which, :, :], in_=ty.t[:], func=AF.Sin),
            [ty.r], [tabs.r])

    c32 = T(fw, "c32", [128, 8], F32)
    cth = T(fw, "cth", [128, 8], F32)
    fw.dma("sp", c32.t[:], c_d, writes=[c32.r])
    act(lambda e: e.activation(out=cth.t[:], in_=c32.t[:], func=AF.Tanh, scale=0.5), [c32.r], [cth.r])
    dve(lambda e: e.scalar_tensor_tensor(out=cth.t[:], in0=cth.t[:], scalar=1.0, in1=c32.t[:],
                                         op0=ALU.add, op1=ALU.mult), [cth.r, c32.r], [cth.r])
    dve(lambda e: e.tensor_scalar(out=cth.t[:], in0=cth.t[:], scalar1=0.5, scalar2=None, op0=ALU.mult),
        [cth.r], [cth.r])

    out_toks = []

    for l in range(L):
        bmodF = T(fw, "bmodF%d" % l, [128, 24], F32)
        pregF = T(fw, "pregF%d" % l, [128, 8], F32)
        fw.dma("sp", bmodF.t[:], bmodF_d[l], writes=[bmodF.r])
        fw.dma("sp", pregF.t[:], pregF_d[l], writes=[pregF.r])
        fw.dma("sp", gwe.t[:], gwe_d[l], writes=[gwe.r])
        fw.dma("sp", ggain.t[:], ggain_d[l].partition_broadcast(128), writes=[ggain.r])
        fw.dma("sp", esink2.t[:], sink_d[l].partition_broadcast(128), writes=[esink2.r])
        act(lambda e: e.activation(out=esink2.t[:], in_=esink2.t[:], func=AF.Exp), [esink2.r], [esink2.r])
        dve(lambda e: e.tensor_scalar(out=esink2.t[:], in0=esink2.t[:], scalar1=2.0, scalar2=None,
                                      op0=ALU.mult), [esink2.r], [esink2.r])
        dve(lambda e: e.tensor_copy(out=cb.t[:], in_=cth.t[:].unsqueeze(2).broadcast_to([128, 8, 128])),
            [cth.r], [cb.r])
        banks = [(pp[0], R_pp[0]), (pp[1], R_pp[1]), (psc[0], R_sc[0]), (psc[1], R_sc[1]),
                 (po[0], R_o[0]), (po[1], R_o[1])]
        i = 0
        for kc in range(8):
            for third in range(3):
                stl = wst if l == 0 else wst + wst_extra
                st = stl[i % len(stl)]
                i += 1
                fw.dma("sp", st.t[:], wmod_d[l, kc * 128:(kc + 1) * 128, third * 1024:(third + 1) * 1024],
                       writes=[st.r])
                for hf in range(2):
                    bk, rb = banks[third * 2 + hf]
                    fw.mm(lambda e: e.matmul(bk[:, :], lhsT=cb.t[:, kc, :], rhs=st.t[:, hf * 512:(hf + 1) * 512],
                                             start=(kc == 0), stop=(kc == 7)),
                          reads=[cb.r, st.r], writes=[rb], last=True)
        for which, dst in ((0, shF), (1, small)):
            for hf in range(2):
                bk, rb = banks[which * 2 + hf]
                dve(lambda e: e.tensor_tensor(
                    out=big32_.t[:].rearrange("p (k n) -> p k n", n=128),
                    in0=bk[:, :].rearrange("p (k n) -> p k n", n=128),
                    in1=C("ident").unsqueeze(1).broadcast_to([128, 4, 128]), op=ALU.mult),
                    [rb, cst.r], [big32_.r])
                dve(lambda e: e.tensor_reduce(out=dst.t[:, 4 * hf:4 * hf + 4],
                                              in_=big32_.t[:].rearrange("p (k n) -> p k n", n=128),
                                              op=ALU.add, axis=AX.X), [big32_.r], [dst.r])
        dve(lambda e: e.tensor_tensor(out=shF.t[:], in0=shF.t[:], in1=bmodF.t[:, 0:8], op=ALU.add),
            [shF.r, bmodF.r], [shF.r])
        dve(lambda e: e.tensor_tensor(out=small.t[:, 0:8], in0=small.t[:, 0:8], in1=bmodF.t[:, 8:16], op=ALU.add),
            [small.r, bmodF.r], [small.r])
        dve(lambda e: e.scalar_tensor_tensor(out=gmF.t[:], in0=small.t[:, 0:8], scalar=1.0, in1=pregF.t[:],
                                             op0=ALU.add, op1=ALU.mult), [small.r, pregF.r], [gmF.r])
        fw.dma("sp", gpg.t[:], bgate_d[l].partition_broadcast(128), writes=[gpg.r])
        for hf in range(2):
            bk, rb = banks[4 + hf]
            dve(lambda e: e.tensor_tensor(out=gpg.t[:, hf * 512:(hf + 1) * 512], in0=bk[:, :],
                                          in1=gpg.t[:, hf * 512:(hf + 1) * 512], op=ALU.add),
                [rb, gpg.r], [gpg.r])
        for hf in range(2):
            fw.dma("sp", rqk32.t[:], postg_d[l, hf * 512:(hf + 1) * 512].partition_broadcast(128), writes=[rqk32.r])
            dve(lambda e: e.tensor_tensor(out=gpg.t[:, hf * 512:(hf + 1) * 512], in0=gpg.t[:, hf * 512:(hf + 1) * 512],
                                          in1=rqk32.t[:], op=ALU.mult), [gpg.r, rqk32.r], [gpg.r])
        nst = min(2, n_tiles) if l == 0 else n_tiles
        if l == 0:
            for t_ in range(nst):
                act(lambda e: e.activation(out=xn.t[:], in_=xs[:, t_, :], func=AF.Square,
                                           accum_out=ssq[0].t[:, t_:t_ + 1]), [R_x[t_]], [xn.r, ssq[0].r])
        dve(lambda e: e.tensor_scalar(out=rstdT[l].t[:, 0:nst], in0=ssq[l].t[:, 0:nst], scalar1=1.0 / D,
                                      scalar2=EPS, op0=ALU.mult, op1=ALU.add), [ssq[l].r], [rstdT[l].r])
        pool(lambda e: e.tensor_tensor(out=rstdT[l].t[:, 0:nst], in0=rstdT[l].t[:, 0:nst],
                                       in1=C("neghalf16")[:, 0:nst], op=ALU.pow), [rstdT[l].r, cst.r], [rstdT[l].r])
        dve(lambda e: e.tensor_scalar(out=e0r.t[:], in0=C("ident")[:, 0:2], scalar1=rstdT[l].t[:, 0:1], scalar2=None,
                                      op0=ALU.mult), [cst.r, rstdT[l].r], [e0r.r])
        for kc in range(8):
            fw.mm(lambda e: e.matmul(po[1][:, 2 * kc:2 * kc + 2], lhsT=xs[:, 0, kc * 128:(kc + 1) * 128], rhs=e0r.t[:],
                                     start=True, stop=True),
                  reads=[R_x[0], e0r.r], writes=[R_o[1]], last=(kc == 7))
        dve(lambda e: e.tensor_tensor(out=h0F.t[:, 0:8], in0=po[1][:, 0:16].rearrange("p (k two) -> p k two", two=2)[:, :, 0],
                                      in1=gmF.t[:], op=ALU.mult), [R_o[1], gmF.r], [h0F.r])
        dve(lambda e: e.tensor_tensor(out=h0F.t[:, 0:8], in0=h0F.t[:, 0:8], in1=shF.t[:], op=ALU.add),
            [h0F.r, shF.r], [h0F.r])
        dve(lambda e: e.tensor_copy(out=cb.t[:], in_=h0F.t[:, 0:8].unsqueeze(2).broadcast_to([128, 8, 128])),
            [h0F.r], [cb.r])
        for kc in range(8):
            st = wst[kc % 2]
            fw.dma("sp", st.t[:, 0:768], wqk_d[l, kc * 128:(kc + 1) * 128, :], writes=[st.r])
            for hf in range(2):
                fw.mm(lambda e: e.matmul(pp[hf][:, 0:384], lhsT=cb.t[:, kc, :], rhs=st.t[:, hf * 384:(hf + 1) * 384],
                                         start=(kc == 0), stop=(kc == 7)),
                      reads=[cb.r, st.r], writes=[R_pp[hf]], last=True)
        act(lambda e: e.activation(out=big32_.t[:, 0:384], in_=pp[0][:, 0:384], func=AF.Copy), [R_pp[0]], [big32_.r])
        dve(lambda e: e.tensor_tensor(out=big32_.t[:, 0:384], in0=big32_.t[:, 0:384], in1=pp[1][:, 0:384], op=ALU.mult),
            [big32_.r, R_pp[1]], [big32_.r])
        dve(lambda e: e.tensor_reduce(out=s00.t[:, 0:4], in_=big32_.t[:, 0:256].rearrange("p (h d) -> p h d", d=64),
                                      axis=AX.X, op=ALU.add), [big32_.r], [s00.r])
        dve(lambda e: e.tensor_reduce(out=s00.t[:, 4:8], in_=big32_.t[:, 256:384].rearrange("p (h d) -> p h d", d=32),
                                      axis=AX.X, op=ALU.add), [big32_.r], [s00.r])
        dve(lambda e: e.tensor_scalar(out=s00.t[:, 0:4], in0=s00.t[:, 0:4], scalar1=0.125, scalar2=None, op0=ALU.mult),
            [s00.r], [s00.r])
        dve(lambda e: e.tensor_scalar(out=s00.t[:, 4:8], in0=s00.t[:, 4:8], scalar1=float(32 ** -0.5), scalar2=None,
                                      op0=ALU.mult), [s00.r], [s00.r])
        pool(lambda e: e.memset(retS.t[:], 0.0), [], [retS.r])
        pool(lambda e: e.memset(retSb.t[:], 0.0), [], [retSb.r])
        pool(lambda e: e.memset(glaS.t[:], 0.0), [], [glaS.r])
        pool(lambda e: e.memset(glaSb.t[:], 0.0), [], [glaSb.r])

        fw.barrier()
        def genA(t):
            p2 = t % 2
            p3 = t % 3
            xt = xs[:, t, :]
            Rx = R_x[t]
            AB = [(pp[0], R_pp[0]), (pp[1], R_pp[1]), (psc[1], R_sc[1])]
            GB = (psc[0], R_sc[0])
            dve(lambda e: e.tensor_scalar(out=xn.t[:], in0=xt, scalar1=rstdT[l].t[:, t:t + 1], scalar2=None,
                                          op0=ALU.mult), [Rx, rstdT[l].r], [xn.r])
            for kc in range(8):
                fw.mm(lambda e: e.transpose(out=ptA[:, kc * 128:(kc + 1) * 128], in_=xn.t[:, kc * 128:(kc + 1) * 128],
                                            identity=identb.t[:]),
                      reads=[xn.r, identb.r], writes=[R_tA], last=(kc == 7))
            if l == 0 and t + 2 < n_tiles:
                t2 = t + 2
                act(lambda e: e.activation(out=xn.t[:], in_=xs[:, t2, :], func=AF.Square,
                                           accum_out=ssq[0].t[:, t2:t2 + 1]), [R_x[t2]], [xn.r, ssq[0].r])
                dve(lambda e: e.tensor_scalar(out=rstdT[0].t[:, t2:t2 + 1], in0=ssq[0].t[:, t2:t2 + 1], scalar1=1.0 / D,
                                              scalar2=EPS, op0=ALU.mult, op1=ALU.add), [ssq[0].r], [rstdT[0].r])
                pool(lambda e: e.tensor_tensor(out=rstdT[0].t[:, t2:t2 + 1], in0=rstdT[0].t[:, t2:t2 + 1],
                                               in1=C("neghalf")[:, 0:1], op=ALU.pow), [rstdT[0].r, cst.r], [rstdT[0].r])
            for kc in range(8):
                if kc < 4:
                    act(lambda e: e.activation(out=hT.t[:, kc, :], in_=ptA[:, kc * 128:(kc + 1) * 128],
                                               func=AF.Identity, scale=gmF.t[:, kc:kc + 1], bias=shF.t[:, kc:kc + 1]),
                        [R_tA, gmF.r, shF.r], [hT.r])
                else:
                    dve(lambda e: e.tensor_scalar(out=hT.t[:, kc, :], in0=ptA[:, kc * 128:(kc + 1) * 128],
                                                  scalar1=gmF.t[:, kc:kc + 1], scalar2=shF.t[:, kc:kc + 1],
                                                  op0=ALU.mult, op1=ALU.add),
                        [R_tA, gmF.r, shF.r], [hT.r])
            yield

            def proj(ci, bank):
                for kc in range(8):
                    fw.mm(lambda e: e.matmul(AB[bank][0][:, :], lhsT=hT.t[:, kc, :],
                                             rhs=win[:, kc, COFFS[ci]:COFFS[ci] + 512],
                                             start=(kc == 0), stop=(kc == 7)),
                          reads=[hT.r] + R_winc[kc], writes=[AB[bank][1]], last=(kc == 7))

            cA = tabs.t[:, 0, t, :]
            sA = tabs.t[:, 1, t, :]
            cR = tabs.t[:, 2, t, :]
            sR = tabs.t[:, 3, t, :]
            cd = rqk32.t[:, 0:256].rearrange("p (h f) -> p h f", f=32)
            sd = rqk32.t[:, 256:512].rearrange("p (h f) -> p h f", f=32)
            dec_b = C("dec8").unsqueeze(2).broadcast_to([128, 8, 32])
            pool(lambda e: e.tensor_tensor(out=cd, in0=cR.unsqueeze(1).broadcast_to([128, 8, 32]), in1=dec_b,
                                           op=ALU.mult), [tabs.r, cst.r], [rqk32.r])
            pool(lambda e: e.tensor_tensor(out=sd, in0=sR.unsqueeze(1).broadcast_to([128, 8, 32]), in1=dec_b,
                                           op=ALU.mult), [tabs.r, cst.r], [rqk32.r])

            def rotary(src, R_src, nh, cos, sin, dst, dst_off, R_dst, final_eng, perhead=False, R_tab=None):
                n = nh * 64
                R_tab = tabs.r if R_tab is None else R_tab
                rotA, rotB = (rotA1, rotB1) if nh == 8 else (rotA2, rotB2)
                s4 = src.rearrange("p (h two f) -> p h two f", two=2, f=32)
                a4 = rotA.t[:, 0:n].rearrange("p (h two f) -> p h two f", two=2, f=32)
                b4 = rotB.t[:, 0:n].rearrange("p (h two f) -> p h two f", two=2, f=32)
                d4 = dst[:, dst_off:dst_off + n].rearrange("p (h two f) -> p h two f", two=2, f=32)
                if perhead:
                    cos4 = cos.unsqueeze(2).broadcast_to([128, nh, 2, 32])
                    sin3 = sin
                else:
                    cos4 = cos.unsqueeze(1).unsqueeze(1).broadcast_to([128, nh, 2, 32])
                    sin3 = sin.unsqueeze(1).broadcast_to([128, nh, 32])
                dve(lambda e: e.tensor_tensor(out=a4, in0=s4, in1=cos4, op=ALU.mult),
                    [R_src, R_tab], [rotA.r])
                dve(lambda e: e.scalar_tensor_tensor(out=b4[:, :, 0, :], in0=s4[:, :, 1, :], scalar=-1.0, in1=sin3,
                                                     op0=ALU.mult, op1=ALU.mult), [R_src, R_tab], [rotB.r])
                dve(lambda e: e.tensor_tensor(out=b4[:, :, 1, :], in0=s4[:, :, 0, :], in1=sin3, op=ALU.mult),
                    [R_src, R_tab], [rotB.r])
                fw.op(final_eng, lambda e: e.tensor_tensor(out=d4, in0=a4, in1=b4, op=ALU.add),
                      [rotA.r, rotB.r], [R_dst])

            for kc in range(8):
                fw.mm(lambda e: e.matmul(GB[0][:, 0:128], lhsT=win[:, kc, 0:128], rhs=hT.t[:, kc, :],
                                         start=(kc == 0), stop=(kc == 7)),
                      reads=[hT.r] + R_winc[kc], writes=[GB[1]], last=(kc == 7))
            gmk = C("gamask")
            act(lambda e: e.activation(out=gaTe.t[:], in_=GB[0][:, 0:128], func=AF.Identity,
                                       scale=gmk[:, 0:1], bias=gmk[:, 1:2]), [GB[1], cst.r], [gaTe.r])
            proj(0, 0)
            fw.mm(lambda e: e.matmul(GB[0][:, 128:256], lhsT=gaTe.t[:], rhs=gwe.t[:], start=True, stop=True),
                  reads=[gaTe.r, gwe.r], writes=[GB[1]], last=True)
            act(lambda e: e.activation(out=nl.t[:], in_=GB[0][:, 128:256], func=AF.Exp, scale=-1.0),
                [GB[1]], [nl.r])
            act(lambda e: e.activation(out=nl.t[:], in_=nl.t[:], func=AF.Ln, bias=1.0), [nl.r], [nl.r])
            rotary(AB[0][0][:, 0:512], AB[0][1], 8, cA, sA, aqk.t, 0, aqk.r, "pool")
            yield
            proj(1, 1)
            rotary(AB[1][0][:, 0:128], AB[1][1], 2, cA, sA, aqk.t, 512, aqk.r, "pool")
            act(lambda e: e.activation(out=vext[p3].t[:, :, 0:64],
                                       in_=AB[1][0][:, 128:256].rearrange("p (g d) -> p g d", d=64), func=AF.Copy),
                [AB[1][1]], [vext[p3].r])
            rotary(AB[1][0][:, 256:512], AB[1][1], 4, cd[:, 0:4, :], sd[:, 0:4, :], rqk[p2].t, 0, rqk[p2].r, "pool",
                   perhead=True, R_tab=rqk32.r)
            yield
            proj(2, 2)
            rotary(AB[2][0][:, 0:256], AB[2][1], 4, cd[:, 4:8, :], sd[:, 4:8, :], rqk[p2].t, 256, rqk[p2].r, "pool",
                   perhead=True, R_tab=rqk32.r)
            act(lambda e: e.activation(out=rv[p2].t[:], in_=AB[2][0][:, 256:512], func=AF.Copy), [AB[2][1]], [rv[p2].r])
            yield
            for ci, bank in ((3, 0), (4, 1)):
                proj(ci, bank)
                o = (ci - 3) * 512
                act(lambda e: e.activation(out=th.t[:], in_=AB[bank][0][:, :], func=AF.Tanh, scale=0.5),
                    [AB[bank][1]], [th.r])
                dve(lambda e: e.scalar_tensor_tensor(out=sg[p2].t[:, o:o + 512], in0=th.t[:], scalar=1.0,
                                                     in1=AB[bank][0][:, :], op0=ALU.add, op1=ALU.mult),
                    [th.r, AB[bank][1]], [sg[p2].r])
                yield
            fw.mm(lambda e: e.matmul(GB[0][:, 128:256], lhsT=C("tri_in"), rhs=nl.t[:], start=True, stop=True),
                  reads=[cst.r, nl.r], writes=[GB[1]], last=False)
            fw.mm(lambda e: e.matmul(GB[0][:, 256:384], lhsT=C("tri_rev"), rhs=nl.t[:], start=True, stop=True),
                  reads=[cst.r, nl.r], writes=[GB[1]], last=False)
            fw.mm(lambda e: e.matmul(GB[0][:, 384:386], lhsT=nl.t[:], rhs=C("ncol"), start=True, stop=True),
                  reads=[cst.r, nl.r], writes=[GB[1]], last=True)
            act(lambda e: e.activation(out=Eq.t[:], in_=GB[0][:, 128:256], func=AF.Exp), [GB[1]], [Eq.r])
            act(lambda e: e.activation(out=Ek.t[:], in_=GB[0][:, 128:256], func=AF.Exp, scale=-1.0),
                [GB[1]], [Ek.r])
            act(lambda e: e.activation(out=Er.t[:], in_=GB[0][:, 256:384], func=AF.Exp), [GB[1]], [Er.r])
            act(lambda e: e.activation(out=ebl[p2].t[:], in_=GB[0][:, 384:386], func=AF.Exp), [GB[1]], [ebl[p2].r])
            proj(5, 2)
            dve(lambda e: e.scalar_tensor_tensor(out=gqk.t[:, 0:128], in0=AB[2][0][:, 0:128], scalar=float(32 ** -0.5),
                                                 in1=Eq.t[:], op0=ALU.mult, op1=ALU.mult),
                [AB[2][1], Eq.r], [gqk.r])
            dve(lambda e: e.tensor_tensor(out=gqk.t[:, 128:256], in0=AB[2][0][:, 128:256], in1=Ek.t[:], op=ALU.mult),
                [AB[2][1], Ek.r], [gqk.r])
            dve(lambda e: e.tensor_tensor(out=gkh[p2].t[:], in0=AB[2][0][:, 128:256], in1=Er.t[:], op=ALU.mult),
                [AB[2][1], Er.r], [gkh[p2].r])
            act(lambda e: e.activation(out=gv[p2].t[:], in_=AB[2][0][:, 256:512], func=AF.Copy), [AB[2][1]], [gv[p2].r])
            yield
            for j in range(5):
                fw.mm(lambda e: e.transpose(out=ptB[:, j * 128:(j + 1) * 128], in_=aqk.t[:, j * 128:(j + 1) * 128],
                                            identity=identb.t[:]),
                      reads=[aqk.r, identb.r], writes=[R_tB], last=(j == 4))
            src4 = ptB[:, 0:512].rearrange("p (c q) -> p c q", q=128)
            act(lambda e: e.activation(out=qbd[p2][0].t[0:64, :, :], in_=src4[0:64, :, :], func=AF.Copy),
                [R_tB], [qbd[p2][0].r])
            act(lambda e: e.activation(out=kT[p3].t[:], in_=ptB[:, 512:640], func=AF.Copy), [R_tB], [kT[p3].r])
            dve(lambda e: e.tensor_copy(out=qbd[p2][1].t[64:128, :, :], in_=src4[64:128, :, :]), [R_tB], [qbd[p2][1].r])
            yield
            for j in range(4):
                fw.mm(lambda e: e.transpose(out=ptB[:, j * 128:(j + 1) * 128], in_=rqk[p2].t[:, j * 128:(j + 1) * 128],
                                            identity=identb.t[:]),
                      reads=[rqk[p2].r, identb.r], writes=[R_tB], last=False)
            for j in range(2):
                fw.mm(lambda e: e.transpose(out=ptB[:, (4 + j) * 128:(5 + j) * 128],
                                            in_=gqk.t[:, j * 128:(j + 1) * 128], identity=identb.t[:]),
                      reads=[gqk.r, identb.r], writes=[R_tB], last=(j == 1))
            for c in range(2):
                act(lambda e: e.activation(out=rqbd[p2][c].t[0:64, 0, :], in_=ptB[0:64, c * 128:(c + 1) * 128],
                                           func=AF.Copy), [R_tB], [rqbd[p2][c].r])
            act(lambda e: e.activation(out=rkT[p2].t[:].rearrange("p c q -> p (c q)"), in_=ptB[:, 256:512],
                                       func=AF.Copy), [R_tB], [rkT[p2].r])
            act(lambda e: e.activation(out=gqT[p2].t[:], in_=ptB[:, 512:640], func=AF.Copy), [R_tB], [gqT[p2].r])
            for hh in (0, 2):
                act(lambda e: e.activation(out=gqbd[p2].t[32 * hh:32 * hh + 32, hh, :],
                                           in_=ptB[32 * hh:32 * hh + 32, 512:640], func=AF.Copy),
                    [R_tB], [gqbd[p2].r])
            for c in range(2):
                dve(lambda e: e.tensor_copy(out=rqbd[p2][c].t[64:128, 1, :], in_=ptB[64:128, c * 128:(c + 1) * 128]),
                    [R_tB], [rqbd[p2][c].r])
            dve(lambda e: e.tensor_copy(out=rqT[p2].t[:].rearrange("p c q -> p (c q)"), in_=ptB[:, 0:256]),
                [R_tB], [rqT[p2].r])
            for hh in (1, 3):
                dve(lambda e: e.tensor_copy(out=gqbd[p2].t[32 * hh:32 * hh + 32, hh, :],
                                            in_=ptB[32 * hh:32 * hh + 32, 512:640]), [R_tB], [gqbd[p2].r])
            dve(lambda e: e.tensor_copy(out=gkT[p2].t[:], in_=ptB[:, 640:768]), [R_tB], [gkT[p2].r])
            yield

        def genB(t):
            p2 = t % 2
            p3 = t % 3
            pv3 = (t - 1) % 3
            xt = xs[:, t, :]
            Rx = R_x[t]
            blocks = [(p3, mcurb)] + ([(pv3, mprevb)] if t > 0 else [])
            for g in range(2):
                for bi, (kp, mk) in enumerate(blocks):
                    bank = 0
                    fw.mm(lambda e: e.matmul(psc[bank][:, :], lhsT=kT[kp].t[:],
                                             rhs=qbd[p2][g].t[:].rearrange("p c q -> p (c q)"), start=True, stop=False),
                          reads=[kT[kp].r, qbd[p2][g].r], writes=[R_sc[bank]], last=False)
                    fw.mm(lambda e: e.matmul(psc[bank][:, :], lhsT=identb.t[:], rhs=mk.t[:], start=False, stop=True),
                          reads=[identb.r, mk.r], writes=[R_sc[bank]], last=True)
                    act(lambda e: e.activation(out=PT[g][bi].t[:], in_=psc[bank][:, :], func=AF.Exp, scale=0.125),
                        [R_sc[bank]], [PT[g][bi].r])
                yield
            for g in range(2):
                for c in range(4):
                    for bi, (kp, mk) in enumerate(blocks):
                        fw.mm(lambda e: e.matmul(po[g][:, c * 65:(c + 1) * 65], lhsT=PT[g][bi].t[:, c * 128:(c + 1) * 128],
                                                 rhs=vext[kp].t[:, g, :], start=(bi == 0), stop=(bi == len(blocks) - 1)),
                              reads=[PT[g][bi].r, vext[kp].r], writes=[R_o[g]],
                              last=(c == 3 and bi == len(blocks) - 1))
                o4 = po[g][:, 0:260].rearrange("p (c d) -> p c d", d=65)
                dve(lambda e: e.scalar_tensor_tensor(out=stat1.t[:, 0:4], in0=o4[:, :, 64], scalar=2.0,
                                                     in1=esink2.t[:, 4 * g:4 * g + 4], op0=ALU.mult, op1=ALU.add),
                    [R_o[g], esink2.r], [stat1.r])
                dve(lambda e: e.reciprocal(out=stat1.t[:, 4:8], in_=stat1.t[:, 0:4]), [stat1.r], [stat1.r])
                dve(lambda e: e.tensor_tensor(out=an.t[:].rearrange("p (c d) -> p c d", d=64), in0=o4[:, :, 0:64],
                                              in1=stat1.t[:, 4:8].unsqueeze(2).broadcast_to([128, 4, 64]),
                                              op=ALU.mult), [R_o[g], stat1.r], [an.r])
                pool(lambda e: e.tensor_tensor(out=mix.t[:, g * 256:(g + 1) * 256], in0=an.t[:],
                                               in1=sg[p2].t[:, g * 256:(g + 1) * 256], op=ALU.mult),
                     [an.r, sg[p2].r], [mix.r])
                yield

            def head_norm(bank, R_bank, off, gain, rraw, rsqt, rsq_r, stat2):
                act(lambda e: e.activation(out=rraw.t[:], in_=bank[:, 0:256], func=AF.Copy), [R_bank], [rraw.r])
                act(lambda e: e.activation(out=rsqt, in_=bank[:, 0:256], func=AF.Square), [R_bank], [rsq_r])
                dve(lambda e: e.tensor_reduce(out=stat2.t[:, 0:4], in_=rsqt.rearrange("p (h d) -> p h d", d=64),
                                              axis=AX.X, op=ALU.add), [rsq_r], [stat2.r])
                dve(lambda e: e.tensor_scalar(out=stat2.t[:, 0:4], in0=stat2.t[:, 0:4], scalar1=4.0 / 64.0,
                                              scalar2=4.0 * EPS, op0=ALU.mult, op1=ALU.add), [stat2.r], [stat2.r])
                pool(lambda e: e.tensor_tensor(out=stat2.t[:, 4:8], in0=stat2.t[:, 0:4], in1=C("neghalf")[:, 0:4],
                                               op=ALU.pow), [stat2.r, cst.r], [stat2.r])
                dve(lambda e: e.tensor_tensor(out=rraw.t[:].rearrange("p (h d) -> p h d", d=64),
                                              in0=rraw.t[:].rearrange("p (h d) -> p h d", d=64),
                                              in1=stat2.t[:, 4:8].unsqueeze(2).broadcast_to([128, 4, 64]), op=ALU.mult),
                    [rraw.r, stat2.r], [rraw.r])
                if gain is not None:
                    pool(lambda e: e.tensor_tensor(out=rraw.t[:].rearrange("p (h d) -> p h d", d=64),
                                                   in0=rraw.t[:].rearrange("p (h d) -> p h d", d=64),
                                                   in1=gain.t[:].unsqueeze(1).broadcast_to([128, 4, 64]), op=ALU.mult),
                         [rraw.r, gain.r], [rraw.r])
                pool(lambda e: e.tensor_tensor(out=mix.t[:, off:off + 256], in0=rraw.t[:], in1=sg[p2].t[:, off:off + 256],
                                               op=ALU.mult), [rraw.r, sg[p2].r], [mix.r])

            for c in range(2):
                fw.mm(lambda e: e.matmul(psc[0][:, c * 256:(c + 1) * 256], lhsT=rkT[p2].t[:, c, :],
                                         rhs=rqbd[p2][c].t[:].rearrange("p h q -> p (h q)"), start=True, stop=True),
                      reads=[rkT[p2].r, rqbd[p2][c].r], writes=[R_sc[0]], last=(c == 1))
            dve(lambda e: e.tensor_tensor(out=scT.t[:], in0=psc[0][:, :], in1=causalb.t[:], op=ALU.mult),
                [R_sc[0], causalb.r], [scT.r])
            if t == 0:
                dve(lambda e: e.tensor_copy(out=scT.t[0:1, :].rearrange("p (h i) -> p h i", i=128)[:, :, 0],
                                            in_=s00.t[0:1, 0:4]), [scT.r, s00.r], [scT.r])
            yield
            for hh in range(4):
                fw.mm(lambda e: e.matmul(po[0][:, hh * 64:(hh + 1) * 64], lhsT=scT.t[:, hh * 128:(hh + 1) * 128],
                                         rhs=rv[p2].t[:, hh * 64:(hh + 1) * 64], start=(hh == 0), stop=False),
                      reads=[scT.r, rv[p2].r], writes=[R_o[0]], last=False)
            for c in range(2):
                fw.mm(lambda e: e.matmul(po[0][:, c * 128:(c + 1) * 128], lhsT=rqT[p2].t[:, c, :],
                                         rhs=retSb.t[:, c * 128:(c + 1) * 128], start=False, stop=(c == 1)),
                      reads=[rqT[p2].r, retSb.r], writes=[R_o[0]], last=False)
            for c in range(2):
                fw.mm(lambda e: e.matmul(po[0][:, 256 + c * 128:256 + (c + 1) * 128],
                                         lhsT=rqk[p2].t[:, 256 + c * 128:256 + (c + 1) * 128],
                                         rhs=rv[p2].t[:, c * 128:(c + 1) * 128], start=True, stop=True),
                      reads=[rqk[p2].r, rv[p2].r], writes=[R_o[0]], last=(c == 1))
            head_norm(po[0], R_o[0], 512, None, rraw, rsq.t[:], rsq.r, stat2)
            dve(lambda e: e.tensor_tensor(out=dS.t[:], in0=po[0][:, 256:512], in1=C("bd64dec"), op=ALU.mult),
                [R_o[0], cst.r], [dS.r])
            pool(lambda e: e.tensor_tensor(out=retS.t[:].rearrange("p (c n) -> p c n", n=128),
                                           in0=retS.t[:].rearrange("p (c n) -> p c n", n=128),
                                           in1=C("sdec").unsqueeze(2).broadcast_to([128, 2, 128]), op=ALU.mult),
                 [retS.r, cst.r], [retS.r])
            pool(lambda e: e.tensor_tensor(out=retS.t[:], in0=retS.t[:], in1=dS.t[:], op=ALU.add),
                 [retS.r, dS.r], [retS.r])
            act(lambda e: e.activation(out=retSb.t[:], in_=retS.t[:], func=AF.Copy), [retS.r], [retSb.r])
            yield

            fw.mm(lambda e: e.matmul(psc[0][:, :], lhsT=gkT[p2].t[:], rhs=gqbd[p2].t[:].rearrange("p h q -> p (h q)"),
                                     start=True, stop=True),
                  reads=[gkT[p2].r, gqbd[p2].r], writes=[R_sc[0]], last=True)
            dve(lambda e: e.tensor_tensor(out=scT.t[:], in0=psc[0][:, :], in1=causalb.t[:], op=ALU.mult),
                [R_sc[0], causalb.r], [scT.r])
            if t == 0:
                dve(lambda e: e.tensor_copy(out=scT.t[0:1, :].rearrange("p (h i) -> p h i", i=128)[:, :, 0],
                                            in_=s00.t[0:1, 4:8]), [scT.r, s00.r], [scT.r])
            yield
            for hh in range(4):
                fw.mm(lambda e: e.matmul(po[1][:, hh * 64:(hh + 1) * 64], lhsT=scT.t[:, hh * 128:(hh + 1) * 128],
                                         rhs=gv[p2].t[:, hh * 64:(hh + 1) * 64], start=(hh == 0), stop=False),
                      reads=[scT.r, gv[p2].r], writes=[R_o[1]], last=False)
            fw.mm(lambda e: e.matmul(po[1][:, 0:256], lhsT=gqT[p2].t[:], rhs=glaSb.t[:], start=False, stop=True),
                  reads=[gqT[p2].r, glaSb.r], writes=[R_o[1]], last=False)
            fw.mm(lambda e: e.matmul(po[1][:, 256:512], lhsT=gkh[p2].t[:], rhs=gv[p2].t[:], start=True, stop=True),
                  reads=[gkh[p2].r, gv[p2].r], writes=[R_o[1]], last=True)
            head_norm(po[1], R_o[1], 768, ggain, an, big32_.t[:, 0:256], big32_.r, stat3)
            dve(lambda e: e.tensor_tensor(out=dS.t[:], in0=po[1][:, 256:512], in1=C("bd32"), op=ALU.mult),
                [R_o[1], cst.r], [dS.r])
            dve(lambda e: e.scalar_tensor_tensor(out=glaS.t[:], in0=glaS.t[:], scalar=ebl[p2].t[:, 0:1], in1=dS.t[:],
                                                 op0=ALU.mult, op1=ALU.add), [glaS.r, ebl[p2].r, dS.r], [glaS.r])
            act(lambda e: e.activation(out=glaSb.t[:], in_=glaS.t[:], func=AF.Copy), [glaS.r], [glaSb.r])
            yield

            if dbg is not None and l == 0 and t in dbg:
                dump("hT_%d" % t, hT.t[:].rearrange("p k q -> p (k q)"), hT.r, 1024)
                dump("aqk_%d" % t, aqk.t[:], aqk.r, 640)
                dump("rqk_%d" % t, rqk[p2].t[:], rqk[p2].r, 512)
                dump("gqk_%d" % t, gqk.t[:], gqk.r, 256)
                dump("gkh_%d" % t, gkh[p2].t[:], gkh[p2].r, 128)
                dump("sg_%d" % t, sg[p2].t[:], sg[p2].r, 1024)
                dump("nl_%d" % t, nl.t[:], nl.r, 128)
                dump("mix_%d" % t, mix.t[:], mix.r, 1024)
                dump("retS_%d" % t, retS.t[:], retS.r, 256)
                dump("glaS_%d" % t, glaS.t[:], glaS.r, 256)
            for j in range(8):
                fw.mm(lambda e: e.transpose(out=ptB[:, j * 128:(j + 1) * 128], in_=mix.t[:, j * 128:(j + 1) * 128],
                                            identity=identb.t[:]),
                      reads=[mix.r, identb.r], writes=[R_tB], last=(j == 7))
            act(lambda e: e.activation(out=mixT.t[:, 0:4, :].rearrange("p k q -> p (k q)"), in_=ptB[:, 0:512],
                                       func=AF.Copy), [R_tB], [mixT.r])
            dve(lambda e: e.tensor_copy(out=mixT.t[:, 4:8, :].rearrange("p k q -> p (k q)"), in_=ptB[:, 512:1024]),
                [R_tB], [mixT.r])
            yield
            for hf in range(2):
                for kc in range(8):
                    fw.mm(lambda e: e.matmul(po[hf][:, :], lhsT=mixT.t[:, kc, :],
                                             rhs=wout[:, kc, hf * 512:(hf + 1) * 512], start=(kc == 0), stop=(kc == 7)),
                          reads=[mixT.r, R_woutc[kc]], writes=[R_o[hf]], last=(kc == 7))
                yield
            for hf in range(2):
                act(lambda e: e.activation(out=mix.t[:, hf * 512:(hf + 1) * 512], in_=po[hf][:, :], func=AF.Square,
                                           accum_out=stat.t[:, 16 + hf:17 + hf]), [R_o[hf]], [mix.r, stat.r])
            dve(lambda e: e.tensor_tensor(out=stat.t[:, 18:19], in0=stat.t[:, 16:17], in1=stat.t[:, 17:18], op=ALU.add),
                [stat.r], [stat.r])
            dve(lambda e: e.tensor_scalar(out=stat.t[:, 19:20], in0=stat.t[:, 18:19], scalar1=1.0 / D, scalar2=EPS,
                                          op0=ALU.mult, op1=ALU.add), [stat.r], [stat.r])
            pool(lambda e: e.tensor_tensor(out=stat.t[:, 20:21], in0=stat.t[:, 19:20], in1=C("neghalf")[:, 0:1],
                                           op=ALU.pow), [stat.r, cst.r], [stat.r])
            for hf in range(2):
                dve(lambda e: e.scalar_tensor_tensor(out=big32[hf].t[:], in0=po[hf][:, :],
                                                     scalar=stat.t[:, 20:21], in1=gpg.t[:, hf * 512:(hf + 1) * 512],
                                                     op0=ALU.mult, op1=ALU.mult),
                    [R_o[hf], stat.r, gpg.r], [big32[hf].r])
                dve(lambda e: e.tensor_tensor(out=xt[:, hf * 512:(hf + 1) * 512], in0=xt[:, hf * 512:(hf + 1) * 512],
                                              in1=big32[hf].t[:], op=ALU.add), [Rx, big32[hf].r], [Rx])
            if l == L - 1:
                out_toks.append(fw.dma("sp", out_d[t * 128:(t + 1) * 128, :], xt, reads=[Rx]))
            else:
                act(lambda e: e.activation(out=mix.t[:], in_=xt, func=AF.Square, accum_out=ssq[l + 1].t[:, t:t + 1]),
                    [Rx], [mix.r, ssq[l + 1].r])
            yield

        if l == 0:
            for t_ in range(2, n_tiles):
                fw.dma("sp", xs[:, t_, :], x_d[t_ * 128:(t_ + 1) * 128, :], writes=[R_x[t_]], nobar=True)
        for _ in genA(0):
            pass
        for t in range(n_tiles):
            gb = genB(t)
            ga_ = genA(t + 1) if t + 1 < n_tiles else iter(())
            alive_a = alive_b = True
            while alive_a or alive_b:
                if alive_b:
                    try:
                        next(gb)
                    except StopIteration:
                        alive_b = False
                if alive_a:
                    try:
                        next(ga_)
                    except StopIteration:
                        alive_a = False
        if l + 1 < L:
            load_weights(l + 1, early=True)
        if dbg is not None and l == 0:
            dump("gmF", gmF.t[:], gmF.r, 8)
            dump("shF", shF.t[:], shF.r, 8)
            dump("gpg", gpg.t[:], gpg.r, 1024)
            dump("tabs", tabs.t[:].rearrange("p a t f -> p (a t f)"), tabs.r, 4 * NT * 32)
        fw.barrier()

    fw.flush()
    e = fw.engs["sp"]
    for s in fw.dsems + fw.dsems_nb + [x[0] for x in fw.dsems_sw]:
        if s.count:
            e.wait((s, s.count, "dma"))
    fw.close()
    return nc


_NC_CACHE = {}


def _prep_shared(w_mod, b_mod, pre_norm_gain, post_norm_gain, w_in, attn_sinks, gla_gate_w, gla_gate_b,
                 gla_norm_gain, w_out):
    L = w_in.shape[0]
    f = lambda a: np.ascontiguousarray(np.asarray(a, dtype=np.float32))
    b_mod = np.asarray(b_mod, np.float32)
    gwe = np.zeros((L, 128, 128), np.float32)
    gwe[:, 0:16, :] = np.asarray(gla_gate_w, np.float32)
    gwe[:, 16, :] = np.asarray(gla_gate_b, np.float32)
    return {
        "w_mod": f(w_mod),
        "bmodF": f(b_mod.reshape(L, 24, 128).transpose(0, 2, 1)),
        "bgate": f(b_mod[:, 2048:3072]),
        "pregF": f(np.asarray(pre_norm_gain, np.float32).reshape(L, 8, 128).transpose(0, 2, 1)),
        "postg": f(post_norm_gain),
        "w_in": f(np.asarray(w_in, np.float32)[:, :, _PERM]),
        "w_qk32": f(np.asarray(w_in, np.float32)[:, :, _QK32]),
        "sinks": f(attn_sinks),
        "gwe": gwe,
        "ggain": f(gla_norm_gain),
        "w_out": f(w_out),
        "consts": _CBLOB,
        "cmask": _CMASK,
    }


def kernel(x, c, positions, w_mod, b_mod, pre_norm_gain, post_norm_gain, w_in,
           attn_sinks, gla_gate_w, gla_gate_b, gla_norm_gain, w_out):
    x = np.asarray(x, np.float32)
    c = np.asarray(c, np.float32)
    positions = np.asarray(positions, np.int32)
    B = x.shape[0]
    shared = _prep_shared(w_mod, b_mod, pre_norm_gain, post_norm_gain, w_in, attn_sinks, gla_gate_w,
                          gla_gate_b, gla_norm_gain, w_out)
    if "nc" not in _NC_CACHE:
        _NC_CACHE["nc"] = build(n_layers=2)
    nc = _NC_CACHE["nc"]
    in_maps = []
    for b in range(B):
        m = dict(shared)
        m["x"] = np.ascontiguousarray(x[b])
        m["c"] = np.ascontiguousarray(c[b].reshape(8, 128).T)
        m["pos"] = np.ascontiguousarray(positions[b].reshape(NT, 128).T)
        in_maps.append(m)
    res = run_bass_kernel_spmd(nc, in_maps, core_ids=list(range(B)))
    return np.stack([np.asarray(r["out"], np.float32) for r in res.results], axis=0)
```
